# Optimizing a Trainium2 kernel written in Bass

```python
import math
import jax, jax.numpy as jnp
from jax import lax
import numpy as np


D_MODEL = 1024
BATCH = 4
SEQ = 8192
DEPTH = 2

SSD_EXPAND = 2
SSD_INNER = SSD_EXPAND * D_MODEL
SSD_HEAD_DIM = 64
SSD_HEADS = SSD_INNER // SSD_HEAD_DIM
SSD_GROUPS = 4
SSD_STATE = 128
SSD_CONV = 5
SSD_CHUNK = 128
SSD_CONV_CH = SSD_INNER + 2 * SSD_GROUPS * SSD_STATE

DIFF_HEAD_DIM = 64
DIFF_HEADS = D_MODEL // (2 * DIFF_HEAD_DIM)
DIFF_WIDTH = DIFF_HEADS * 2 * DIFF_HEAD_DIM
Q_BLOCK = 128
ROPE_THETA = 10000.0

MEM_TOKENS = 256
CROSS_HEADS = 4
CROSS_HEAD_DIM = D_MODEL // CROSS_HEADS
CROSS_WIDTH = CROSS_HEADS * CROSS_HEAD_DIM

N_BRANCH = 3
IN_COLS = (SSD_INNER + SSD_CONV_CH + 2 * SSD_HEADS + 4 * DIFF_WIDTH
           + 2 * CROSS_WIDTH + N_BRANCH * D_MODEL)
DEEPNORM_ALPHA = (2 * DEPTH) ** 0.25
DEEPNORM_BETA = (8 * DEPTH) ** -0.25
NORM_EPS = 1e-5

kernel_name = 'hybrid_ssd_diffattn_memxattn_deepnorm'


def _in_splits():
    sizes = [SSD_INNER, SSD_CONV_CH, 2 * SSD_HEADS,
             DIFF_WIDTH, DIFF_WIDTH, DIFF_WIDTH, DIFF_WIDTH,
             CROSS_WIDTH, CROSS_WIDTH]
    return [int(v) for v in np.cumsum(sizes)]


def _rmsnorm(t, w):
    tf = t.astype(jnp.float32)
    tf = tf * lax.rsqrt(jnp.mean(tf * tf, axis=-1, keepdims=True) + NORM_EPS)
    return tf * w.astype(jnp.float32)


def _layernorm(t, g, b):
    tf = t.astype(jnp.float32)
    mu = jnp.mean(tf, axis=-1, keepdims=True)
    var = jnp.mean(jnp.square(tf - mu), axis=-1, keepdims=True)
    return (tf - mu) * lax.rsqrt(var + NORM_EPS) * g.astype(jnp.float32) + b.astype(jnp.float32)


def _rope_tables(positions):
    inv = 1.0 / (ROPE_THETA ** (jnp.arange(0, DIFF_HEAD_DIM, 2, dtype=jnp.float32) / DIFF_HEAD_DIM))
    ang = positions.astype(jnp.float32)[..., None] * inv
    return jnp.cos(ang), jnp.sin(ang)


def _apply_rope(t, cos, sin):
    half = DIFF_HEAD_DIM // 2
    tf = t.astype(jnp.float32)
    t1, t2 = tf[..., :half], tf[..., half:]
    c = cos[:, :, None, None, :]
    s = sin[:, :, None, None, :]
    return jnp.concatenate([t1 * c - t2 * s, t1 * s + t2 * c], axis=-1)


def _centred_dwconv(t, w, b):
    pad = SSD_CONV // 2
    out = lax.conv_general_dilated(
        t, w[:, None, :].astype(t.dtype), window_strides=(1,),
        padding=[(pad, pad)], dimension_numbers=('NWC', 'WIO', 'NWC'),
        feature_group_count=t.shape[-1])
    return out + b.astype(t.dtype)


def _ssd_scan(xs, a, bm, cm):
    Bsz, L, H, P = xs.shape
    G, N = bm.shape[2], bm.shape[3]
    R = H // G
    Q = SSD_CHUNK
    C = L // Q
    xs = xs.astype(jnp.float32).reshape(Bsz, C, Q, G, R, P)
    bm = bm.astype(jnp.float32).reshape(Bsz, C, Q, G, N)
    cm = cm.astype(jnp.float32).reshape(Bsz, C, Q, G, N)
    a = a.astype(jnp.float32).reshape(Bsz, C, Q, G, R).transpose(0, 3, 4, 1, 2)
    a_cum = jnp.cumsum(a, axis=-1)
    lower = jnp.tril(jnp.ones((Q, Q), dtype=bool))
    seg = a_cum[..., :, None] - a_cum[..., None, :]
    decay = jnp.exp(jnp.where(lower, seg, -jnp.inf))
    y_diag = jnp.einsum('bcqgn,bcsgn,bgrcqs,bcsgrp->bcqgrp', cm, bm, decay, xs)
    decay_states = jnp.exp(a_cum[..., -1:] - a_cum)
    states = jnp.einsum('bcqgn,bgrcq,bcqgrp->bcgrpn', bm, decay_states, xs)
    chunk_decay = jnp.moveaxis(jnp.exp(a_cum[..., -1]), -1, 0)
    def step(h, inp):
        s_c, d_c = inp
        return h * d_c[..., None, None] + s_c, h
    h0 = jnp.zeros((Bsz, G, R, P, N), jnp.float32)
    _, prev = lax.scan(step, h0, (jnp.moveaxis(states, 1, 0), chunk_decay))
    prev = jnp.moveaxis(prev, 0, 1)
    y_off = jnp.einsum('bcqgn,bcgrpn,bgrcq->bcqgrp', cm, prev, jnp.exp(a_cum))
    return (y_diag + y_off).reshape(Bsz, L, H, P)


def _ssd_branch(z, xbc, dt_raw, conv_w, conv_b, dt_bias, a_log, d_skip, norm_w):
    Bsz, L, _ = z.shape
    gn = SSD_GROUPS * SSD_STATE
    xbc = jax.nn.silu(_centred_dwconv(xbc, conv_w, conv_b))
    xs = xbc[..., :SSD_INNER].reshape(Bsz, L, SSD_HEADS, SSD_HEAD_DIM).astype(jnp.float32)
    bm = xbc[..., SSD_INNER:SSD_INNER + gn].reshape(Bsz, L, SSD_GROUPS, SSD_STATE)
    cm = xbc[..., SSD_INNER + gn:].reshape(Bsz, L, SSD_GROUPS, SSD_STATE)
    dt = jax.nn.softplus(dt_raw.reshape(Bsz, L, 2, SSD_HEADS).astype(jnp.float32)
                         + dt_bias.astype(jnp.float32))
    a = -jnp.exp(a_log.astype(jnp.float32))
    flip = lambda t: jnp.flip(t, axis=1)
    y_fwd = _ssd_scan(xs * dt[:, :, 0, :, None], dt[:, :, 0] * a[0], bm, cm)
    y_bwd = flip(_ssd_scan(flip(xs * dt[:, :, 1, :, None]), flip(dt[:, :, 1] * a[1]),
                           flip(bm), flip(cm)))
    y = y_fwd + y_bwd + d_skip.astype(jnp.float32)[:, None] * xs
    y = y.reshape(Bsz, L, SSD_INNER) * jax.nn.silu(z.astype(jnp.float32))
    return _rmsnorm(y, norm_w)


def _diff_attention(q, k, v, lam):
    Bsz, S = q.shape[0], q.shape[1]
    nb = S // Q_BLOCK
    qb = jnp.moveaxis(q.reshape(Bsz, nb, Q_BLOCK, DIFF_HEADS, 2, DIFF_HEAD_DIM), 1, 0)
    vf = v.astype(jnp.float32)
    scale = DIFF_HEAD_DIM ** -0.5
    def block(qblk):
        s = jnp.einsum('bqhcd,bkhcd->bhcqk', qblk, k) * scale
        p = jax.nn.softmax(s.astype(jnp.float32), axis=-1)
        w = p[:, :, 0] - lam * p[:, :, 1]
        return jnp.einsum('bhqk,bkhe->bqhe', w, vf)
    o = lax.map(block, qb)
    return jnp.moveaxis(o, 0, 1).reshape(Bsz, S, DIFF_HEADS, 2 * DIFF_HEAD_DIM)


def _memory_cross_attention(q, mem_kv):
    mk, mv = mem_kv[:, :, 0], mem_kv[:, :, 1]
    s = jnp.einsum('bshd,bmhd->bhsm', q, mk).astype(jnp.float32) * (CROSS_HEAD_DIM ** -0.5)
    p = jax.nn.softmax(s, axis=-1)
    return jnp.einsum('bhsm,bmhd->bshd', p, mv.astype(jnp.float32))


def setup_inputs(seed: int = 0) -> dict:
    key = jax.random.key(seed)
    ks = jax.random.split(key, 20)
    f32 = jnp.float32
    def nrm(k, shape, scale):
        return jax.random.normal(k, shape, f32) * scale
    x = nrm(ks[0], (BATCH, SEQ, D_MODEL), 1.0)
    mem = nrm(ks[1], (BATCH, MEM_TOKENS, D_MODEL), 1.0)
    offsets = jax.random.randint(ks[2], (BATCH, 1), 0, 4096, dtype=jnp.int32)
    positions = offsets + jnp.arange(SEQ, dtype=jnp.int32)[None, :]
    w_in = nrm(ks[3], (DEPTH, D_MODEL, IN_COLS), D_MODEL ** -0.5)
    conv_w = nrm(ks[4], (DEPTH, SSD_CONV, SSD_CONV_CH), SSD_CONV ** -0.5)
    conv_b = nrm(ks[5], (DEPTH, SSD_CONV_CH), 0.01)
    dt0 = jnp.exp(jax.random.uniform(ks[6], (DEPTH, 2, SSD_HEADS), f32,
                                     math.log(1e-3), math.log(1e-1)))
    dt_bias = dt0 + jnp.log(-jnp.expm1(-dt0))
    a_log = jnp.log(jax.random.uniform(ks[7], (DEPTH, 2, SSD_HEADS), f32, 1.0, 16.0))
    d_skip = 1.0 + nrm(ks[8], (DEPTH, SSD_HEADS), 0.01)
    ssd_norm_w = 1.0 + nrm(ks[9], (DEPTH, SSD_INNER), 0.01)
    diff_lam = nrm(ks[10], (DEPTH, 4, DIFF_HEAD_DIM), 0.1)
    diff_norm_w = 1.0 + nrm(ks[11], (DEPTH, 2 * DIFF_HEAD_DIM), 0.01)
    w_mem_kv = nrm(ks[12], (DEPTH, D_MODEL, 2 * CROSS_WIDTH), D_MODEL ** -0.5)
    w_br_ssd = nrm(ks[13], (DEPTH, SSD_INNER, D_MODEL), SSD_INNER ** -0.5 * DEEPNORM_BETA)
    w_br_diff = nrm(ks[14], (DEPTH, DIFF_WIDTH, D_MODEL), DIFF_WIDTH ** -0.5 * DEEPNORM_BETA)
    w_br_cross = nrm(ks[15], (DEPTH, CROSS_WIDTH, D_MODEL), CROSS_WIDTH ** -0.5 * DEEPNORM_BETA)
    gate_b = nrm(ks[16], (DEPTH, N_BRANCH, D_MODEL), 0.01)
    w_out = nrm(ks[17], (DEPTH, D_MODEL, D_MODEL), D_MODEL ** -0.5 * DEEPNORM_BETA)
    ln_g = 1.0 + nrm(ks[18], (DEPTH, D_MODEL), 0.01)
    ln_b = nrm(ks[19], (DEPTH, D_MODEL), 0.01)
    return {'x': x, 'mem': mem, 'positions': positions, 'w_in': w_in,
            'conv_w': conv_w, 'conv_b': conv_b, 'dt_bias': dt_bias, 'a_log': a_log,
            'd_skip': d_skip, 'ssd_norm_w': ssd_norm_w, 'diff_lam': diff_lam,
            'diff_norm_w': diff_norm_w, 'w_mem_kv': w_mem_kv, 'w_br_ssd': w_br_ssd,
            'w_br_diff': w_br_diff, 'w_br_cross': w_br_cross, 'gate_b': gate_b,
            'w_out': w_out, 'ln_g': ln_g, 'ln_b': ln_b}


def reference(x, mem, positions, w_in, conv_w, conv_b, dt_bias, a_log, d_skip,
              ssd_norm_w, diff_lam, diff_norm_w, w_mem_kv, w_br_ssd, w_br_diff,
              w_br_cross, gate_b, w_out, ln_g, ln_b):
    Bsz, S, _ = x.shape
    cos, sin = _rope_tables(positions)
    splits = _in_splits()
    for layer in range(DEPTH):
        lambda_init = 0.8 - 0.6 * math.exp(-0.3 * layer)
        proj = jnp.einsum('bsd,de->bse', x, w_in[layer])
        z, xbc, dt_raw, dq, dk, dv, dg, cq, cg, gl = jnp.split(proj, splits, axis=-1)

        y_ssd = _ssd_branch(z, xbc, dt_raw, conv_w[layer], conv_b[layer], dt_bias[layer],
                            a_log[layer], d_skip[layer], ssd_norm_w[layer])

        lq = diff_lam[layer].astype(jnp.float32)
        lam = (jnp.exp(jnp.sum(lq[0] * lq[1])) - jnp.exp(jnp.sum(lq[2] * lq[3]))
               + lambda_init)
        q = _apply_rope(dq.reshape(Bsz, S, DIFF_HEADS, 2, DIFF_HEAD_DIM), cos, sin)
        k = _apply_rope(dk.reshape(Bsz, S, DIFF_HEADS, 2, DIFF_HEAD_DIM), cos, sin)
        v = dv.reshape(Bsz, S, DIFF_HEADS, 2 * DIFF_HEAD_DIM)
        o = _diff_attention(q, k, v, lam)
        o = _rmsnorm(o, diff_norm_w[layer]) * (1.0 - lambda_init)
        y_diff = o.reshape(Bsz, S, DIFF_WIDTH) * jax.nn.silu(dg.astype(jnp.float32))

        mem_kv = jnp.einsum('bmd,de->bme', mem, w_mem_kv[layer]).reshape(
            Bsz, MEM_TOKENS, 2, CROSS_HEADS, CROSS_HEAD_DIM)
        oc = _memory_cross_attention(cq.reshape(Bsz, S, CROSS_HEADS, CROSS_HEAD_DIM), mem_kv)
        y_cross = oc.reshape(Bsz, S, CROSS_WIDTH) * jax.nn.silu(cg.astype(jnp.float32))

        gates = jax.nn.sigmoid(gl.reshape(Bsz, S, N_BRANCH, D_MODEL).astype(jnp.float32)
                               + gate_b[layer].astype(jnp.float32))
        merged = (gates[:, :, 0] * jnp.einsum('bse,ed->bsd', y_ssd, w_br_ssd[layer])
                  + gates[:, :, 1] * jnp.einsum('bse,ed->bsd', y_diff, w_br_diff[layer])
                  + gates[:, :, 2] * jnp.einsum('bse,ed->bsd', y_cross, w_br_cross[layer]))
        out = jnp.einsum('bsd,de->bse', merged, w_out[layer])

        x = _layernorm(DEEPNORM_ALPHA * x.astype(jnp.float32) + out,
                       ln_g[layer], ln_b[layer]).astype(x.dtype)
    return x
```

```python
import math
from contextlib import ExitStack
import numpy as np
import concourse.bass as bass
import concourse.mybir as mybir
from concourse.bass_utils import run_bass_kernel_spmd

F32 = mybir.dt.float32
BF16 = mybir.dt.bfloat16
I32 = mybir.dt.int32
ALU = mybir.AluOpType
AF = mybir.ActivationFunctionType

ENGS = ["pe", "act", "dve", "pool", "sp"]

D = 1024
DEPTH = 2
IN_COLS = 14400
EPS = 1e-5
ALPHA = (2 * DEPTH) ** 0.25
NEG = -30000.0

O_Z, O_XBC, O_DT, O_DQ, O_DK, O_DV, O_DG, O_CQ, O_CG, O_GL = (
    0, 2048, 5120, 5184, 6208, 7232, 8256, 9280, 10304, 11328)
R_DTB, R_ALOG, R_DSK, R_SNW, R_DNW, R_LNG, R_LNB, R_LAM, R_GB, R_END = (
    0, 64, 128, 160, 2208, 2336, 3360, 4384, 4640, 7712)
C_ID, C_TL, C_TU, C_MF, C_MB, C_INV, C_ONE, C_END = 0, 128, 256, 384, 512, 640, 672, 800


class _Rec:
    def __init__(self):
        self.call = None

    def __getattr__(self, name):
        def f(*args, **kw):
            self.call = (name, args, kw)
            return self
        return f


class Sched:
    def __init__(self, nc, stack):
        self.nc = nc
        self.stack = stack
        self.esem = {e: stack.enter_context(nc.semaphore("s_" + e)) for e in ENGS}
        self.count = {e: 0 for e in ENGS}
        self.dsem = {}
        self.waited = {e: {} for e in ENGS}
        self.last_w = {}
        self.readers = {}
        self.ops = {e: [] for e in ENGS}
        self.ninstr = 0

    def _sem(self, key):
        if key in self.esem:
            return self.esem[key]
        return self.dsem[key][0]

    def _deps(self, reads, writes):
        deps = {}

        def add(k, v):
            if deps.get(k, 0) < v:
                deps[k] = v
        for k in reads:
            if k in self.last_w:
                add(*self.last_w[k])
        for k in writes:
            if k in self.last_w:
                add(*self.last_w[k])
            for kk, vv in self.readers.get(k, {}).items():
                add(kk, vv)
        return deps

    def _emit_waits(self, e, deps, skip_self=False):
        for k, v in deps.items():
            if skip_self and k == e:
                continue
            if self.waited[e].get(k, 0) >= v:
                continue
            self.waited[e][k] = v
            sem = self._sem(k)
            self.ops[e].append(lambda eng, sem=sem, v=v: eng.wait_ge(sem, v))

    def _record(self, key, val, reads, writes):
        for k in writes:
            self.last_w[k] = (key, val)
            self.readers[k] = {}
        for k in reads:
            r = self.readers.setdefault(k, {})
            if r.get(key, 0) < val:
                r[key] = val

    def op(self, e, fn, reads=(), writes=()):
        rec = _Rec()
        fn(rec)
        name, args, kw = rec.call
        deps = self._deps(reads, writes)
        self._emit_waits(e, deps, skip_self=(e == "pe"))
        self.count[e] += 1
        val = self.count[e]
        sem = self.esem[e]
        self.ops[e].append(lambda eng, name=name, args=args, kw=kw, sem=sem:
                           getattr(eng, name)(*args, **kw).then_inc(sem, 1))
        self._record(e, val, reads, writes)
        self.ninstr += 1

    def dma(self, q, slot, out, in_, reads=(), writes=(), **kw):
        slot = q + "_" + slot
        if slot not in self.dsem:
            h = self.stack.enter_context(self.nc.semaphore("d_" + slot))
            self.dsem[slot] = [h, 0]
        deps = self._deps(reads, writes)
        self._emit_waits(q, deps)
        self.dsem[slot][1] += 16
        val = self.dsem[slot][1]
        sem = self.dsem[slot][0]
        self.ops[q].append(
            lambda eng, out=out, in_=in_, sem=sem, kw=kw:
            eng.dma_start(out=out, in_=in_, **kw).then_inc(sem, 16))
        self._record(slot, val, reads, writes)
        self.ninstr += 1

    def flush(self):
        alld = {e: self.count[e] for e in ENGS if self.count[e] > 0 and e != "sp"}
        for k, (h, c) in self.dsem.items():
            if c > 0:
                alld[k] = c
        self._emit_waits("sp", alld)
        ops = self.ops
        self.ops = {e: [] for e in ENGS}
        with self.nc.Block() as block:
            @block.tensor
            def _(eng):
                for f in ops["pe"]:
                    f(eng)

            @block.scalar
            def _(eng):
                for f in ops["act"]:
                    f(eng)

            @block.vector
            def _(eng):
                for f in ops["dve"]:
                    f(eng)

            @block.gpsimd
            def _(eng):
                for f in ops["pool"]:
                    f(eng)

            @block.sync
            def _(eng):
                for f in ops["sp"]:
                    f(eng)
        for e in ENGS:
            for k, v in alld.items():
                if self.waited[e].get(k, 0) < v:
                    self.waited[e][k] = v
        self.last_w = {}
        self.readers = {}


def bc(ap, shape):
    return ap.broadcast_to(shape)


def build_program(SEQ, dbg=()):
    NT = SEQ // 128
    NB = SEQ // 512
    nc = bass.Bass("TRN2", target_bir_lowering=False)

    def din(name, shape, dt=F32):
        return nc.dram_tensor(name, shape, dt, kind="ExternalInput").ap()

    x_in = din("x", [SEQ, D])
    mem_in = din("mem", [256, D])
    pos_in = din("pos", [128, NT], I32)
    w_in = din("w_in", [DEPTH, D, IN_COLS])
    w_kv = din("w_kv", [DEPTH, D, 2048])
    w_br = din("w_br", [DEPTH, 4096, D])
    w_out = din("w_out", [DEPTH, D, D])
    rowp = din("rowp", [DEPTH, 128, R_END])
    colp = din("colp", [DEPTH, 128, 24, 6])
    consts = din("consts", [128, C_END])
    out_ap = nc.dram_tensor("out", [SEQ, D], F32, kind="ExternalOutput").ap()

    def scr(name, shape, dt):
        kind = "ExternalOutput" if name in dbg else "Internal"
        return nc.dram_tensor(name, shape, dt, kind=kind).ap()

    xT_s = scr("xT_s", [128, 8, SEQ], BF16)
    raw_s = scr("raw_s", [2, 12, 128, SEQ + 4], F32)
    siluz_s = scr("siluz_s", [SEQ, 2048], BF16)
    dt_s = scr("dt_s", [SEQ, 2, 32], F32)
    qT_s = scr("qT_s", [8, 128, SEQ], BF16)
    kT_s = scr("kT_s", [8, 128, SEQ], BF16)
    v_s = scr("v_s", [8, SEQ, 130], BF16)
    sdg_s = scr("sdg_s", [SEQ, 1024], BF16)
    ycross_s = scr("ycross_s", [SEQ, 1024], BF16)
    xs_s = scr("xs_s", [2, SEQ, 1024], BF16)
    btm_s = scr("btm_s", [2, SEQ, 256], BF16)
    bt_s = scr("bt_s", [2, 2, 128, SEQ], BF16)
    ct_s = scr("ct_s", [2, 2, 128, SEQ], BF16)
    ypart_s = scr("ypart_s", [2, SEQ, 1024], F32)
    yssd_s = scr("yssd_s", [SEQ, 2048], BF16)
    ss_s = scr("ss_s", [2, 128, NT], F32)
    ydiff_s = scr("ydiff_s", [SEQ, 1024], BF16)
    mT_s = scr("mT_s", [128, 8, SEQ], BF16)
    x1_s = scr("x1_s", [SEQ, D], F32)

    with ExitStack() as st:
        S = Sched(nc, st)

        uniq = [0]

        def sb(name, shape, dt, stack=st):
            uniq[0] += 1
            return stack.enter_context(nc.sbuf_tensor("%s_%d" % (name, uniq[0]), shape, dt))

        PB = [st.enter_context(nc.psum_tensor("pb%d" % i, [128, 512], F32)) for i in range(8)]
        pbk = ["pb%d" % i for i in range(8)]
        pctr = [0]

        def nextbank():
            i = pctr[0] % 8
            pctr[0] += 1
            return i

        def pbf(i):
            return PB[i][:].bitcast(BF16)

        cpy_ctr = [0]

        def evac(out, in_, reads, writes, eng=None):
            if eng is None:
                eng = "act" if cpy_ctr[0] % 2 == 0 else "dve"
                cpy_ctr[0] += 1
            if eng == "act":
                S.op("act", lambda e: e.copy(out, in_), reads, writes)
            else:
                S.op(eng, lambda e: e.tensor_copy(out, in_), reads, writes)

        def mm(out, lhsT, rhs, start, stop, reads, writes):
            S.op("pe", lambda e: e.matmul(out, lhsT, rhs, start=start, stop=stop), reads, writes)

        def tr(out, in_, ident, reads, writes):
            S.op("pe", lambda e: e.transpose(out, in_, ident), reads, writes)

        ident_b = sb("ident_b", [128, 128], BF16)
        ones_b = sb("ones_b", [128, 128], BF16)
        ones_f = sb("ones_f", [128, 128], F32)
        tri_f = sb("tri_f", [128, 2, 128], F32)
        tri_b = sb("tri_b", [128, 2, 128], BF16)
        mb_b = sb("mb_b", [128, 2, 4, 128], BF16)
        cos_t = sb("cos_t", [128, NT, 32], F32)
        sin_t = sb("sin_t", [128, NT, 32], F32)
        mkT = sb("mkT", [128, 8, 256], BF16)
        mv_aug = sb("mv_aug", [128, 2, 4, 258], BF16)
        neg_lam = sb("neg_lam", [128, 1], F32)
        A_all = sb("A_all", [128, 64], F32)

        with ExitStack() as ls:
            cst = sb("cst", [128, C_END], F32, ls)
            posi = sb("posi", [128, NT], I32, ls)
            posf = sb("posf", [128, NT], F32, ls)
            ang = sb("ang", [128, NT, 32], F32, ls)
            kf = sb("kf", [128, NT, 32], F32, ls)
            ki = sb("ki", [128, NT, 32], I32, ls)
            a2 = sb("a2", [128, NT, 32], F32, ls)
            S.dma("sp", "pl0", cst[:], consts, writes=["cst"])
            S.dma("sp", "pl1", posi[:], pos_in, writes=["posi"])
            S.op("dve", lambda e: e.tensor_copy(ident_b[:], cst[:, C_ID:C_ID + 128]), ["cst"], ["ident_b"])
            S.op("dve", lambda e: e.tensor_copy(ones_b[:], cst[:, C_ONE:C_ONE + 128]), ["cst"], ["ones_b"])
            S.op("dve", lambda e: e.tensor_copy(ones_f[:], cst[:, C_ONE:C_ONE + 128]), ["cst"], ["ones_f"])
            S.op("dve", lambda e: e.tensor_copy(tri_f[:, 0, :], cst[:, C_TL:C_TL + 128]), ["cst"], ["tri0"])
            S.op("dve", lambda e: e.tensor_copy(tri_f[:, 1, :], cst[:, C_TU:C_TU + 128]), ["cst"], ["tri1"])
            S.op("dve", lambda e: e.tensor_copy(tri_b[:, 0, :], cst[:, C_TL:C_TL + 128]), ["cst"], ["trib0"])
            S.op("dve", lambda e: e.tensor_copy(tri_b[:, 1, :], cst[:, C_TU:C_TU + 128]), ["cst"], ["trib1"])
            for d_ in range(2):
                off = C_MF if d_ == 0 else C_MB
                S.op("dve", lambda e, d_=d_, off=off: e.tensor_copy(
                    mb_b[:, d_, :, :], bc(cst[:, off:off + 128].unsqueeze(1), [128, 4, 128])),
                    ["cst"], ["mb%d" % d_])
            S.op("dve", lambda e: e.tensor_copy(posf[:], posi[:]), ["posi"], ["posf"])
            S.op("dve", lambda e: e.tensor_tensor(
                out=ang[:], in0=bc(posf[:].unsqueeze(2), [128, NT, 32]),
                in1=bc(cst[:, C_INV:C_INV + 32].unsqueeze(1), [128, NT, 32]), op=ALU.mult),
                ["posf", "cst"], ["ang"])
            C1 = float(np.float32(2 * math.pi))
            C2 = float(2 * math.pi - C1)
            S.op("dve", lambda e: e.tensor_scalar(kf[:], ang[:], 1.0 / (2 * math.pi), None, op0=ALU.mult), ["ang"], ["kf"])
            S.op("dve", lambda e: e.tensor_copy(ki[:], kf[:]), ["kf"], ["ki"])
            S.op("dve", lambda e: e.tensor_copy(kf[:], ki[:]), ["ki"], ["kf"])
            S.op("dve", lambda e: e.scalar_tensor_tensor(out=ang[:], in0=kf[:], scalar=-C1, in1=ang[:], op0=ALU.mult, op1=ALU.add), ["kf", "ang"], ["ang"])
            S.op("dve", lambda e: e.scalar_tensor_tensor(out=ang[:], in0=kf[:], scalar=-C2, in1=ang[:], op0=ALU.mult, op1=ALU.add), ["kf", "ang"], ["ang"])

            def wrap_sin(dst, shift, key):
                S.op("dve", lambda e: e.tensor_scalar(a2[:], ang[:], shift, None, op0=ALU.add), ["ang", "a2"], ["a2"])
                S.op("dve", lambda e: e.tensor_scalar(kf[:], a2[:], math.pi, None, op0=ALU.is_gt), ["a2", "kf"], ["kf"])
                S.op("dve", lambda e: e.scalar_tensor_tensor(out=a2[:], in0=kf[:], scalar=-2 * math.pi, in1=a2[:], op0=ALU.mult, op1=ALU.add), ["kf", "a2"], ["a2"])
                S.op("dve", lambda e: e.tensor_scalar(kf[:], a2[:], -math.pi, None, op0=ALU.is_lt), ["a2", "kf"], ["kf"])
                S.op("dve", lambda e: e.scalar_tensor_tensor(out=a2[:], in0=kf[:], scalar=2 * math.pi, in1=a2[:], op0=ALU.mult, op1=ALU.add), ["kf", "a2"], ["a2"])
                S.op("act", lambda e: e.activation(dst[:], a2[:], AF.Sin), ["a2"], [key])
            wrap_sin(sin_t, 0.0, "sin_t")
            wrap_sin(cos_t, math.pi / 2, "cos_t")
            S.flush()

        for L in range(DEPTH):
            xsrc = x_in if L == 0 else x1_s
            xdst = x1_s if L == 0 else out_ap
            lam_init = 0.8 - 0.6 * math.exp(-0.3 * L)

            with ExitStack() as ls:
                lamt = sb("lamt", [128, 256], F32, ls)
                lp = sb("lp", [128, 2, 64], F32, ls)
                ls2 = sb("ls2", [128, 2], F32, ls)
                alog = sb("alog", [128, 64], F32, ls)
                memb = sb("memb", [128, 2, D], BF16, ls)
                memT = sb("memT", [128, 8, 256], BF16, ls)
                wkv = sb("wkv", [128, 8, 2048], BF16, ls)
                S.dma("sp", "a0", lamt[:], rowp[L, :, R_LAM:R_LAM + 256], writes=["lamt"])
                S.dma("sp", "a1", alog[:], rowp[L, :, R_ALOG:R_ALOG + 64], writes=["alog"])
                S.dma("pool", "a2", memb[:], mem_in.rearrange("(c p) d -> p c d", p=128), writes=["memb"])
                S.dma("pool", "a3", wkv[:], w_kv[L].rearrange("(k p) c -> p k c", p=128), writes=["wkv"])
                lv = lamt[:].rearrange("p (a b) -> p a b", a=4)
                S.op("dve", lambda e: e.tensor_tensor(out=lp[:, 0, :], in0=lv[:, 0, :], in1=lv[:, 1, :], op=ALU.mult), ["lamt"], ["lp0"])
                S.op("dve", lambda e: e.tensor_tensor(out=lp[:, 1, :], in0=lv[:, 2, :], in1=lv[:, 3, :], op=ALU.mult), ["lamt"], ["lp1"])
                S.op("dve", lambda e: e.reduce_sum(ls2[:], lp[:], axis=mybir.AxisListType.X), ["lp0", "lp1"], ["ls2"])
                S.op("act", lambda e: e.activation(ls2[:], ls2[:], AF.Exp), ["ls2"], ["ls2"])
                S.op("dve", lambda e: e.tensor_tensor(out=neg_lam[:], in0=ls2[:, 1:2], in1=ls2[:, 0:1], op=ALU.subtract), ["ls2"], ["neg_lam"])
                S.op("dve", lambda e: e.tensor_scalar(neg_lam[:], neg_lam[:], -lam_init, None, op0=ALU.add), ["neg_lam"], ["neg_lam"])
                S.op("act", lambda e: e.activation(A_all[:], alog[:], AF.Exp), ["alog"], ["A_all"])
                S.op("dve", lambda e: e.tensor_scalar(A_all[:], A_all[:], -1.0, None, op0=ALU.mult), ["A_all"], ["A_all"])
                for mc in range(2):
                    bi = nextbank()
                    pv = pbf(bi).rearrange("p (a b) -> p a b", a=8)
                    for k in range(8):
                        tr(pv[:, k, :], memb[:, mc, k * 128:(k + 1) * 128], ident_b[:], ["memb", "ident_b"], [pbk[bi]])
                    evac(memT[:, :, mc * 128:(mc + 1) * 128], pv, [pbk[bi]], ["memT"])
                for j in range(8):
                    bi = nextbank()
                    for k in range(8):
                        mm(PB[bi][:, 0:256], wkv[:, k, j * 128:(j + 1) * 128], memT[:, k, :], k == 0, k == 7,
                           ["wkv", "memT"], [pbk[bi]])
                    evac(mkT[:, j, :], PB[bi][:, 0:256], [pbk[bi]], ["mkT"])
                S.op("pool", lambda e: e.memset(mv_aug[:, :, :, 256:257], 1.0), [], ["mv_aug"])
                S.op("pool", lambda e: e.memset(mv_aug[:, :, :, 257:258], 0.0), [], ["mv_aug"])
                for mc in range(2):
                    for cb in range(2):
                        bi = nextbank()
                        for k in range(8):
                            mm(PB[bi][:], memT[:, k, mc * 128:(mc + 1) * 128], wkv[:, k, 1024 + cb * 512:1024 + (cb + 1) * 512],
                               k == 0, k == 7, ["wkv", "memT"], [pbk[bi]])
                        evac(mv_aug[:, mc, cb * 2:(cb + 1) * 2, 0:256], PB[bi][:].rearrange("p (a b) -> p a b", a=2),
                             [pbk[bi]], ["mv_aug"])
                S.flush()

            with ExitStack() as ls:
                xb = [sb("xb%d" % i, [128, 4, D], BF16, ls) for i in range(2)]
                xTb = [sb("xTb%d" % i, [128, 8, 512], BF16, ls) for i in range(2)]
                for b in range(NB):
                    s = b % 2
                    S.dma("pool", "xb%d" % s, xb[s][:], xsrc[b * 512:(b + 1) * 512, :].rearrange("(t p) d -> p t d", p=128),
                          writes=["xb%d" % s])
                    for tt in range(4):
                        bi = nextbank()
                        pv = pbf(bi).rearrange("p (a b) -> p a b", a=8)
                        for k in range(8):
                            tr(pv[:, k, :], xb[s][:, tt, k * 128:(k + 1) * 128], ident_b[:], ["xb%d" % s], [pbk[bi]])
                        evac(xTb[s][:, :, tt * 128:(tt + 1) * 128], pv, [pbk[bi]], ["xTb%d" % s])
                    S.dma("sp", "xTb%d" % s, xT_s[:, :, b * 512:(b + 1) * 512], xTb[s][:], reads=["xTb%d" % s])
                S.flush()

            for hh in range(2):
                with ExitStack() as ls:
                    wfm = sb("wfm", [128, 8, 1536], BF16, ls)
                    xTb = [sb("xTb%d" % i, [128, 8, 512], BF16, ls) for i in range(2)]
                    stg = [sb("stg%d" % i, [128, 12, 512], F32, ls) for i in range(2)]
                    zt = sb("zt", [128, 12, 2], F32, ls)
                    wv = w_in[L].rearrange("(k p) c -> p k c", p=128)
                    S.dma("pool", "w0", wfm[:, :, 0:1024], wv[:, :, O_XBC + hh * 1024:O_XBC + (hh + 1) * 1024], writes=["wfm"])
                    S.dma("pool", "w0", wfm[:, :, 1024:1280], wv[:, :, O_XBC + 2048 + hh * 256:O_XBC + 2048 + (hh + 1) * 256], writes=["wfm"])
                    S.dma("pool", "w0", wfm[:, :, 1280:1536], wv[:, :, O_XBC + 2560 + hh * 256:O_XBC + 2560 + (hh + 1) * 256], writes=["wfm"])
                    S.op("pool", lambda e: e.memset(zt[:], 0.0), [], ["zt"])
                    rv = raw_s[hh].rearrange("c p s -> p c s")
                    S.dma("sp", "zt", rv[:, :, 0:2], zt[:], reads=["zt"])
                    S.dma("sp", "zt", rv[:, :, SEQ + 2:SEQ + 4], zt[:], reads=["zt"])
                    for b in range(NB):
                        s = b % 2
                        S.dma("sp", "xTb%d" % s, xTb[s][:], xT_s[:, :, b * 512:(b + 1) * 512], writes=["xTb%d" % s])
                        for c in range(12):
                            bi = nextbank()
                            for k in range(8):
                                mm(PB[bi][:], wfm[:, k, c * 128:(c + 1) * 128], xTb[s][:, k, :], k == 0, k == 7,
                                   ["wfm", "xTb%d" % s], [pbk[bi]])
                            evac(stg[s][:, c, :], PB[bi][:], [pbk[bi]], ["stg%d" % s])
                        S.dma("sp", "stg%d" % s, rv[:, :, 2 + b * 512:2 + (b + 1) * 512], stg[s][:], reads=["stg%d" % s])
                    S.flush()

                with ExitStack() as ls:
                    wtm = sb("wtm", [128, 8, 4128], BF16, ls)
                    dtb = sb("dtb", [128, 32], F32, ls)
                    xTb = [sb("xTb%d" % i, [128, 8, 512], BF16, ls) for i in range(2)]
                    zs = [sb("zs%d" % i, [128, 4, 1024], BF16, ls) for i in range(2)]
                    dts = [sb("dts%d" % i, [128, 4, 32], F32, ls) for i in range(2)]
                    qTs = [sb("qTs%d" % i, [128, 4, 512], BF16, ls) for i in range(2)]
                    kTs = [sb("kTs%d" % i, [128, 4, 512], BF16, ls) for i in range(2)]
                    vs = [sb("vs%d" % i, [128, 4, 4, 130], BF16, ls) for i in range(2)]
                    dgs = [sb("dgs%d" % i, [128, 4, 512], BF16, ls) for i in range(2)]
                    ycs = [sb("ycs%d" % i, [128, 4, 512], BF16, ls) for i in range(2)]
                    dtx = sb("dtx", [128, 32], F32, ls)
                    ra = sb("ra", [128, 8, 32], F32, ls)
                    rb = sb("rb", [128, 8, 32], F32, ls)
                    qr = sb("qr", [128, 512], BF16, ls)
                    cqb = sb("cqb", [128, 512], BF16, ls)
                    cqT = sb("cqT", [128, 4, 128], BF16, ls)
                    pTc = sb("pTc", [128, 4, 128], BF16, ls)
                    cgs = sb("cgs", [128, 512], F32, ls)
                    rc = sb("rc", [128, 2], F32, ls)
                    wv = w_in[L].rearrange("(k p) c -> p k c", p=128)
                    segs = [(0, O_Z + hh * 1024, 1024), (1024, O_DT + hh * 16, 16), (1040, O_DT + 32 + hh * 16, 16),
                            (1056, O_DQ + hh * 512, 512), (1568, O_DK + hh * 512, 512), (2080, O_DV + hh * 512, 512),
                            (2592, O_DG + hh * 512, 512), (3104, O_CQ + hh * 512, 512), (3616, O_CG + hh * 512, 512)]
                    for (o, so, n) in segs:
                        S.dma("pool", "w0", wtm[:, :, o:o + n], wv[:, :, so:so + n], writes=["wtm"])
                    S.dma("sp", "a0", dtb[:, 0:16], rowp[L, :, R_DTB + hh * 16:R_DTB + hh * 16 + 16], writes=["dtb"])
                    S.dma("sp", "a0", dtb[:, 16:32], rowp[L, :, R_DTB + 32 + hh * 16:R_DTB + 32 + hh * 16 + 16], writes=["dtb"])
                    for i in range(2):
                        S.op("pool", lambda e, i=i: e.memset(vs[i][:, :, :, 128:129], 1.0), [], ["vs%d" % i])
                        S.op("pool", lambda e, i=i: e.memset(vs[i][:, :, :, 129:130], 0.0), [], ["vs%d" % i])

                    def grp(bi, width, t_lhs, col0, xk):
                        for k in range(8):
                            mm(PB[bi][:, 0:width], t_lhs(k), wtm[:, k, col0:col0 + width], k == 0, k == 7, ["wtm", xk], [pbk[bi]])

                    def rope(bi, dst, tkey, t):
                        qv = PB[bi][:].rearrange("p (a h j) -> p a h j", a=8, h=2)
                        t1, t2 = qv[:, :, 0, :], qv[:, :, 1, :]
                        cb_ = bc(cos_t[:, t, :].unsqueeze(1), [128, 8, 32])
                        sb_ = bc(sin_t[:, t, :].unsqueeze(1), [128, 8, 32])
                        dv = dst[:].rearrange("p (a h j) -> p a h j", a=8, h=2)
                        S.op("dve", lambda e: e.tensor_tensor(out=ra[:], in0=t1, in1=cb_, op=ALU.mult), [pbk[bi]], ["ra"])
                        S.op("dve", lambda e: e.tensor_tensor(out=rb[:], in0=t2, in1=sb_, op=ALU.mult), [pbk[bi]], ["rb"])
                        S.op("dve", lambda e: e.tensor_tensor(out=dv[:, :, 0, :], in0=ra[:], in1=rb[:], op=ALU.subtract), ["ra", "rb"], [tkey + "a"])
                        S.op("dve", lambda e: e.tensor_tensor(out=ra[:], in0=t1, in1=sb_, op=ALU.mult), [pbk[bi]], ["ra"])
                        S.op("dve", lambda e: e.tensor_tensor(out=rb[:], in0=t2, in1=cb_, op=ALU.mult), [pbk[bi]], ["rb"])
                        S.op("dve", lambda e: e.tensor_tensor(out=dv[:, :, 1, :], in0=ra[:], in1=rb[:], op=ALU.add), ["ra", "rb"], [tkey + "b"])

                    for b in range(NB):
                        s = b % 2
                        xk = "xTb%d" % s
                        S.dma("sp", xk, xTb[s][:], xT_s[:, :, b * 512:(b + 1) * 512], writes=[xk])
                        for tt in range(4):
                            t = b * 4 + tt
                            lh = lambda k, s=s, tt=tt: xTb[s][:, k, tt * 128:(tt + 1) * 128]
                            for cb in range(2):
                                bi = nextbank()
                                grp(bi, 512, lh, cb * 512, xk)
                                S.op("act", lambda e, bi=bi, cb=cb: e.activation(zs[s][:, tt, cb * 512:(cb + 1) * 512], PB[bi][:], AF.Silu),
                                     [pbk[bi]], ["zs%d" % s])
                            bi = nextbank()
                            grp(bi, 32, lh, 1024, xk)
                            S.op("dve", lambda e, bi=bi: e.tensor_tensor(out=dtx[:], in0=PB[bi][:, 0:32], in1=dtb[:], op=ALU.add),
                                 [pbk[bi], "dtb"], ["dtx"])
                            S.op("act", lambda e: e.activation(dtx[:], dtx[:], AF.Exp), ["dtx"], ["dtx"])
                            S.op("act", lambda e: e.activation(dts[s][:, tt, :], dtx[:], AF.Ln, bias=1.0), ["dtx"], ["dts%d" % s])
                            for (col0, dstT, nm) in ((1056, qTs, "qTs"), (1568, kTs, "kTs")):
                                bi = nextbank()
                                grp(bi, 512, lh, col0, xk)
                                rope(bi, qr, "qr", t)
                                b2 = nextbank()
                                pv = pbf(b2).rearrange("p (a b) -> p a b", a=8)
                                for h in range(4):
                                    tr(pv[:, h, :], qr[:, h * 128:(h + 1) * 128], ident_b[:], ["qra", "qrb"], [pbk[b2]])
                                evac(dstT[s][:, :, tt * 128:(tt + 1) * 128], pv[:, 0:4, :], [pbk[b2]], ["%s%d" % (nm, s)])
                            bi = nextbank()
                            grp(bi, 512, lh, 2080, xk)
                            evac(vs[s][:, tt, :, 0:128], PB[bi][:].rearrange("p (a b) -> p a b", a=4), [pbk[bi]], ["vs%d" % s])
                            bi = nextbank()
                            grp(bi, 512, lh, 2592, xk)
                            S.op("act", lambda e, bi=bi: e.activation(dgs[s][:, tt, :], PB[bi][:], AF.Silu), [pbk[bi]], ["dgs%d" % s])
                            bi = nextbank()
                            grp(bi, 512, lh, 3104, xk)
                            evac(cqb[:], PB[bi][:], [pbk[bi]], ["cqb"])
                            b2 = nextbank()
                            pv = pbf(b2).rearrange("p (a b) -> p a b", a=8)
                            for j in range(4):
                                tr(pv[:, j, :], cqb[:, j * 128:(j + 1) * 128], ident_b[:], ["cqb"], [pbk[b2]])
                            evac(cqT[:], pv[:, 0:4, :], [pbk[b2]], ["cqT"])
                            b3 = nextbank()
                            sv = PB[b3][:].rearrange("p (a b) -> p a b", a=4)
                            for hc in range(2):
                                for mc in range(2):
                                    for dc in range(2):
                                        mm(sv[:, hc * 2 + mc, :], mkT[:, (hh * 2 + hc) * 2 + dc, mc * 128:(mc + 1) * 128],
                                           cqT[:, hc * 2 + dc, :], dc == 0, dc == 1, ["mkT", "cqT"], [pbk[b3]])
                            S.op("act", lambda e, b3=b3: e.activation(pTc[:], PB[b3][:].rearrange("p (a b) -> p a b", a=4), AF.Exp, scale=1.0 / 16.0),
                                 [pbk[b3]], ["pTc"])
                            bi = nextbank()
                            grp(bi, 512, lh, 3616, xk)
                            S.op("act", lambda e, bi=bi: e.activation(cgs[:], PB[bi][:], AF.Silu), [pbk[bi]], ["cgs"])
                            for hc in range(2):
                                b4 = nextbank()
                                for mc in range(2):
                                    mm(PB[b4][:, 0:257], pTc[:, hc * 2 + mc, :], mv_aug[:, mc, hh * 2 + hc, 0:257], mc == 0, mc == 1,
                                       ["pTc", "mv_aug"], [pbk[b4]])
                                S.op("dve", lambda e, b4=b4, hc=hc: e.reciprocal(rc[:, hc:hc + 1], PB[b4][:, 256:257]), [pbk[b4]], ["rc%d" % hc])
                                S.op("dve", lambda e, b4=b4, hc=hc: e.scalar_tensor_tensor(
                                    out=ycs[s][:, tt, hc * 256:(hc + 1) * 256], in0=PB[b4][:, 0:256], scalar=rc[:, hc:hc + 1],
                                    in1=cgs[:, hc * 256:(hc + 1) * 256], op0=ALU.mult, op1=ALU.mult),
                                    [pbk[b4], "rc%d" % hc, "cgs"], ["ycs%d" % s])
                        rows = slice(b * 512, (b + 1) * 512)
                        S.dma("sp", "zs%d" % s, siluz_s[rows, hh * 1024:(hh + 1) * 1024].rearrange("(t p) c -> p t c", p=128), zs[s][:], reads=["zs%d" % s])
                        S.dma("sp", "dts%d" % s, dt_s[rows, hh, :].rearrange("(t p) c -> p t c", p=128), dts[s][:], reads=["dts%d" % s])
                        S.dma("sp", "qTs%d" % s, qT_s[hh * 4:(hh + 1) * 4, :, rows].rearrange("h p s -> p h s"), qTs[s][:], reads=["qTs%d" % s])
                        S.dma("sp", "kTs%d" % s, kT_s[hh * 4:(hh + 1) * 4, :, rows].rearrange("h p s -> p h s"), kTs[s][:], reads=["kTs%d" % s])
                        for h in range(4):
                            S.dma("sp", "vs%d" % s, v_s[hh * 4 + h, rows, :].rearrange("(t p) e -> p t e", p=128), vs[s][:, :, h, :], reads=["vs%d" % s])
                        S.dma("sp", "dgs%d" % s, sdg_s[rows, hh * 512:(hh + 1) * 512].rearrange("(t p) c -> p t c", p=128), dgs[s][:], reads=["dgs%d" % s])
                        S.dma("sp", "ycs%d" % s, ycross_s[rows, hh * 512:(hh + 1) * 512].rearrange("(t p) c -> p t c", p=128), ycs[s][:], reads=["ycs%d" % s])
                    S.flush()

                with ExitStack() as ls:
                    cp = sb("cp", [128, 12, 6], F32, ls)
                    raw = [sb("raw%d" % i, [128, 12, 516], F32, ls) for i in range(2)]
                    acc = [sb("acc%d" % i, [128, 512], F32, ls) for i in range(2)]
                    xc = [sb("xc%d" % i, [128, 12, 512], BF16, ls) for i in range(2)]
                    xst = [sb("xst%d" % i, [128, 4, 1024], BF16, ls) for i in range(2)]
                    bst = [sb("bst%d" % i, [128, 4, 256], BF16, ls) for i in range(2)]
                    S.dma("sp", "a0", cp[:, 0:8, :], colp[L, :, hh * 8:(hh + 1) * 8, :], writes=["cp"])
                    S.dma("sp", "a0", cp[:, 8:10, :], colp[L, :, 16 + hh * 2:16 + (hh + 1) * 2, :], writes=["cp"])
                    S.dma("sp", "a0", cp[:, 10:12, :], colp[L, :, 20 + hh * 2:20 + (hh + 1) * 2, :], writes=["cp"])
                    rv = raw_s[hh].rearrange("c p s -> p c s")
                    for b in range(NB):
                        s = b % 2
                        S.dma("sp", "raw%d" % s, raw[s][:], rv[:, :, b * 512:b * 512 + 516], writes=["raw%d" % s])
                        for c in range(12):
                            a_ = c % 2
                            ak = "acc%d" % a_
                            S.op("dve", lambda e, c=c, a_=a_: e.tensor_scalar(acc[a_][:], raw[s][:, c, 0:512], cp[:, c, 0:1], None, op0=ALU.mult),
                                 ["raw%d" % s, "cp"], [ak])
                            for j in range(1, 5):
                                S.op("dve", lambda e, c=c, a_=a_, j=j: e.scalar_tensor_tensor(
                                    out=acc[a_][:], in0=raw[s][:, c, j:j + 512], scalar=cp[:, c, j:j + 1], in1=acc[a_][:],
                                    op0=ALU.mult, op1=ALU.add), ["raw%d" % s, "cp", ak], [ak])
                            S.op("act", lambda e, c=c, a_=a_: e.activation(xc[s][:, c, :], acc[a_][:], AF.Silu, bias=cp[:, c, 5:6]),
                                 [ak, "cp"], ["xc%d_%d" % (s, c)])
                        for tt in range(4):
                            bi = nextbank()
                            pv = pbf(bi).rearrange("p (a b) -> p a b", a=8)
                            for c in range(8):
                                tr(pv[:, c, :], xc[s][:, c, tt * 128:(tt + 1) * 128], ident_b[:], ["xc%d_%d" % (s, c)], [pbk[bi]])
                            evac(xst[s][:, tt, :].rearrange("p (a b) -> p a b", a=8), pv, [pbk[bi]], ["xst%d" % s])
                            bi = nextbank()
                            pv = pbf(bi).rearrange("p (a b) -> p a b", a=8)
                            for c in range(2):
                                tr(pv[:, c, :], xc[s][:, 8 + c, tt * 128:(tt + 1) * 128], ident_b[:], ["xc%d_%d" % (s, 8 + c)], [pbk[bi]])
                            evac(bst[s][:, tt, :].rearrange("p (a b) -> p a b", a=2), pv[:, 0:2, :], [pbk[bi]], ["bst%d" % s])
                        rows = slice(b * 512, (b + 1) * 512)
                        S.dma("sp", "xst%d" % s, xs_s[hh, rows, :].rearrange("(t p) c -> p t c", p=128), xst[s][:], reads=["xst%d" % s])
                        S.dma("sp", "bst%d" % s, btm_s[hh, rows, :].rearrange("(t p) c -> p t c", p=128), bst[s][:], reads=["bst%d" % s])
                        S.dma("sp", "xcB%d" % s, bt_s[hh, :, :, rows].rearrange("g p s -> p g s"), xc[s][:, 8:10, :],
                              reads=["xc%d_8" % s, "xc%d_9" % s])
                        S.dma("sp", "xcC%d" % s, ct_s[hh, :, :, rows].rearrange("g p s -> p g s"), xc[s][:, 10:12, :],
                              reads=["xc%d_10" % s, "xc%d_11" % s])
                    S.flush()

                for dr in range(2):
                    with ExitStack() as ls:
                        state = sb("state", [128, 2, 512], F32, ls)
                        prevb = sb("prevb", [128, 2, 512], BF16, ls)
                        dsk = sb("dsk", [128, 16], F32, ls)
                        nw = sb("nw", [128, 1024], F32, ls)
                        sscol = sb("sscol", [128, NT], F32, ls)
                        xs = [sb("xs%d" % i, [128, 1024], BF16, ls) for i in range(3)]
                        btm = [sb("btm%d" % i, [128, 256], BF16, ls) for i in range(3)]
                        bt = [sb("bt%d" % i, [128, 2, 128], BF16, ls) for i in range(3)]
                        ct = [sb("ct%d" % i, [128, 2, 128], BF16, ls) for i in range(3)]
                        dtt = [sb("dtt%d" % i, [128, 16], F32, ls) for i in range(3)]
                        ypl = [sb("ypl%d" % i, [128, 1024], F32, ls) for i in range(3)]
                        zsl = [sb("zsl%d" % i, [128, 1024], BF16, ls) for i in range(3)]
                        yo = [sb("yo%d" % i, [128, 1024], F32 if dr == 0 else BF16, ls) for i in range(2)]
                        a_t = [sb("a_t%d" % i, [128, 16], F32, ls) for i in range(2)]
                        a_hi = [sb("a_hi%d" % i, [128, 16], BF16, ls) for i in range(2)]
                        a_lo = [sb("a_lo%d" % i, [128, 16], BF16, ls) for i in range(2)]
                        ut = [sb("ut%d" % i, [128, 32], F32, ls) for i in range(2)]
                        lndt = [sb("lndt%d" % i, [128, 16], F32, ls) for i in range(2)]
                        biasM = [sb("biasM%d" % i, [128, 16], F32, ls) for i in range(2)]
                        wst = [sb("wst%d" % i, [128, 16], F32, ls) for i in range(2)]
                        eu = [sb("eu%d" % i, [128, 16], F32, ls) for i in range(2)]
                        cd = [sb("cd%d" % i, [128, 16], F32, ls) for i in range(2)]
                        rUh = sb("rUh", [128, 16, 128], BF16, ls)
                        rUl = sb("rUl", [128, 16, 128], BF16, ls)
                        dec = [sb("dec%d" % i, [128, 8, 128], BF16, ls) for i in range(2)]
                        MT = [sb("MT%d" % i, [128, 8, 128], BF16, ls) for i in range(2)]
                        xsw = [sb("xsw%d" % i, [128, 512], BF16, ls) for i in range(2)]
                        dsb = [sb("dsb%d" % i, [128, 1024], F32, ls) for i in range(2)]
                        ssb = [sb("ssb%d" % i, [128, 1024], F32, ls) for i in range(2)]
                        t1 = sb("t1", [128, 512], F32, ls)
                        t2 = sb("t2", [128, 512], F32, ls)
                        t3 = [sb("t3_%d" % i, [128, 1024], F32, ls) for i in range(2)]
                        t4 = sb("t4", [128, 1024], F32, ls)
                        yacc = sb("yacc", [128, 1024], F32, ls)
                        yg = sb("yg", [128, 1024], F32, ls)
                        junk = sb("junk", [128, 1024], BF16, ls)
                        S.op("pool", lambda e: e.memset(state[:], 0.0), [], ["state0", "state1"])
                        S.op("pool", lambda e: e.memset(prevb[:], 0.0), [], ["prevb0", "prevb1"])
                        S.dma("sp", "a0", dsk[:], rowp[L, :, R_DSK + hh * 16:R_DSK + (hh + 1) * 16], writes=["dsk"])
                        S.dma("sp", "a1", nw[:], rowp[L, :, R_SNW + hh * 1024:R_SNW + (hh + 1) * 1024], writes=["nw"])
                        Acol = A_all[:, dr * 32 + hh * 16:dr * 32 + hh * 16 + 16]
                        order = list(range(NT)) if dr == 0 else list(range(NT - 1, -1, -1))

                        def chunk_loads(it):
                            c = order[it]
                            s = it % 3
                            rows = slice(c * 128, (c + 1) * 128)
                            S.dma("sp", "dtt%d" % s, dtt[s][:], dt_s[rows, hh, dr * 16:(dr + 1) * 16], writes=["dtt%d" % s])
                            S.dma("sp", "bt%d" % s, bt[s][:], bt_s[hh, :, :, rows].rearrange("g p s -> p g s"), writes=["bt%d" % s])
                            S.dma("sp", "ct%d" % s, ct[s][:], ct_s[hh, :, :, rows].rearrange("g p s -> p g s"), writes=["ct%d" % s])
                            S.dma("sp", "xs%d" % s, xs[s][:], xs_s[hh, rows, :], writes=["xs%d" % s])
                            S.dma("sp", "btm%d" % s, btm[s][:], btm_s[hh, rows, :], writes=["btm%d" % s])
                            if dr == 1:
                                S.dma("sp", "ypl%d" % s, ypl[s][:], ypart_s[hh, rows, :], writes=["ypl%d" % s])
                                S.dma("sp", "zsl%d" % s, zsl[s][:], siluz_s[rows, hh * 1024:(hh + 1) * 1024], writes=["zsl%d" % s])

                        def early(it):
                            s = it % 2
                            ks = str(s)
                            l = it % 3
                            kl = str(l)
                            if dr == 0:
                                S.op("pool", lambda e: e.tensor_tensor(
                                    out=t3[s][:].rearrange("p (a b) -> p a b", a=16), in0=xs[l][:].rearrange("p (a b) -> p a b", a=16),
                                    in1=bc(dsk[:].unsqueeze(2), [128, 16, 64]), op=ALU.mult), ["xs" + kl, "dsk"], ["t3_" + ks])
                            S.op("dve", lambda e: e.tensor_tensor(out=a_t[s][:], in0=dtt[l][:], in1=Acol, op=ALU.mult), ["dtt" + kl, "A_all"], ["a_t" + ks])
                            S.op("dve", lambda e: e.tensor_copy(a_hi[s][:], a_t[s][:]), ["a_t" + ks], ["a_hi" + ks])
                            S.op("dve", lambda e: e.tensor_tensor(out=a_lo[s][:], in0=a_t[s][:], in1=a_hi[s][:], op=ALU.subtract),
                                 ["a_t" + ks, "a_hi" + ks], ["a_lo" + ks])
                            bu = nextbank()
                            mm(PB[bu][:, 0:16], tri_f[:, dr, :], a_t[s][:], True, True, ["a_t" + ks], [pbk[bu]])
                            mm(PB[bu][:, 16:32], ones_f[:], a_t[s][:], True, True, ["a_t" + ks], [pbk[bu]])
                            S.op("act", lambda e: e.copy(ut[s][:], PB[bu][:, 0:32]), [pbk[bu]], ["ut" + ks])
                            S.op("act", lambda e: e.activation(lndt[s][:], dtt[l][:], AF.Ln), ["dtt" + kl], ["lndt" + ks])
                            S.op("dve", lambda e: e.tensor_tensor(out=biasM[s][:], in0=lndt[s][:], in1=ut[s][:, 0:16], op=ALU.subtract),
                                 ["lndt" + ks, "ut" + ks], ["biasM" + ks])
                            S.op("dve", lambda e: e.tensor_tensor(out=wst[s][:], in0=ut[s][:, 16:32], in1=ut[s][:, 0:16], op=ALU.subtract),
                                 ["ut" + ks], ["wst" + ks])
                            S.op("act", lambda e: e.activation(wst[s][:], wst[s][:], AF.Exp), ["wst" + ks], ["wst" + ks])
                            S.op("dve", lambda e: e.tensor_tensor(out=wst[s][:], in0=wst[s][:], in1=dtt[l][:], op=ALU.mult),
                                 ["wst" + ks, "dtt" + kl], ["wst" + ks])
                            S.op("act", lambda e: e.activation(eu[s][:], ut[s][:, 0:16], AF.Exp), ["ut" + ks], ["eu" + ks])
                            S.op("act", lambda e: e.activation(cd[s][:], ut[s][:, 16:32], AF.Exp), ["ut" + ks], ["cd" + ks])
                            S.op("dve", lambda e: e.tensor_tensor(
                                out=rUh[:], in0=bc(tri_b[:, dr, :].unsqueeze(1), [128, 16, 128]),
                                in1=bc(a_hi[s][:].unsqueeze(2), [128, 16, 128]), op=ALU.mult), ["a_hi" + ks], ["rUh"])
                            S.op("dve", lambda e: e.tensor_tensor(
                                out=rUl[:], in0=bc(tri_b[:, dr, :].unsqueeze(1), [128, 16, 128]),
                                in1=bc(a_lo[s][:].unsqueeze(2), [128, 16, 128]), op=ALU.mult), ["a_lo" + ks], ["rUl"])
                            bg = nextbank()
                            for g in range(2):
                                mm(PB[bg][:, g * 128:(g + 1) * 128], bt[l][:, g, :], ct[l][:, g, :], True, True,
                                   ["bt" + kl, "ct" + kl], [pbk[bg]])
                            for g in range(2):
                                dk = "dec%d" % g
                                for hq in range(2):
                                    bq = nextbank()
                                    h0 = g * 8 + hq * 4
                                    bqv = PB[bq][:].rearrange("p (a b) -> p a b", a=4)
                                    mm(bqv, ones_b[:], rUh[:, h0:h0 + 4, :], True, False, ["rUh"], [pbk[bq]])
                                    mm(bqv, ones_b[:], rUl[:, h0:h0 + 4, :], False, False, ["rUl"], [pbk[bq]])
                                    mm(bqv, ident_b[:], mb_b[:, dr, :, :], False, True, [], [pbk[bq]])
                                    for j in range(4):
                                        S.op("act", lambda e: e.activation(
                                            dec[g][:, hq * 4 + j, :], PB[bq][:, j * 128:(j + 1) * 128], AF.Exp,
                                            bias=biasM[s][:, h0 + j:h0 + j + 1]), [pbk[bq], "biasM" + ks], [dk])
                                S.op("dve", lambda e: e.tensor_tensor(
                                    out=MT[g][:], in0=dec[g][:], in1=bc(PB[bg][:, g * 128:(g + 1) * 128].unsqueeze(1), [128, 8, 128]),
                                    op=ALU.mult), [dk, pbk[bg]], ["MT%d" % g])
                                bd = nextbank()
                                for h in range(8):
                                    mm(PB[bd][:, h * 64:(h + 1) * 64], MT[g][:, h, :], xs[l][:, (g * 8 + h) * 64:(g * 8 + h + 1) * 64],
                                       True, True, ["MT%d" % g, "xs" + kl], [pbk[bd]])
                                S.op("act", lambda e: e.copy(dsb[s][:, g * 512:(g + 1) * 512], PB[bd][:]), [pbk[bd]], ["dsb%d_%d" % (s, g)])
                                S.op("dve", lambda e: e.tensor_tensor(
                                    out=xsw[g][:].rearrange("p (a b) -> p a b", a=8),
                                    in0=xs[l][:, g * 512:(g + 1) * 512].rearrange("p (a b) -> p a b", a=8),
                                    in1=bc(wst[s][:, g * 8:(g + 1) * 8].unsqueeze(2), [128, 8, 64]), op=ALU.mult),
                                    ["xs" + kl, "wst" + ks], ["xsw%d" % g])
                                bs_ = nextbank()
                                mm(PB[bs_][:], btm[l][:, g * 128:(g + 1) * 128], xsw[g][:], True, True, ["btm" + kl, "xsw%d" % g], [pbk[bs_]])
                                S.op("act", lambda e: e.copy(ssb[s][:, g * 512:(g + 1) * 512], PB[bs_][:]), [pbk[bs_]], ["ssb%d_%d" % (s, g)])

                        def late(it):
                            c = order[it]
                            s = it % 2
                            ks = str(s)
                            l = it % 3
                            kl = str(l)
                            rows = slice(c * 128, (c + 1) * 128)
                            for g in range(2):
                                bo = nextbank()
                                mm(PB[bo][:], ct[l][:, g, :], prevb[:, g, :], True, True, ["ct" + kl, "prevb%d" % g], [pbk[bo]])
                                S.op("dve", lambda e: e.tensor_tensor(
                                    out=t1[:].rearrange("p (a b) -> p a b", a=8), in0=PB[bo][:].rearrange("p (a b) -> p a b", a=8),
                                    in1=bc(eu[s][:, g * 8:(g + 1) * 8].unsqueeze(2), [128, 8, 64]), op=ALU.mult), [pbk[bo], "eu" + ks], ["t1"])
                                S.op("dve", lambda e: e.tensor_tensor(out=yacc[:, g * 512:(g + 1) * 512], in0=t1[:],
                                                                      in1=dsb[s][:, g * 512:(g + 1) * 512], op=ALU.add),
                                     ["t1", "dsb%d_%d" % (s, g)], ["yacc%d" % g])
                                S.op("pool", lambda e: e.tensor_tensor(
                                    out=t2[:].rearrange("p (a b) -> p a b", a=8), in0=state[:, g, :].rearrange("p (a b) -> p a b", a=8),
                                    in1=bc(cd[s][:, g * 8:(g + 1) * 8].unsqueeze(2), [128, 8, 64]), op=ALU.mult), ["state%d" % g, "cd" + ks], ["t2"])
                                S.op("dve", lambda e: e.tensor_tensor(out=state[:, g, :], in0=t2[:], in1=ssb[s][:, g * 512:(g + 1) * 512], op=ALU.add),
                                     ["t2", "ssb%d_%d" % (s, g)], ["state%d" % g])
                                S.op("act", lambda e: e.copy(prevb[:, g, :], state[:, g, :]), ["state%d" % g], ["prevb%d" % g])
                            if dr == 0:
                                S.op("dve", lambda e: e.tensor_tensor(out=yo[s][:], in0=yacc[:], in1=t3[s][:], op=ALU.add),
                                     ["yacc0", "yacc1", "t3_" + ks], ["yo" + ks])
                                S.dma("sp", "yo%d" % s, ypart_s[hh, rows, :], yo[s][:], reads=["yo" + ks])
                            else:
                                S.op("pool", lambda e: e.tensor_tensor(out=t4[:], in0=yacc[:], in1=ypl[l][:], op=ALU.add),
                                     ["yacc0", "yacc1", "ypl" + kl], ["t4"])
                                S.op("dve", lambda e: e.tensor_tensor(out=yg[:], in0=t4[:], in1=zsl[l][:], op=ALU.mult), ["t4", "zsl" + kl], ["yg"])
                                S.op("act", lambda e: e.activation(junk[:], yg[:], AF.Square, accum_out=sscol[:, c:c + 1]), ["yg"], ["junk", "sscol"])
                                S.op("dve", lambda e: e.tensor_tensor(out=yo[s][:], in0=yg[:], in1=nw[:], op=ALU.mult), ["yg", "nw"], ["yo" + ks])
                                S.dma("sp", "yo%d" % s, yssd_s[rows, hh * 1024:(hh + 1) * 1024], yo[s][:], reads=["yo" + ks])

                        for it in range(min(3, NT)):
                            chunk_loads(it)
                        early(0)
                        for it in range(NT):
                            if it + 1 < NT:
                                early(it + 1)
                            late(it)
                            if it + 3 < NT:
                                chunk_loads(it + 3)
                        if dr == 1:
                            S.dma("sp", "a2", ss_s[hh], sscol[:], reads=["sscol"])
                        S.flush()

                with ExitStack() as ls:
                    qTh = [sb("qTh%d" % i, [128, SEQ], BF16, ls) for i in range(2)]
                    kz = [[sb("kz%d_%d" % (i, c), [128, SEQ], BF16, ls) for c in range(2)] for i in range(2)]
                    vh = [sb("vh%d" % i, [128, NT, 130], BF16, ls) for i in range(2)]
                    pT = [sb("pT%d" % i, [128, 512], BF16, ls) for i in range(3)]
                    dnw = sb("dnw", [128, 128], F32, ls)
                    sdg = [sb("sdg%d" % i, [128, 4, 128], BF16, ls) for i in range(2)]
                    o0 = sb("o0", [128, 4, 128], F32, ls)
                    o1 = sb("o1", [128, 4, 128], F32, ls)
                    rs = sb("rs", [128, 8], F32, ls)
                    ssq = sb("ssq", [128, 4], F32, ls)
                    jk = sb("jk", [128, 128], BF16, ls)
                    yd = [sb("yd%d" % i, [128, 4, 128], BF16, ls) for i in range(2)]
                    S.dma("sp", "a0", dnw[:], rowp[L, :, R_DNW:R_DNW + 128], writes=["dnw"])
                    S.op("dve", lambda e: e.tensor_scalar(dnw[:], dnw[:], 1.0 - lam_init, None, op0=ALU.mult), ["dnw"], ["dnw"])
                    ACC = [4, 5, 6, 7]
                    items = [(hl, qb, c, kt) for hl in range(4) for qb in range(NB) for c in range(2) for kt in range(NT)]
                    nit = len(items)

                    def load_head(hl):
                        hs = hl % 2
                        hg = hh * 4 + hl
                        S.dma("sp", "qTh%d" % hs, qTh[hs][:], qT_s[hg], writes=["qTh%d" % hs])
                        S.dma("sp", "kTh%d" % hs, kz[hs][0][0:64, :], kT_s[hg, 0:64, :], writes=["kTh%d" % hs])
                        S.dma("sp", "kTh%d" % hs, kz[hs][1][64:128, :], kT_s[hg, 64:128, :], writes=["kTh%d" % hs])
                        S.dma("sp", "vh%d" % hs, vh[hs][:], v_s[hg].rearrange("(t p) e -> p t e", p=128), writes=["vh%d" % hs])

                    def qk(i):
                        hl, qb, c, kt = items[i]
                        hs = hl % 2
                        bsx = i % 4
                        mm(PB[bsx][:], kz[hs][c][:, kt * 128:(kt + 1) * 128], qTh[hs][:, qb * 512:(qb + 1) * 512], True, True,
                           ["qTh%d" % hs, "kTh%d" % hs], [pbk[bsx]])

                    def evac_acc(hl, qb, c):
                        for qt in range(4):
                            a = ACC[qt]
                            S.op("dve", lambda e: e.reciprocal(rs[:, c * 4 + qt:c * 4 + qt + 1], PB[a][:, 128:129]),
                                 [pbk[a]], ["rs%d_%d" % (c, qt)])
                            if c == 0:
                                S.op("dve", lambda e: e.tensor_scalar(o0[:, qt, :], PB[a][:, 0:128], rs[:, qt:qt + 1], None, op0=ALU.mult),
                                     [pbk[a], "rs0_%d" % qt], ["o0_%d" % qt])
                            else:
                                S.op("dve", lambda e: e.tensor_tensor(out=rs[:, 4 + qt:5 + qt], in0=rs[:, 4 + qt:5 + qt], in1=neg_lam[:], op=ALU.mult),
                                     ["rs1_%d" % qt, "neg_lam"], ["rs1_%d" % qt])
                                S.op("dve", lambda e: e.scalar_tensor_tensor(
                                    out=o1[:, qt, :], in0=PB[a][:, 0:128], scalar=rs[:, 4 + qt:5 + qt], in1=o0[:, qt, :],
                                    op0=ALU.mult, op1=ALU.add), [pbk[a], "rs1_%d" % qt, "o0_%d" % qt], ["o1_%d" % qt])

                    def post(hl, qb):
                        ds_ = qb % 2
                        hg = hh * 4 + hl
                        for qt in range(4):
                            S.op("act", lambda e: e.activation(jk[:], o1[:, qt, :], AF.Square, accum_out=ssq[:, qt:qt + 1]),
                                 ["o1_%d" % qt], ["jk", "ssq%d" % qt])
                            S.op("act", lambda e: e.activation(ssq[:, qt:qt + 1], ssq[:, qt:qt + 1], AF.Ln, scale=1.0 / 128.0, bias=EPS),
                                 ["ssq%d" % qt], ["ssq%d" % qt])
                            S.op("act", lambda e: e.activation(ssq[:, qt:qt + 1], ssq[:, qt:qt + 1], AF.Exp, scale=-0.5),
                                 ["ssq%d" % qt], ["ssq%d" % qt])
                            S.op("dve", lambda e: e.scalar_tensor_tensor(
                                out=o1[:, qt, :], in0=o1[:, qt, :], scalar=ssq[:, qt:qt + 1], in1=dnw[:],
                                op0=ALU.mult, op1=ALU.mult), ["o1_%d" % qt, "ssq%d" % qt, "dnw"], ["o1_%d" % qt])
                            S.op("dve", lambda e: e.tensor_tensor(out=yd[ds_][:, qt, :], in0=o1[:, qt, :], in1=sdg[ds_][:, qt, :], op=ALU.mult),
                                 ["o1_%d" % qt, "sdg%d" % ds_], ["yd%d" % ds_])
                        S.dma("pool", "yd%d" % ds_, ydiff_s[qb * 512:(qb + 1) * 512, hg * 128:(hg + 1) * 128].rearrange("(t p) c -> p t c", p=128),
                              yd[ds_][:], reads=["yd%d" % ds_])

                    for i_ in range(2):
                        S.op("pool", lambda e: e.memset(kz[i_][0][64:128, :], 0.0), [], ["kTh%d" % i_])
                        S.op("pool", lambda e: e.memset(kz[i_][1][0:64, :], 0.0), [], ["kTh%d" % i_])
                    load_head(0)
                    LOOK = 3
                    for i_ in range(LOOK):
                        qk(i_)
                    deferred = []
                    for i in range(nit):
                        hl, qb, c, kt = items[i]
                        hs = hl % 2
                        hg = hh * 4 + hl
                        if kt == 0 and c == 1:
                            ds_ = qb % 2
                            S.dma("sp", "sdg%d" % ds_, sdg[ds_][:],
                                  sdg_s[qb * 512:(qb + 1) * 512, hg * 128:(hg + 1) * 128].rearrange("(t p) c -> p t c", p=128),
                                  writes=["sdg%d" % ds_])
                        if kt == 0 and c == 0 and qb == 0 and hl + 1 < 4:
                            load_head(hl + 1)
                        bsx = i % 4
                        ps_ = i % 3
                        S.op("act", lambda e: e.activation(pT[ps_][:], PB[bsx][:], AF.Exp, scale=0.125),
                             [pbk[bsx]], ["pT%d" % ps_])
                        for qt in range(4):
                            mm(PB[ACC[qt]][:, 0:129], pT[ps_][:, qt * 128:(qt + 1) * 128], vh[hs][:, kt, 0:129],
                               kt == 0, kt == NT - 1, ["pT%d" % ps_, "vh%d" % hs], [pbk[ACC[qt]]])
                        if i + LOOK < nit:
                            qk(i + LOOK)
                        while deferred and deferred[0][0] <= i:
                            deferred.pop(0)[1]()
                        if kt == NT - 1:
                            evac_acc(hl, qb, c)
                            if c == 1:
                                deferred.append((i + 3, lambda hl=hl, qb=qb: post(hl, qb)))
                    while deferred:
                        deferred.pop(0)[1]()
                    S.flush()

            with ExitStack() as ls:
                wbr = sb("wbr", [128, 32, D], BF16, ls)
                wgl = sb("wgl", [128, 8, 3072], BF16, ls)
                gbb = sb("gbb", [1, 3072], BF16, ls)
                ssa = sb("ssa", [128, 2, NT], F32, ls)
                xTb = [sb("xTb%d" % i, [128, 8, 128], BF16, ls) for i in range(2)]
                ycat = [sb("ycat%d" % i, [128, 4096], BF16, ls) for i in range(2)]
                ycT = sb("ycT", [128, 32, 128], BF16, ls)
                gsb = [sb("gsb%d" % i, [128, 512], F32, ls) for i in range(2)]
                U = sb("U", [128, D], F32, ls)
                V = sb("V", [128, D], F32, ls)
                tmpv = sb("tmpv", [128, 512], F32, ls)
                rstd = sb("rstd", [128, 1], F32, ls)
                mrg = sb("mrg", [128, D], BF16, ls)
                mTs = [sb("mTs%d" % i, [128, 8, 128], BF16, ls) for i in range(2)]
                wv = w_in[L].rearrange("(k p) c -> p k c", p=128)
                for j in range(3):
                    S.dma("pool", "w0", wgl[:, :, j * 1024:(j + 1) * 1024], wv[:, :, O_GL + j * 1024:O_GL + (j + 1) * 1024], writes=["wgl"])
                wbv = w_br[L].rearrange("(k p) c -> p k c", p=128)
                for j in range(4):
                    S.dma("pool", "w1", wbr[:, j * 8:(j + 1) * 8, :], wbv[:, j * 8:(j + 1) * 8, :], writes=["wbr"])
                S.dma("pool", "a0", gbb[:], rowp[L, 0:1, R_GB:R_GB + 3072], writes=["gbb"])
                S.dma("sp", "a1", ssa[:], ss_s.rearrange("h p t -> p h t"), writes=["ssa"])
                pctr[0] = 0
                for b in range(NB):
                    for tt in range(4):
                        t = b * 4 + tt
                        s = t % 2
                        xk = "xTb%d" % s
                        ys = t % 2
                        yk = "ycat%d" % ys
                        rows = slice(t * 128, (t + 1) * 128)
                        S.dma("sp", xk, xTb[s][:], xT_s[:, :, rows], writes=[xk])
                        S.dma("sp", yk, ycat[ys][:, 0:2048], yssd_s[rows, :], writes=[yk])
                        S.dma("sp", yk, ycat[ys][:, 2048:3072], ydiff_s[rows, :], writes=[yk])
                        S.dma("sp", yk, ycat[ys][:, 3072:4096], ycross_s[rows, :], writes=[yk])
                        S.op("dve", lambda e, t=t: e.tensor_tensor(out=rstd[:], in0=ssa[:, 0, t:t + 1], in1=ssa[:, 1, t:t + 1], op=ALU.add), ["ssa"], ["rstd"])
                        S.op("act", lambda e: e.activation(rstd[:], rstd[:], AF.Ln, scale=1.0 / 2048.0, bias=EPS), ["rstd"], ["rstd"])
                        S.op("act", lambda e: e.activation(rstd[:], rstd[:], AF.Exp, scale=-0.5), ["rstd"], ["rstd"])
                        for q4 in range(4):
                            bi = nextbank()
                            pv = pbf(bi).rearrange("p (a b) -> p a b", a=8)
                            for k in range(8):
                                kk = q4 * 8 + k
                                tr(pv[:, k, :], ycat[ys][:, kk * 128:(kk + 1) * 128], ident_b[:], [yk], [pbk[bi]])
                            evac(ycT[:, q4 * 8:(q4 + 1) * 8, :], pv, [pbk[bi]], ["ycT%d" % q4])
                        kranges = [(0, 16), (16, 24), (24, 32)]
                        for j in range(3):
                            for cb in range(2):
                                gs_ = (j * 2 + cb) % 2
                                bgt = nextbank()
                                gc0 = j * 1024 + cb * 512
                                for k in range(8):
                                    mm(PB[bgt][:], xTb[s][:, k, :], wgl[:, k, gc0:gc0 + 512], k == 0, False,
                                       [xk, "wgl"], [pbk[bgt]])
                                mm(PB[bgt][:], ones_b[0:1, :], gbb[0:1, gc0:gc0 + 512], False, True, ["gbb"], [pbk[bgt]])
                                S.op("act", lambda e, bgt=bgt, gs_=gs_: e.activation(gsb[gs_][:], PB[bgt][:], AF.Sigmoid), [pbk[bgt]], ["gsb%d" % gs_])
                                bp = nextbank()
                                k0, k1 = kranges[j]
                                for kk in range(k0, k1):
                                    mm(PB[bp][:], ycT[:, kk, :], wbr[:, kk, cb * 512:(cb + 1) * 512], kk == k0, kk == k1 - 1,
                                       ["ycT%d" % (kk // 8), "wbr"], [pbk[bp]])
                                cs = slice(cb * 512, (cb + 1) * 512)
                                if j == 0:
                                    S.op("dve", lambda e, bp=bp, gs_=gs_, cs=cs: e.tensor_tensor(out=U[:, cs], in0=PB[bp][:], in1=gsb[gs_][:], op=ALU.mult),
                                         [pbk[bp], "gsb%d" % gs_], ["U%d" % cb])
                                elif j == 1:
                                    S.op("dve", lambda e, bp=bp, gs_=gs_, cs=cs: e.tensor_tensor(out=V[:, cs], in0=PB[bp][:], in1=gsb[gs_][:], op=ALU.mult),
                                         [pbk[bp], "gsb%d" % gs_], ["V%d" % cb])
                                else:
                                    S.op("dve", lambda e, bp=bp, gs_=gs_: e.tensor_tensor(out=tmpv[:], in0=PB[bp][:], in1=gsb[gs_][:], op=ALU.mult),
                                         [pbk[bp], "gsb%d" % gs_], ["tmpv"])
                                    S.op("pool", lambda e, cs=cs: e.tensor_tensor(out=V[:, cs], in0=V[:, cs], in1=tmpv[:], op=ALU.add),
                                         ["V%d" % cb, "tmpv"], ["V%d" % cb])
                        S.op("dve", lambda e: e.scalar_tensor_tensor(out=mrg[:], in0=U[:], scalar=rstd[:], in1=V[:], op0=ALU.mult, op1=ALU.add),
                             ["U0", "U1", "V0", "V1", "rstd"], ["mrg"])
                        bi = nextbank()
                        pv = pbf(bi).rearrange("p (a b) -> p a b", a=8)
                        for k in range(8):
                            tr(pv[:, k, :], mrg[:, k * 128:(k + 1) * 128], ident_b[:], ["mrg"], [pbk[bi]])
                        evac(mTs[s][:], pv, [pbk[bi]], ["mTs%d" % s])
                        S.dma("sp", "mTs%d" % s, mT_s[:, :, rows], mTs[s][:], reads=["mTs%d" % s])
                S.flush()

            with ExitStack() as ls:
                wo = sb("wo", [128, 8, D], BF16, ls)
                lng = sb("lng", [128, D], F32, ls)
                lnb = sb("lnb", [128, D], F32, ls)
                mTb = [sb("mTb%d" % i, [128, 8, 512], BF16, ls) for i in range(2)]
                xr = [sb("xr%d" % i, [128, D], F32, ls) for i in range(2)]
                r = sb("r", [128, D], F32, ls)
                st6 = sb("st6", [128, 2, 6], F32, ls)
                mvv = sb("mvv", [128, 2], F32, ls)
                xn = [sb("xn%d" % i, [128, D], F32, ls) for i in range(2)]
                S.dma("pool", "w0", wo[:], w_out[L].rearrange("(k p) c -> p k c", p=128), writes=["wo"])
                S.dma("sp", "a0", lng[:], rowp[L, :, R_LNG:R_LNG + D], writes=["lng"])
                S.dma("sp", "a1", lnb[:], rowp[L, :, R_LNB:R_LNB + D], writes=["lnb"])
                for b in range(NB):
                    s = b % 2
                    S.dma("sp", "mTb%d" % s, mTb[s][:], mT_s[:, :, b * 512:(b + 1) * 512], writes=["mTb%d" % s])
                    for tt in range(4):
                        t = b * 4 + tt
                        xs_ = t % 2
                        rows = slice(t * 128, (t + 1) * 128)
                        S.dma("sp", "xr%d" % xs_, xr[xs_][:], xsrc[rows, :], writes=["xr%d" % xs_])
                        for cb in range(2):
                            bi = nextbank()
                            for k in range(8):
                                mm(PB[bi][:], mTb[s][:, k, tt * 128:(tt + 1) * 128], wo[:, k, cb * 512:(cb + 1) * 512], k == 0, k == 7,
                                   ["mTb%d" % s, "wo"], [pbk[bi]])
                            S.op("dve", lambda e, bi=bi, cb=cb, xs_=xs_: e.scalar_tensor_tensor(
                                out=r[:, cb * 512:(cb + 1) * 512], in0=xr[xs_][:, cb * 512:(cb + 1) * 512], scalar=ALPHA, in1=PB[bi][:],
                                op0=ALU.mult, op1=ALU.add), ["xr%d" % xs_, pbk[bi]], ["r%d" % cb])
                            S.op("dve", lambda e, cb=cb: e.bn_stats(st6[:, cb, :], r[:, cb * 512:(cb + 1) * 512]), ["r%d" % cb], ["st6_%d" % cb])
                        S.op("dve", lambda e: e.bn_aggr(mvv[:], st6[:]), ["st6_0", "st6_1"], ["mvv"])
                        S.op("act", lambda e: e.activation(mvv[:, 1:2], mvv[:, 1:2], AF.Ln, bias=EPS), ["mvv"], ["mvv"])
                        S.op("act", lambda e: e.activation(mvv[:, 1:2], mvv[:, 1:2], AF.Exp, scale=-0.5), ["mvv"], ["mvv"])
                        S.op("dve", lambda e, xs_=xs_: e.tensor_scalar(xn[xs_][:], r[:], mvv[:, 0:1], mvv[:, 1:2], op0=ALU.subtract, op1=ALU.mult),
                             ["r0", "r1", "mvv"], ["xn%d" % xs_])
                        S.op("pool", lambda e, xs_=xs_: e.tensor_tensor(out=xn[xs_][:], in0=xn[xs_][:], in1=lng[:], op=ALU.mult),
                             ["xn%d" % xs_, "lng"], ["xn%d" % xs_])
                        S.op("dve", lambda e, xs_=xs_: e.tensor_tensor(out=xn[xs_][:], in0=xn[xs_][:], in1=lnb[:], op=ALU.add),
                             ["xn%d" % xs_, "lnb"], ["xn%d" % xs_])
                        S.dma("sp", "xn%d" % xs_, xdst[rows, :], xn[xs_][:], reads=["xn%d" % xs_])
                S.flush()
        print("program built: instr =", S.ninstr, flush=True)
    return nc


def make_consts():
    c = np.zeros((128, C_END), np.float32)
    s = np.arange(128)[:, None]
    t = np.arange(128)[None, :]
    c[:, C_ID:C_ID + 128] = (s == t)
    c[:, C_TL:C_TL + 128] = (s <= t)
    c[:, C_TU:C_TU + 128] = (s >= t)
    c[:, C_MF:C_MF + 128] = np.where(s <= t, 0.0, NEG)
    c[:, C_MB:C_MB + 128] = np.where(s >= t, 0.0, NEG)
    inv = (1.0 / (np.float32(10000.0) ** (np.arange(0, 64, 2, dtype=np.float32) / np.float32(64)))).astype(np.float32)
    c[:, C_INV:C_INV + 32] = inv[None, :]
    c[:, C_ONE:C_ONE + 128] = 1.0
    return c


def prep_inputs(inp, SEQ):
    f = lambda a: np.ascontiguousarray(np.asarray(a))
    B = inp["x"].shape[0]
    NT = SEQ // 128
    rowp = np.concatenate([
        f(inp["dt_bias"]).reshape(DEPTH, 64), f(inp["a_log"]).reshape(DEPTH, 64), f(inp["d_skip"]).reshape(DEPTH, 32),
        f(inp["ssd_norm_w"]), f(inp["diff_norm_w"]), f(inp["ln_g"]), f(inp["ln_b"]),
        f(inp["diff_lam"]).reshape(DEPTH, 256), f(inp["gate_b"]).reshape(DEPTH, 3072)], axis=1).astype(np.float32)
    rowp = np.ascontiguousarray(np.broadcast_to(rowp[:, None, :], (DEPTH, 128, R_END)))
    cw = f(inp["conv_w"])
    cb = f(inp["conv_b"])
    colp = np.concatenate([cw, cb[:, None, :]], axis=1)
    colp = np.ascontiguousarray(colp.reshape(DEPTH, 6, 24, 128).transpose(0, 3, 2, 1)).astype(np.float32)
    w_br = np.ascontiguousarray(np.concatenate([f(inp["w_br_ssd"]), f(inp["w_br_diff"]), f(inp["w_br_cross"])], axis=1))
    consts = make_consts()
    maps = []
    for b in range(B):
        pos = f(inp["positions"])[b].astype(np.int32).reshape(NT, 128).T
        maps.append({
            "x": f(inp["x"][b]), "mem": f(inp["mem"][b]), "pos": np.ascontiguousarray(pos),
            "w_in": f(inp["w_in"]), "w_kv": f(inp["w_mem_kv"]), "w_br": w_br, "w_out": f(inp["w_out"]),
            "rowp": rowp, "colp": colp, "consts": consts})
    return maps


_CACHE = {}


def kernel(**inputs):
    x = np.asarray(inputs["x"])
    B, SEQ, _ = x.shape
    if SEQ not in _CACHE:
        _CACHE[SEQ] = build_program(SEQ)
    nc = _CACHE[SEQ]
    maps = prep_inputs(inputs, SEQ)
    res = run_bass_kernel_spmd(nc, maps, core_ids=list(range(B)))
    out = np.stack([np.asarray(res.results[b]["out"]) for b in range(B)], axis=0)
    return out.astype(np.float32)
```

```python
import math
from contextlib import ExitStack
import numpy as np
import concourse.bass as bass
import concourse.mybir as mybir
from concourse.bass_utils import run_bass_kernel_spmd

F32 = mybir.dt.float32
BF16 = mybir.dt.bfloat16
I32 = mybir.dt.int32
ALU = mybir.AluOpType
AF = mybir.ActivationFunctionType

ENGS = ["pe", "act", "dve", "pool", "sp"]

D = 1024
DEPTH = 2
IN_COLS = 14400
EPS = 1e-5
ALPHA = (2 * DEPTH) ** 0.25
NEG = -30000.0

O_Z, O_XBC, O_DT, O_DQ, O_DK, O_DV, O_DG, O_CQ, O_CG, O_GL = (
    0, 2048, 5120, 5184, 6208, 7232, 8256, 9280, 10304, 11328)
R_DTB, R_ALOG, R_DSK, R_SNW, R_DNW, R_LNG, R_LNB, R_LAM, R_GB, R_END = (
    0, 64, 128, 160, 2208, 2336, 3360, 4384, 4640, 7712)
C_ID, C_TL, C_TU, C_MF, C_MB, C_INV, C_ONE, C_END = 0, 128, 256, 384, 512, 640, 672, 800


class _Rec:
    def __init__(self):
        self.call = None

    def __getattr__(self, name):
        def f(*args, **kw):
            self.call = (name, args, kw)
            return self
        return f


class Sched:
    def __init__(self, nc, stack):
        self.nc = nc
        self.stack = stack
        self.esem = {e: stack.enter_context(nc.semaphore("s_" + e)) for e in ENGS}
        self.count = {e: 0 for e in ENGS}
        self.dsem = {}
        self.waited = {e: {} for e in ENGS}
        self.last_w = {}
        self.readers = {}
        self.ops = {e: [] for e in ENGS}
        self.ninstr = 0

    def _sem(self, key):
        if key in self.esem:
            return self.esem[key]
        return self.dsem[key][0]

    def _deps(self, reads, writes):
        deps = {}

        def add(k, v):
            if deps.get(k, 0) < v:
                deps[k] = v
        for k in reads:
            if k in self.last_w:
                add(*self.last_w[k])
        for k in writes:
            if k in self.last_w:
                add(*self.last_w[k])
            for kk, vv in self.readers.get(k, {}).items():
                add(kk, vv)
        return deps

    def _emit_waits(self, e, deps, skip_self=False):
        for k, v in deps.items():
            if skip_self and k == e:
                continue
            if self.waited[e].get(k, 0) >= v:
                continue
            self.waited[e][k] = v
            sem = self._sem(k)
            self.ops[e].append(lambda eng, sem=sem, v=v: eng.wait_ge(sem, v))

    def _record(self, key, val, reads, writes):
        for k in writes:
            self.last_w[k] = (key, val)
            self.readers[k] = {}
        for k in reads:
            r = self.readers.setdefault(k, {})
            if r.get(key, 0) < val:
                r[key] = val

    def op(self, e, fn, reads=(), writes=()):
        rec = _Rec()
        fn(rec)
        name, args, kw = rec.call
        deps = self._deps(reads, writes)
        self._emit_waits(e, deps, skip_self=(e == "pe"))
        self.count[e] += 1
        val = self.count[e]
        sem = self.esem[e]
        self.ops[e].append(lambda eng, name=name, args=args, kw=kw, sem=sem:
                           getattr(eng, name)(*args, **kw).then_inc(sem, 1))
        self._record(e, val, reads, writes)
        self.ninstr += 1

    def dma(self, q, slot, out, in_, reads=(), writes=(), **kw):
        slot = q + "_" + slot
        if slot not in self.dsem:
            h = self.stack.enter_context(self.nc.semaphore("d_" + slot))
            self.dsem[slot] = [h, 0]
        deps = self._deps(reads, writes)
        self._emit_waits(q, deps)
        self.dsem[slot][1] += 16
        val = self.dsem[slot][1]
        sem = self.dsem[slot][0]
        self.ops[q].append(
            lambda eng, out=out, in_=in_, sem=sem, kw=kw:
            eng.dma_start(out=out, in_=in_, **kw).then_inc(sem, 16))
        self._record(slot, val, reads, writes)
        self.ninstr += 1

    def flush(self):
        alld = {e: self.count[e] for e in ENGS if self.count[e] > 0 and e != "sp"}
        for k, (h, c) in self.dsem.items():
            if c > 0:
                alld[k] = c
        self._emit_waits("sp", alld)
        ops = self.ops
        self.ops = {e: [] for e in ENGS}
        with self.nc.Block() as block:
            @block.tensor
            def _(eng):
                for f in ops["pe"]:
                    f(eng)

            @block.scalar
            def _(eng):
                for f in ops["act"]:
                    f(eng)

            @block.vector
            def _(eng):
                for f in ops["dve"]:
                    f(eng)

            @block.gpsimd
            def _(eng):
                for f in ops["pool"]:
                    f(eng)

            @block.sync
            def _(eng):
                for f in ops["sp"]:
                    f(eng)
        for e in ENGS:
            for k, v in alld.items():
                if self.waited[e].get(k, 0) < v:
                    self.waited[e][k] = v
        self.last_w = {}
        self.readers = {}


def bc(ap, shape):
    return ap.broadcast_to(shape)


def build_program(SEQ, dbg=()):
    NT = SEQ // 128
    NB = SEQ // 512
    nc = bass.Bass("TRN2", target_bir_lowering=False)

    def din(name, shape, dt=F32):
        return nc.dram_tensor(name, shape, dt, kind="ExternalInput").ap()

    x_in = din("x", [SEQ, D])
    mem_in = din("mem", [256, D])
    pos_in = din("pos", [128, NT], I32)
    w_in = din("w_in", [DEPTH, D, IN_COLS])
    w_kv = din("w_kv", [DEPTH, D, 2048])
    w_br = din("w_br", [DEPTH, 4096, D])
    w_out = din("w_out", [DEPTH, D, D])
    rowp = din("rowp", [DEPTH, 128, R_END])
    colp = din("colp", [DEPTH, 128, 24, 6])
    consts = din("consts", [128, C_END])
    out_ap = nc.dram_tensor("out", [SEQ, D], F32, kind="ExternalOutput").ap()

    def scr(name, shape, dt):
        kind = "ExternalOutput" if name in dbg else "Internal"
        return nc.dram_tensor(name, shape, dt, kind=kind).ap()

    xT_s = scr("xT_s", [128, 8, SEQ], BF16)
    raw_s = scr("raw_s", [2, 12, 128, SEQ + 4], F32)
    siluz_s = scr("siluz_s", [SEQ, 2048], BF16)
    dt_s = scr("dt_s", [SEQ, 2, 32], F32)
    qT_s = scr("qT_s", [8, 128, SEQ], BF16)
    kT_s = scr("kT_s", [8, 128, SEQ], BF16)
    v_s = scr("v_s", [8, SEQ, 130], BF16)
    sdg_s = scr("sdg_s", [SEQ, 1024], BF16)
    ycross_s = scr("ycross_s", [SEQ, 1024], BF16)
    xs_s = scr("xs_s", [2, SEQ, 1024], BF16)
    btm_s = scr("btm_s", [2, SEQ, 256], BF16)
    bt_s = scr("bt_s", [2, 2, 128, SEQ], BF16)
    ct_s = scr("ct_s", [2, 2, 128, SEQ], BF16)
    ypart_s = scr("ypart_s", [2, SEQ, 1024], F32)
    yssd_s = scr("yssd_s", [SEQ, 2048], BF16)
    ss_s = scr("ss_s", [2, 128, NT], F32)
    ydiff_s = scr("ydiff_s", [SEQ, 1024], BF16)
    mT_s = scr("mT_s", [128, 8, SEQ], BF16)
    x1_s = scr("x1_s", [SEQ, D], F32)

    with ExitStack() as st:
        S = Sched(nc, st)

        uniq = [0]

        def sb(name, shape, dt, stack=st):
            uniq[0] += 1
            return stack.enter_context(nc.sbuf_tensor("%s_%d" % (name, uniq[0]), shape, dt))

        PB = [st.enter_context(nc.psum_tensor("pb%d" % i, [128, 512], F32)) for i in range(8)]
        pbk = ["pb%d" % i for i in range(8)]
        pctr = [0]

        nbmod = [8]

        def nextbank():
            i = pctr[0] % nbmod[0]
            pctr[0] += 1
            return i

        def pbf(i):
            return PB[i][:].bitcast(BF16)

        cpy_ctr = [0]

        def evac(out, in_, reads, writes, eng=None):
            if eng is None:
                eng = "act" if cpy_ctr[0] % 2 == 0 else "dve"
                cpy_ctr[0] += 1
            if eng == "act":
                S.op("act", lambda e: e.copy(out, in_), reads, writes)
            else:
                S.op(eng, lambda e: e.tensor_copy(out, in_), reads, writes)

        def mm(out, lhsT, rhs, start, stop, reads, writes):
            S.op("pe", lambda e: e.matmul(out, lhsT, rhs, start=start, stop=stop), reads, writes)

        def tr(out, in_, ident, reads, writes):
            S.op("pe", lambda e: e.transpose(out, in_, ident), reads, writes)

        ident_b = sb("ident_b", [128, 128], BF16)
        ones_b = sb("ones_b", [128, 128], BF16)
        ones_f = sb("ones_f", [128, 128], F32)
        tri_f = sb("tri_f", [128, 2, 128], F32)
        tri_b = sb("tri_b", [128, 2, 128], BF16)
        mb_b = sb("mb_b", [128, 2, 4, 128], BF16)
        cos_t = sb("cos_t", [128, NT, 32], F32)
        sin_t = sb("sin_t", [128, NT, 32], F32)
        mkT = sb("mkT", [128, 8, 256], BF16)
        mv_aug = sb("mv_aug", [128, 2, 4, 258], BF16)
        neg_lam = sb("neg_lam", [128, 1], F32)
        A_all = sb("A_all", [128, 64], F32)

        with ExitStack() as ls:
            cst = sb("cst", [128, C_END], F32, ls)
            posi = sb("posi", [128, NT], I32, ls)
            posf = sb("posf", [128, NT], F32, ls)
            ang = sb("ang", [128, NT, 32], F32, ls)
            kf = sb("kf", [128, NT, 32], F32, ls)
            ki = sb("ki", [128, NT, 32], I32, ls)
            a2 = sb("a2", [128, NT, 32], F32, ls)
            S.dma("sp", "pl0", cst[:], consts, writes=["cst"])
            S.dma("sp", "pl1", posi[:], pos_in, writes=["posi"])
            S.op("dve", lambda e: e.tensor_copy(ident_b[:], cst[:, C_ID:C_ID + 128]), ["cst"], ["ident_b"])
            S.op("dve", lambda e: e.tensor_copy(ones_b[:], cst[:, C_ONE:C_ONE + 128]), ["cst"], ["ones_b"])
            S.op("dve", lambda e: e.tensor_copy(ones_f[:], cst[:, C_ONE:C_ONE + 128]), ["cst"], ["ones_f"])
            S.op("dve", lambda e: e.tensor_copy(tri_f[:, 0, :], cst[:, C_TL:C_TL + 128]), ["cst"], ["tri0"])
            S.op("dve", lambda e: e.tensor_copy(tri_f[:, 1, :], cst[:, C_TU:C_TU + 128]), ["cst"], ["tri1"])
            S.op("dve", lambda e: e.tensor_copy(tri_b[:, 0, :], cst[:, C_TL:C_TL + 128]), ["cst"], ["trib0"])
            S.op("dve", lambda e: e.tensor_copy(tri_b[:, 1, :], cst[:, C_TU:C_TU + 128]), ["cst"], ["trib1"])
            for d_ in range(2):
                off = C_MF if d_ == 0 else C_MB
                S.op("dve", lambda e, d_=d_, off=off: e.tensor_copy(
                    mb_b[:, d_, :, :], bc(cst[:, off:off + 128].unsqueeze(1), [128, 4, 128])),
                    ["cst"], ["mb%d" % d_])
            S.op("dve", lambda e: e.tensor_copy(posf[:], posi[:]), ["posi"], ["posf"])
            S.op("dve", lambda e: e.tensor_tensor(
                out=ang[:], in0=bc(posf[:].unsqueeze(2), [128, NT, 32]),
                in1=bc(cst[:, C_INV:C_INV + 32].unsqueeze(1), [128, NT, 32]), op=ALU.mult),
                ["posf", "cst"], ["ang"])
            C1 = float(np.float32(2 * math.pi))
            C2 = float(2 * math.pi - C1)
            S.op("dve", lambda e: e.tensor_scalar(kf[:], ang[:], 1.0 / (2 * math.pi), None, op0=ALU.mult), ["ang"], ["kf"])
            S.op("dve", lambda e: e.tensor_copy(ki[:], kf[:]), ["kf"], ["ki"])
            S.op("dve", lambda e: e.tensor_copy(kf[:], ki[:]), ["ki"], ["kf"])
            S.op("dve", lambda e: e.scalar_tensor_tensor(out=ang[:], in0=kf[:], scalar=-C1, in1=ang[:], op0=ALU.mult, op1=ALU.add), ["kf", "ang"], ["ang"])
            S.op("dve", lambda e: e.scalar_tensor_tensor(out=ang[:], in0=kf[:], scalar=-C2, in1=ang[:], op0=ALU.mult, op1=ALU.add), ["kf", "ang"], ["ang"])

            def wrap_sin(dst, shift, key):
                S.op("dve", lambda e: e.tensor_scalar(a2[:], ang[:], shift, None, op0=ALU.add), ["ang", "a2"], ["a2"])
                S.op("dve", lambda e: e.tensor_scalar(kf[:], a2[:], math.pi, None, op0=ALU.is_gt), ["a2", "kf"], ["kf"])
                S.op("dve", lambda e: e.scalar_tensor_tensor(out=a2[:], in0=kf[:], scalar=-2 * math.pi, in1=a2[:], op0=ALU.mult, op1=ALU.add), ["kf", "a2"], ["a2"])
                S.op("dve", lambda e: e.tensor_scalar(kf[:], a2[:], -math.pi, None, op0=ALU.is_lt), ["a2", "kf"], ["kf"])
                S.op("dve", lambda e: e.scalar_tensor_tensor(out=a2[:], in0=kf[:], scalar=2 * math.pi, in1=a2[:], op0=ALU.mult, op1=ALU.add), ["kf", "a2"], ["a2"])
                S.op("act", lambda e: e.activation(dst[:], a2[:], AF.Sin), ["a2"], [key])
            wrap_sin(sin_t, 0.0, "sin_t")
            wrap_sin(cos_t, math.pi / 2, "cos_t")
            S.flush()

        for L in range(DEPTH):
            xsrc = x_in if L == 0 else x1_s
            xdst = x1_s if L == 0 else out_ap
            lam_init = 0.8 - 0.6 * math.exp(-0.3 * L)

            with ExitStack() as ls:
                lamt = sb("lamt", [128, 256], F32, ls)
                lp = sb("lp", [128, 2, 64], F32, ls)
                ls2 = sb("ls2", [128, 2], F32, ls)
                alog = sb("alog", [128, 64], F32, ls)
                memb = sb("memb", [128, 2, D], BF16, ls)
                memT = sb("memT", [128, 8, 256], BF16, ls)
                wkv = sb("wkv", [128, 8, 2048], BF16, ls)
                S.dma("sp", "a0", lamt[:], rowp[L, :, R_LAM:R_LAM + 256], writes=["lamt"])
                S.dma("sp", "a1", alog[:], rowp[L, :, R_ALOG:R_ALOG + 64], writes=["alog"])
                S.dma("pool", "a2", memb[:], mem_in.rearrange("(c p) d -> p c d", p=128), writes=["memb"])
                S.dma("pool", "a3", wkv[:], w_kv[L].rearrange("(k p) c -> p k c", p=128), writes=["wkv"])
                lv = lamt[:].rearrange("p (a b) -> p a b", a=4)
                S.op("dve", lambda e: e.tensor_tensor(out=lp[:, 0, :], in0=lv[:, 0, :], in1=lv[:, 1, :], op=ALU.mult), ["lamt"], ["lp0"])
                S.op("dve", lambda e: e.tensor_tensor(out=lp[:, 1, :], in0=lv[:, 2, :], in1=lv[:, 3, :], op=ALU.mult), ["lamt"], ["lp1"])
                S.op("dve", lambda e: e.reduce_sum(ls2[:], lp[:], axis=mybir.AxisListType.X), ["lp0", "lp1"], ["ls2"])
                S.op("act", lambda e: e.activation(ls2[:], ls2[:], AF.Exp), ["ls2"], ["ls2"])
                S.op("dve", lambda e: e.tensor_tensor(out=neg_lam[:], in0=ls2[:, 1:2], in1=ls2[:, 0:1], op=ALU.subtract), ["ls2"], ["neg_lam"])
                S.op("dve", lambda e: e.tensor_scalar(neg_lam[:], neg_lam[:], -lam_init, None, op0=ALU.add), ["neg_lam"], ["neg_lam"])
                S.op("act", lambda e: e.activation(A_all[:], alog[:], AF.Exp), ["alog"], ["A_all"])
                S.op("dve", lambda e: e.tensor_scalar(A_all[:], A_all[:], -1.0, None, op0=ALU.mult), ["A_all"], ["A_all"])
                for mc in range(2):
                    bi = nextbank()
                    pv = pbf(bi).rearrange("p (a b) -> p a b", a=8)
                    for k in range(8):
                        tr(pv[:, k, :], memb[:, mc, k * 128:(k + 1) * 128], ident_b[:], ["memb", "ident_b"], [pbk[bi]])
                    evac(memT[:, :, mc * 128:(mc + 1) * 128], pv, [pbk[bi]], ["memT"])
                for j in range(8):
                    bi = nextbank()
                    for k in range(8):
                        mm(PB[bi][:, 0:256], wkv[:, k, j * 128:(j + 1) * 128], memT[:, k, :], k == 0, k == 7,
                           ["wkv", "memT"], [pbk[bi]])
                    evac(mkT[:, j, :], PB[bi][:, 0:256], [pbk[bi]], ["mkT"])
                S.op("pool", lambda e: e.memset(mv_aug[:, :, :, 256:257], 1.0), [], ["mv_aug"])
                S.op("pool", lambda e: e.memset(mv_aug[:, :, :, 257:258], 0.0), [], ["mv_aug"])
                for mc in range(2):
                    for cb in range(2):
                        bi = nextbank()
                        for k in range(8):
                            mm(PB[bi][:], memT[:, k, mc * 128:(mc + 1) * 128], wkv[:, k, 1024 + cb * 512:1024 + (cb + 1) * 512],
                               k == 0, k == 7, ["wkv", "memT"], [pbk[bi]])
                        evac(mv_aug[:, mc, cb * 2:(cb + 1) * 2, 0:256], PB[bi][:].rearrange("p (a b) -> p a b", a=2),
                             [pbk[bi]], ["mv_aug"])
                S.flush()

            with ExitStack() as ls:
                xb = [sb("xb%d" % i, [128, 4, D], BF16, ls) for i in range(2)]
                xTb = [sb("xTb%d" % i, [128, 8, 512], BF16, ls) for i in range(2)]
                for b in range(NB):
                    s = b % 2
                    S.dma("pool", "xb%d" % s, xb[s][:], xsrc[b * 512:(b + 1) * 512, :].rearrange("(t p) d -> p t d", p=128),
                          writes=["xb%d" % s])
                    for tt in range(4):
                        bi = nextbank()
                        pv = pbf(bi).rearrange("p (a b) -> p a b", a=8)
                        for k in range(8):
                            tr(pv[:, k, :], xb[s][:, tt, k * 128:(k + 1) * 128], ident_b[:], ["xb%d" % s], [pbk[bi]])
                        evac(xTb[s][:, :, tt * 128:(tt + 1) * 128], pv, [pbk[bi]], ["xTb%d" % s])
                    S.dma("sp", "xTb%d" % s, xT_s[:, :, b * 512:(b + 1) * 512], xTb[s][:], reads=["xTb%d" % s])
                S.flush()

            for hh in range(2):
                with ExitStack() as ls:
                    wfm = sb("wfm", [128, 8, 1536], BF16, ls)
                    xTb = [sb("xTb%d" % i, [128, 8, 512], BF16, ls) for i in range(2)]
                    stg = [sb("stg%d" % i, [128, 12, 512], F32, ls) for i in range(2)]
                    zt = sb("zt", [128, 12, 2], F32, ls)
                    wv = w_in[L].rearrange("(k p) c -> p k c", p=128)
                    S.dma("pool", "w0", wfm[:, :, 0:1024], wv[:, :, O_XBC + hh * 1024:O_XBC + (hh + 1) * 1024], writes=["wfm"])
                    S.dma("pool", "w0", wfm[:, :, 1024:1280], wv[:, :, O_XBC + 2048 + hh * 256:O_XBC + 2048 + (hh + 1) * 256], writes=["wfm"])
                    S.dma("pool", "w0", wfm[:, :, 1280:1536], wv[:, :, O_XBC + 2560 + hh * 256:O_XBC + 2560 + (hh + 1) * 256], writes=["wfm"])
                    S.op("pool", lambda e: e.memset(zt[:], 0.0), [], ["zt"])
                    rv = raw_s[hh].rearrange("c p s -> p c s")
                    S.dma("sp", "zt", rv[:, :, 0:2], zt[:], reads=["zt"])
                    S.dma("sp", "zt", rv[:, :, SEQ + 2:SEQ + 4], zt[:], reads=["zt"])
                    for b in range(NB):
                        s = b % 2
                        S.dma("sp", "xTb%d" % s, xTb[s][:], xT_s[:, :, b * 512:(b + 1) * 512], writes=["xTb%d" % s])
                        for c in range(12):
                            bi = nextbank()
                            for k in range(8):
                                mm(PB[bi][:], wfm[:, k, c * 128:(c + 1) * 128], xTb[s][:, k, :], k == 0, k == 7,
                                   ["wfm", "xTb%d" % s], [pbk[bi]])
                            evac(stg[s][:, c, :], PB[bi][:], [pbk[bi]], ["stg%d" % s])
                        S.dma("sp", "stg%d" % s, rv[:, :, 2 + b * 512:2 + (b + 1) * 512], stg[s][:], reads=["stg%d" % s])
                    S.flush()

                with ExitStack() as ls:
                    wtm = sb("wtm", [128, 8, 4128], BF16, ls)
                    dtb = sb("dtb", [128, 32], F32, ls)
                    xTb = [sb("xTb%d" % i, [128, 8, 512], BF16, ls) for i in range(2)]
                    zs = [sb("zs%d" % i, [128, 4, 1024], BF16, ls) for i in range(2)]
                    dts = [sb("dts%d" % i, [128, 4, 32], F32, ls) for i in range(2)]
                    qTs = [sb("qTs%d" % i, [128, 4, 512], BF16, ls) for i in range(2)]
                    kTs = [sb("kTs%d" % i, [128, 4, 512], BF16, ls) for i in range(2)]
                    vs = [sb("vs%d" % i, [128, 4, 4, 130], BF16, ls) for i in range(2)]
                    dgs = [sb("dgs%d" % i, [128, 4, 512], BF16, ls) for i in range(2)]
                    ycs = [sb("ycs%d" % i, [128, 4, 512], BF16, ls) for i in range(2)]
                    dtx = sb("dtx", [128, 32], F32, ls)
                    ra = sb("ra", [128, 8, 32], F32, ls)
                    rb = sb("rb", [128, 8, 32], F32, ls)
                    qr = sb("qr", [128, 512], BF16, ls)
                    cqb = sb("cqb", [128, 512], BF16, ls)
                    cqT = sb("cqT", [128, 4, 128], BF16, ls)
                    pTc = sb("pTc", [128, 4, 128], BF16, ls)
                    cgs = sb("cgs", [128, 512], F32, ls)
                    rc = sb("rc", [128, 2], F32, ls)
                    wv = w_in[L].rearrange("(k p) c -> p k c", p=128)
                    segs = [(0, O_Z + hh * 1024, 1024), (1024, O_DT + hh * 16, 16), (1040, O_DT + 32 + hh * 16, 16),
                            (1056, O_DQ + hh * 512, 512), (1568, O_DK + hh * 512, 512), (2080, O_DV + hh * 512, 512),
                            (2592, O_DG + hh * 512, 512), (3104, O_CQ + hh * 512, 512), (3616, O_CG + hh * 512, 512)]
                    for (o, so, n) in segs:
                        S.dma("pool", "w0", wtm[:, :, o:o + n], wv[:, :, so:so + n], writes=["wtm"])
                    S.dma("sp", "a0", dtb[:, 0:16], rowp[L, :, R_DTB + hh * 16:R_DTB + hh * 16 + 16], writes=["dtb"])
                    S.dma("sp", "a0", dtb[:, 16:32], rowp[L, :, R_DTB + 32 + hh * 16:R_DTB + 32 + hh * 16 + 16], writes=["dtb"])
                    for i in range(2):
                        S.op("pool", lambda e, i=i: e.memset(vs[i][:, :, :, 128:129], 1.0), [], ["vs%d" % i])
                        S.op("pool", lambda e, i=i: e.memset(vs[i][:, :, :, 129:130], 0.0), [], ["vs%d" % i])

                    def grp(bi, width, t_lhs, col0, xk):
                        for k in range(8):
                            mm(PB[bi][:, 0:width], t_lhs(k), wtm[:, k, col0:col0 + width], k == 0, k == 7, ["wtm", xk], [pbk[bi]])

                    def rope(bi, dst, tkey, t):
                        qv = PB[bi][:].rearrange("p (a h j) -> p a h j", a=8, h=2)
                        t1, t2 = qv[:, :, 0, :], qv[:, :, 1, :]
                        cb_ = bc(cos_t[:, t, :].unsqueeze(1), [128, 8, 32])
                        sb_ = bc(sin_t[:, t, :].unsqueeze(1), [128, 8, 32])
                        dv = dst[:].rearrange("p (a h j) -> p a h j", a=8, h=2)
                        S.op("dve", lambda e: e.tensor_tensor(out=ra[:], in0=t1, in1=cb_, op=ALU.mult), [pbk[bi]], ["ra"])
                        S.op("dve", lambda e: e.tensor_tensor(out=rb[:], in0=t2, in1=sb_, op=ALU.mult), [pbk[bi]], ["rb"])
                        S.op("dve", lambda e: e.tensor_tensor(out=dv[:, :, 0, :], in0=ra[:], in1=rb[:], op=ALU.subtract), ["ra", "rb"], [tkey + "a"])
                        S.op("dve", lambda e: e.tensor_tensor(out=ra[:], in0=t1, in1=sb_, op=ALU.mult), [pbk[bi]], ["ra"])
                        S.op("dve", lambda e: e.tensor_tensor(out=rb[:], in0=t2, in1=cb_, op=ALU.mult), [pbk[bi]], ["rb"])
                        S.op("dve", lambda e: e.tensor_tensor(out=dv[:, :, 1, :], in0=ra[:], in1=rb[:], op=ALU.add), ["ra", "rb"], [tkey + "b"])

                    for b in range(NB):
                        s = b % 2
                        xk = "xTb%d" % s
                        S.dma("sp", xk, xTb[s][:], xT_s[:, :, b * 512:(b + 1) * 512], writes=[xk])
                        for tt in range(4):
                            t = b * 4 + tt
                            lh = lambda k, s=s, tt=tt: xTb[s][:, k, tt * 128:(tt + 1) * 128]
                            for cb in range(2):
                                bi = nextbank()
                                grp(bi, 512, lh, cb * 512, xk)
                                S.op("act", lambda e, bi=bi, cb=cb: e.activation(zs[s][:, tt, cb * 512:(cb + 1) * 512], PB[bi][:], AF.Silu),
                                     [pbk[bi]], ["zs%d" % s])
                            bi = nextbank()
                            grp(bi, 32, lh, 1024, xk)
                            S.op("dve", lambda e, bi=bi: e.tensor_tensor(out=dtx[:], in0=PB[bi][:, 0:32], in1=dtb[:], op=ALU.add),
                                 [pbk[bi], "dtb"], ["dtx"])
                            S.op("act", lambda e: e.activation(dtx[:], dtx[:], AF.Exp), ["dtx"], ["dtx"])
                            S.op("act", lambda e: e.activation(dts[s][:, tt, :], dtx[:], AF.Ln, bias=1.0), ["dtx"], ["dts%d" % s])
                            for (col0, dstT, nm) in ((1056, qTs, "qTs"), (1568, kTs, "kTs")):
                                bi = nextbank()
                                grp(bi, 512, lh, col0, xk)
                                rope(bi, qr, "qr", t)
                                b2 = nextbank()
                                pv = pbf(b2).rearrange("p (a b) -> p a b", a=8)
                                for h in range(4):
                                    tr(pv[:, h, :], qr[:, h * 128:(h + 1) * 128], ident_b[:], ["qra", "qrb"], [pbk[b2]])
                                evac(dstT[s][:, :, tt * 128:(tt + 1) * 128], pv[:, 0:4, :], [pbk[b2]], ["%s%d" % (nm, s)])
                            bi = nextbank()
                            grp(bi, 512, lh, 2080, xk)
                            evac(vs[s][:, tt, :, 0:128], PB[bi][:].rearrange("p (a b) -> p a b", a=4), [pbk[bi]], ["vs%d" % s])
                            bi = nextbank()
                            grp(bi, 512, lh, 2592, xk)
                            S.op("act", lambda e, bi=bi: e.activation(dgs[s][:, tt, :], PB[bi][:], AF.Silu), [pbk[bi]], ["dgs%d" % s])
                            bi = nextbank()
                            grp(bi, 512, lh, 3104, xk)
                            evac(cqb[:], PB[bi][:], [pbk[bi]], ["cqb"])
                            b2 = nextbank()
                            pv = pbf(b2).rearrange("p (a b) -> p a b", a=8)
                            for j in range(4):
                                tr(pv[:, j, :], cqb[:, j * 128:(j + 1) * 128], ident_b[:], ["cqb"], [pbk[b2]])
                            evac(cqT[:], pv[:, 0:4, :], [pbk[b2]], ["cqT"])
                            b3 = nextbank()
                            sv = PB[b3][:].rearrange("p (a b) -> p a b", a=4)
                            for hc in range(2):
                                for mc in range(2):
                                    for dc in range(2):
                                        mm(sv[:, hc * 2 + mc, :], mkT[:, (hh * 2 + hc) * 2 + dc, mc * 128:(mc + 1) * 128],
                                           cqT[:, hc * 2 + dc, :], dc == 0, dc == 1, ["mkT", "cqT"], [pbk[b3]])
                            S.op("act", lambda e, b3=b3: e.activation(pTc[:], PB[b3][:].rearrange("p (a b) -> p a b", a=4), AF.Exp, scale=1.0 / 16.0),
                                 [pbk[b3]], ["pTc"])
                            bi = nextbank()
                            grp(bi, 512, lh, 3616, xk)
                            S.op("act", lambda e, bi=bi: e.activation(cgs[:], PB[bi][:], AF.Silu), [pbk[bi]], ["cgs"])
                            for hc in range(2):
                                b4 = nextbank()
                                for mc in range(2):
                                    mm(PB[b4][:, 0:257], pTc[:, hc * 2 + mc, :], mv_aug[:, mc, hh * 2 + hc, 0:257], mc == 0, mc == 1,
                                       ["pTc", "mv_aug"], [pbk[b4]])
                                S.op("dve", lambda e, b4=b4, hc=hc: e.reciprocal(rc[:, hc:hc + 1], PB[b4][:, 256:257]), [pbk[b4]], ["rc%d" % hc])
                                S.op("dve", lambda e, b4=b4, hc=hc: e.scalar_tensor_tensor(
                                    out=ycs[s][:, tt, hc * 256:(hc + 1) * 256], in0=PB[b4][:, 0:256], scalar=rc[:, hc:hc + 1],
                                    in1=cgs[:, hc * 256:(hc + 1) * 256], op0=ALU.mult, op1=ALU.mult),
                                    [pbk[b4], "rc%d" % hc, "cgs"], ["ycs%d" % s])
                        rows = slice(b * 512, (b + 1) * 512)
                        S.dma("sp", "zs%d" % s, siluz_s[rows, hh * 1024:(hh + 1) * 1024].rearrange("(t p) c -> p t c", p=128), zs[s][:], reads=["zs%d" % s])
                        S.dma("sp", "dts%d" % s, dt_s[rows, hh, :].rearrange("(t p) c -> p t c", p=128), dts[s][:], reads=["dts%d" % s])
                        S.dma("sp", "qTs%d" % s, qT_s[hh * 4:(hh + 1) * 4, :, rows].rearrange("h p s -> p h s"), qTs[s][:], reads=["qTs%d" % s])
                        S.dma("sp", "kTs%d" % s, kT_s[hh * 4:(hh + 1) * 4, :, rows].rearrange("h p s -> p h s"), kTs[s][:], reads=["kTs%d" % s])
                        for h in range(4):
                            S.dma("sp", "vs%d" % s, v_s[hh * 4 + h, rows, :].rearrange("(t p) e -> p t e", p=128), vs[s][:, :, h, :], reads=["vs%d" % s])
                        S.dma("sp", "dgs%d" % s, sdg_s[rows, hh * 512:(hh + 1) * 512].rearrange("(t p) c -> p t c", p=128), dgs[s][:], reads=["dgs%d" % s])
                        S.dma("sp", "ycs%d" % s, ycross_s[rows, hh * 512:(hh + 1) * 512].rearrange("(t p) c -> p t c", p=128), ycs[s][:], reads=["ycs%d" % s])
                    S.flush()

                with ExitStack() as ls:
                    cp = sb("cp", [128, 12, 6], F32, ls)
                    raw = [sb("raw%d" % i, [128, 12, 516], F32, ls) for i in range(2)]
                    acc = [sb("acc%d" % i, [128, 512], F32, ls) for i in range(2)]
                    xc = [sb("xc%d" % i, [128, 12, 512], BF16, ls) for i in range(2)]
                    xst = [sb("xst%d" % i, [128, 4, 1024], BF16, ls) for i in range(2)]
                    bst = [sb("bst%d" % i, [128, 4, 256], BF16, ls) for i in range(2)]
                    S.dma("sp", "a0", cp[:, 0:8, :], colp[L, :, hh * 8:(hh + 1) * 8, :], writes=["cp"])
                    S.dma("sp", "a0", cp[:, 8:10, :], colp[L, :, 16 + hh * 2:16 + (hh + 1) * 2, :], writes=["cp"])
                    S.dma("sp", "a0", cp[:, 10:12, :], colp[L, :, 20 + hh * 2:20 + (hh + 1) * 2, :], writes=["cp"])
                    rv = raw_s[hh].rearrange("c p s -> p c s")
                    for b in range(NB):
                        s = b % 2
                        S.dma("sp", "raw%d" % s, raw[s][:], rv[:, :, b * 512:b * 512 + 516], writes=["raw%d" % s])
                        for c in range(12):
                            a_ = c % 2
                            ak = "acc%d" % a_
                            S.op("dve", lambda e, c=c, a_=a_: e.tensor_scalar(acc[a_][:], raw[s][:, c, 0:512], cp[:, c, 0:1], None, op0=ALU.mult),
                                 ["raw%d" % s, "cp"], [ak])
                            for j in range(1, 5):
                                S.op("dve", lambda e, c=c, a_=a_, j=j: e.scalar_tensor_tensor(
                                    out=acc[a_][:], in0=raw[s][:, c, j:j + 512], scalar=cp[:, c, j:j + 1], in1=acc[a_][:],
                                    op0=ALU.mult, op1=ALU.add), ["raw%d" % s, "cp", ak], [ak])
                            S.op("act", lambda e, c=c, a_=a_: e.activation(xc[s][:, c, :], acc[a_][:], AF.Silu, bias=cp[:, c, 5:6]),
                                 [ak, "cp"], ["xc%d_%d" % (s, c)])
                        for tt in range(4):
                            bi = nextbank()
                            pv = pbf(bi).rearrange("p (a b) -> p a b", a=8)
                            for c in range(8):
                                tr(pv[:, c, :], xc[s][:, c, tt * 128:(tt + 1) * 128], ident_b[:], ["xc%d_%d" % (s, c)], [pbk[bi]])
                            evac(xst[s][:, tt, :].rearrange("p (a b) -> p a b", a=8), pv, [pbk[bi]], ["xst%d" % s])
                            bi = nextbank()
                            pv = pbf(bi).rearrange("p (a b) -> p a b", a=8)
                            for c in range(2):
                                tr(pv[:, c, :], xc[s][:, 8 + c, tt * 128:(tt + 1) * 128], ident_b[:], ["xc%d_%d" % (s, 8 + c)], [pbk[bi]])
                            evac(bst[s][:, tt, :].rearrange("p (a b) -> p a b", a=2), pv[:, 0:2, :], [pbk[bi]], ["bst%d" % s])
                        rows = slice(b * 512, (b + 1) * 512)
                        S.dma("sp", "xst%d" % s, xs_s[hh, rows, :].rearrange("(t p) c -> p t c", p=128), xst[s][:], reads=["xst%d" % s])
                        S.dma("sp", "bst%d" % s, btm_s[hh, rows, :].rearrange("(t p) c -> p t c", p=128), bst[s][:], reads=["bst%d" % s])
                        S.dma("sp", "xcB%d" % s, bt_s[hh, :, :, rows].rearrange("g p s -> p g s"), xc[s][:, 8:10, :],
                              reads=["xc%d_8" % s, "xc%d_9" % s])
                        S.dma("sp", "xcC%d" % s, ct_s[hh, :, :, rows].rearrange("g p s -> p g s"), xc[s][:, 10:12, :],
                              reads=["xc%d_10" % s, "xc%d_11" % s])
                    S.flush()

                for dr in range(2):
                    with ExitStack() as ls:
                        state = sb("state", [128, 2, 512], F32, ls)
                        prevb = sb("prevb", [128, 2, 512], BF16, ls)
                        dsk = sb("dsk", [128, 16], F32, ls)
                        nw = sb("nw", [128, 1024], F32, ls)
                        sscol = sb("sscol", [128, NT], F32, ls)
                        xs = [sb("xs%d" % i, [128, 1024], BF16, ls) for i in range(3)]
                        btm = [sb("btm%d" % i, [128, 256], BF16, ls) for i in range(3)]
                        bt = [sb("bt%d" % i, [128, 2, 128], BF16, ls) for i in range(3)]
                        ct = [sb("ct%d" % i, [128, 2, 128], BF16, ls) for i in range(3)]
                        dtt = [sb("dtt%d" % i, [128, 16], F32, ls) for i in range(3)]
                        ypl = [sb("ypl%d" % i, [128, 1024], F32, ls) for i in range(3)]
                        zsl = [sb("zsl%d" % i, [128, 1024], BF16, ls) for i in range(3)]
                        yo = [sb("yo%d" % i, [128, 1024], F32 if dr == 0 else BF16, ls) for i in range(2)]
                        a_t = [sb("a_t%d" % i, [128, 16], F32, ls) for i in range(2)]
                        a_hi = [sb("a_hi%d" % i, [128, 16], BF16, ls) for i in range(2)]
                        a_lo = [sb("a_lo%d" % i, [128, 16], BF16, ls) for i in range(2)]
                        ut = [sb("ut%d" % i, [128, 32], F32, ls) for i in range(2)]
                        lndt = [sb("lndt%d" % i, [128, 16], F32, ls) for i in range(2)]
                        biasM = [sb("biasM%d" % i, [128, 16], F32, ls) for i in range(2)]
                        wst = [sb("wst%d" % i, [128, 16], F32, ls) for i in range(2)]
                        eu = [sb("eu%d" % i, [128, 16], F32, ls) for i in range(2)]
                        cd = [sb("cd%d" % i, [128, 16], F32, ls) for i in range(2)]
                        rUh = sb("rUh", [128, 16, 128], BF16, ls)
                        rUl = sb("rUl", [128, 16, 128], BF16, ls)
                        dec = [sb("dec%d" % i, [128, 8, 128], BF16, ls) for i in range(2)]
                        MT = [sb("MT%d" % i, [128, 8, 128], BF16, ls) for i in range(2)]
                        xsw = [sb("xsw%d" % i, [128, 512], BF16, ls) for i in range(2)]
                        dsb = [sb("dsb%d" % i, [128, 1024], F32, ls) for i in range(2)]
                        ssb = [sb("ssb%d" % i, [128, 1024], F32, ls) for i in range(2)]
                        t1 = sb("t1", [128, 512], F32, ls)
                        t2 = sb("t2", [128, 512], F32, ls)
                        t3 = [sb("t3_%d" % i, [128, 1024], F32, ls) for i in range(2)]
                        t4 = sb("t4", [128, 1024], F32, ls)
                        yacc = sb("yacc", [128, 1024], F32, ls)
                        yg = sb("yg", [128, 1024], F32, ls)
                        junk = sb("junk", [128, 1024], BF16, ls)
                        S.op("pool", lambda e: e.memset(state[:], 0.0), [], ["state0", "state1"])
                        S.op("pool", lambda e: e.memset(prevb[:], 0.0), [], ["prevb0", "prevb1"])
                        S.dma("sp", "a0", dsk[:], rowp[L, :, R_DSK + hh * 16:R_DSK + (hh + 1) * 16], writes=["dsk"])
                        S.dma("sp", "a1", nw[:], rowp[L, :, R_SNW + hh * 1024:R_SNW + (hh + 1) * 1024], writes=["nw"])
                        Acol = A_all[:, dr * 32 + hh * 16:dr * 32 + hh * 16 + 16]
                        order = list(range(NT)) if dr == 0 else list(range(NT - 1, -1, -1))

                        def chunk_loads(it):
                            c = order[it]
                            s = it % 3
                            rows = slice(c * 128, (c + 1) * 128)
                            S.dma("sp", "dtt%d" % s, dtt[s][:], dt_s[rows, hh, dr * 16:(dr + 1) * 16], writes=["dtt%d" % s])
                            S.dma("sp", "bt%d" % s, bt[s][:], bt_s[hh, :, :, rows].rearrange("g p s -> p g s"), writes=["bt%d" % s])
                            S.dma("sp", "ct%d" % s, ct[s][:], ct_s[hh, :, :, rows].rearrange("g p s -> p g s"), writes=["ct%d" % s])
                            S.dma("sp", "xs%d" % s, xs[s][:], xs_s[hh, rows, :], writes=["xs%d" % s])
                            S.dma("sp", "btm%d" % s, btm[s][:], btm_s[hh, rows, :], writes=["btm%d" % s])
                            if dr == 1:
                                S.dma("sp", "ypl%d" % s, ypl[s][:], ypart_s[hh, rows, :], writes=["ypl%d" % s])
                                S.dma("sp", "zsl%d" % s, zsl[s][:], siluz_s[rows, hh * 1024:(hh + 1) * 1024], writes=["zsl%d" % s])

                        bg_of = {}
                        nbmod[0] = 6

                        def early(it):
                            s = it % 2
                            ks = str(s)
                            l = it % 3
                            kl = str(l)
                            if dr == 0:
                                S.op("pool", lambda e: e.tensor_tensor(
                                    out=t3[s][:].rearrange("p (a b) -> p a b", a=16), in0=xs[l][:].rearrange("p (a b) -> p a b", a=16),
                                    in1=bc(dsk[:].unsqueeze(2), [128, 16, 64]), op=ALU.mult), ["xs" + kl, "dsk"], ["t3_" + ks])
                            S.op("dve", lambda e: e.tensor_tensor(out=a_t[s][:], in0=dtt[l][:], in1=Acol, op=ALU.mult), ["dtt" + kl, "A_all"], ["a_t" + ks])
                            S.op("dve", lambda e: e.tensor_copy(a_hi[s][:], a_t[s][:]), ["a_t" + ks], ["a_hi" + ks])
                            S.op("dve", lambda e: e.tensor_tensor(out=a_lo[s][:], in0=a_t[s][:], in1=a_hi[s][:], op=ALU.subtract),
                                 ["a_t" + ks, "a_hi" + ks], ["a_lo" + ks])
                            bu = nextbank()
                            mm(PB[bu][:, 0:16], tri_f[:, dr, :], a_t[s][:], True, True, ["a_t" + ks], [pbk[bu]])
                            mm(PB[bu][:, 16:32], ones_f[:], a_t[s][:], True, True, ["a_t" + ks], [pbk[bu]])
                            S.op("act", lambda e: e.copy(ut[s][:], PB[bu][:, 0:32]), [pbk[bu]], ["ut" + ks])
                            S.op("act", lambda e: e.activation(lndt[s][:], dtt[l][:], AF.Ln), ["dtt" + kl], ["lndt" + ks])
                            S.op("dve", lambda e: e.tensor_tensor(out=biasM[s][:], in0=lndt[s][:], in1=ut[s][:, 0:16], op=ALU.subtract),
                                 ["lndt" + ks, "ut" + ks], ["biasM" + ks])
                            S.op("dve", lambda e: e.tensor_tensor(out=wst[s][:], in0=ut[s][:, 16:32], in1=ut[s][:, 0:16], op=ALU.subtract),
                                 ["ut" + ks], ["wst" + ks])
                            S.op("act", lambda e: e.activation(wst[s][:], wst[s][:], AF.Exp), ["wst" + ks], ["wst" + ks])
                            S.op("dve", lambda e: e.tensor_tensor(out=wst[s][:], in0=wst[s][:], in1=dtt[l][:], op=ALU.mult),
                                 ["wst" + ks, "dtt" + kl], ["wst" + ks])
                            S.op("act", lambda e: e.activation(eu[s][:], ut[s][:, 0:16], AF.Exp), ["ut" + ks], ["eu" + ks])
                            S.op("act", lambda e: e.activation(cd[s][:], ut[s][:, 16:32], AF.Exp), ["ut" + ks], ["cd" + ks])
                            S.op("dve", lambda e: e.tensor_tensor(
                                out=rUh[:], in0=bc(tri_b[:, dr, :].unsqueeze(1), [128, 16, 128]),
                                in1=bc(a_hi[s][:].unsqueeze(2), [128, 16, 128]), op=ALU.mult), ["a_hi" + ks], ["rUh"])
                            S.op("dve", lambda e: e.tensor_tensor(
                                out=rUl[:], in0=bc(tri_b[:, dr, :].unsqueeze(1), [128, 16, 128]),
                                in1=bc(a_lo[s][:].unsqueeze(2), [128, 16, 128]), op=ALU.mult), ["a_lo" + ks], ["rUl"])
                            bg = 6 + (it % 2)
                            for g in range(2):
                                mm(PB[bg][:, g * 128:(g + 1) * 128], bt[l][:, g, :], ct[l][:, g, :], True, True,
                                   ["bt" + kl, "ct" + kl], [pbk[bg]])
                            for g in range(2):
                                dk = "dec%d" % g
                                for hq in range(2):
                                    bq = nextbank()
                                    h0 = g * 8 + hq * 4
                                    bqv = PB[bq][:].rearrange("p (a b) -> p a b", a=4)
                                    mm(bqv, ones_b[:], rUh[:, h0:h0 + 4, :], True, False, ["rUh"], [pbk[bq]])
                                    mm(bqv, ones_b[:], rUl[:, h0:h0 + 4, :], False, False, ["rUl"], [pbk[bq]])
                                    mm(bqv, ident_b[:], mb_b[:, dr, :, :], False, True, [], [pbk[bq]])
                                    for j in range(4):
                                        S.op("act", lambda e: e.activation(
                                            dec[g][:, hq * 4 + j, :], PB[bq][:, j * 128:(j + 1) * 128], AF.Exp,
                                            bias=biasM[s][:, h0 + j:h0 + j + 1]), [pbk[bq], "biasM" + ks], [dk])
                            bg_of[it] = bg

                        def early_b(it):
                            s = it % 2
                            ks = str(s)
                            l = it % 3
                            kl = str(l)
                            bg = bg_of[it]
                            for g in range(2):
                                dk = "dec%d" % g
                                S.op("dve", lambda e: e.tensor_tensor(
                                    out=MT[g][:], in0=dec[g][:], in1=bc(PB[bg][:, g * 128:(g + 1) * 128].unsqueeze(1), [128, 8, 128]),
                                    op=ALU.mult), [dk, pbk[bg]], ["MT%d" % g])
                                bd = nextbank()
                                for h in range(8):
                                    mm(PB[bd][:, h * 64:(h + 1) * 64], MT[g][:, h, :], xs[l][:, (g * 8 + h) * 64:(g * 8 + h + 1) * 64],
                                       True, True, ["MT%d" % g, "xs" + kl], [pbk[bd]])
                                S.op("act", lambda e: e.copy(dsb[s][:, g * 512:(g + 1) * 512], PB[bd][:]), [pbk[bd]], ["dsb%d_%d" % (s, g)])
                                S.op("dve", lambda e: e.tensor_tensor(
                                    out=xsw[g][:].rearrange("p (a b) -> p a b", a=8),
                                    in0=xs[l][:, g * 512:(g + 1) * 512].rearrange("p (a b) -> p a b", a=8),
                                    in1=bc(wst[s][:, g * 8:(g + 1) * 8].unsqueeze(2), [128, 8, 64]), op=ALU.mult),
                                    ["xs" + kl, "wst" + ks], ["xsw%d" % g])
                                bs_ = nextbank()
                                mm(PB[bs_][:], btm[l][:, g * 128:(g + 1) * 128], xsw[g][:], True, True, ["btm" + kl, "xsw%d" % g], [pbk[bs_]])
                                S.op("act", lambda e: e.copy(ssb[s][:, g * 512:(g + 1) * 512], PB[bs_][:]), [pbk[bs_]], ["ssb%d_%d" % (s, g)])

                        def late(it):
                            c = order[it]
                            s = it % 2
                            ks = str(s)
                            l = it % 3
                            kl = str(l)
                            rows = slice(c * 128, (c + 1) * 128)
                            for g in range(2):
                                bo = nextbank()
                                mm(PB[bo][:], ct[l][:, g, :], prevb[:, g, :], True, True, ["ct" + kl, "prevb%d" % g], [pbk[bo]])
                                S.op("dve", lambda e: e.tensor_tensor(
                                    out=t1[:].rearrange("p (a b) -> p a b", a=8), in0=PB[bo][:].rearrange("p (a b) -> p a b", a=8),
                                    in1=bc(eu[s][:, g * 8:(g + 1) * 8].unsqueeze(2), [128, 8, 64]), op=ALU.mult), [pbk[bo], "eu" + ks], ["t1"])
                                S.op("dve", lambda e: e.tensor_tensor(out=yacc[:, g * 512:(g + 1) * 512], in0=t1[:],
                                                                      in1=dsb[s][:, g * 512:(g + 1) * 512], op=ALU.add),
                                     ["t1", "dsb%d_%d" % (s, g)], ["yacc%d" % g])
                                S.op("pool", lambda e: e.tensor_tensor(
                                    out=t2[:].rearrange("p (a b) -> p a b", a=8), in0=state[:, g, :].rearrange("p (a b) -> p a b", a=8),
                                    in1=bc(cd[s][:, g * 8:(g + 1) * 8].unsqueeze(2), [128, 8, 64]), op=ALU.mult), ["state%d" % g, "cd" + ks], ["t2"])
                                S.op("dve", lambda e: e.tensor_tensor(out=state[:, g, :], in0=t2[:], in1=ssb[s][:, g * 512:(g + 1) * 512], op=ALU.add),
                                     ["t2", "ssb%d_%d" % (s, g)], ["state%d" % g])
                                S.op("act", lambda e: e.copy(prevb[:, g, :], state[:, g, :]), ["state%d" % g], ["prevb%d" % g])
                            if dr == 0:
                                S.op("dve", lambda e: e.tensor_tensor(out=yo[s][:], in0=yacc[:], in1=t3[s][:], op=ALU.add),
                                     ["yacc0", "yacc1", "t3_" + ks], ["yo" + ks])
                                S.dma("sp", "yo%d" % s, ypart_s[hh, rows, :], yo[s][:], reads=["yo" + ks])
                            else:
                                S.op("pool", lambda e: e.tensor_tensor(out=t4[:], in0=yacc[:], in1=ypl[l][:], op=ALU.add),
                                     ["yacc0", "yacc1", "ypl" + kl], ["t4"])
                                S.op("dve", lambda e: e.tensor_tensor(out=yg[:], in0=t4[:], in1=zsl[l][:], op=ALU.mult), ["t4", "zsl" + kl], ["yg"])
                                S.op("act", lambda e: e.activation(junk[:], yg[:], AF.Square, accum_out=sscol[:, c:c + 1]), ["yg"], ["junk", "sscol"])
                                S.op("dve", lambda e: e.tensor_tensor(out=yo[s][:], in0=yg[:], in1=nw[:], op=ALU.mult), ["yg", "nw"], ["yo" + ks])
                                S.dma("sp", "yo%d" % s, yssd_s[rows, hh * 1024:(hh + 1) * 1024], yo[s][:], reads=["yo" + ks])

                        for it in range(min(3, NT)):
                            chunk_loads(it)
                        early(0)
                        early_b(0)
                        for it in range(NT):
                            if it + 1 < NT:
                                early(it + 1)
                            late(it)
                            if it + 1 < NT:
                                early_b(it + 1)
                            if it + 3 < NT:
                                chunk_loads(it + 3)
                        if dr == 1:
                            S.dma("sp", "a2", ss_s[hh], sscol[:], reads=["sscol"])
                        nbmod[0] = 8
                        S.flush()

                with ExitStack() as ls:
                    qTh = [sb("qTh%d" % i, [128, SEQ], BF16, ls) for i in range(2)]
                    kz = [[sb("kz%d_%d" % (i, c), [128, SEQ], BF16, ls) for c in range(2)] for i in range(2)]
                    vh = [sb("vh%d" % i, [128, NT, 130], BF16, ls) for i in range(2)]
                    pT = [sb("pT%d" % i, [128, 512], BF16, ls) for i in range(3)]
                    dnw = sb("dnw", [128, 128], F32, ls)
                    sdg = [sb("sdg%d" % i, [128, 4, 128], BF16, ls) for i in range(2)]
                    o0 = sb("o0", [128, 4, 128], F32, ls)
                    o1 = sb("o1", [128, 4, 128], F32, ls)
                    rs = sb("rs", [128, 8], F32, ls)
                    ssq = sb("ssq", [128, 4], F32, ls)
                    jk = sb("jk", [128, 128], BF16, ls)
                    yd = [sb("yd%d" % i, [128, 4, 128], BF16, ls) for i in range(2)]
                    S.dma("sp", "a0", dnw[:], rowp[L, :, R_DNW:R_DNW + 128], writes=["dnw"])
                    S.op("dve", lambda e: e.tensor_scalar(dnw[:], dnw[:], 1.0 - lam_init, None, op0=ALU.mult), ["dnw"], ["dnw"])
                    ACC = [4, 5, 6, 7]
                    items = [(hl, qb, c, kt) for hl in range(4) for qb in range(NB) for c in range(2) for kt in range(NT)]
                    nit = len(items)

                    def load_head(hl):
                        hs = hl % 2
                        hg = hh * 4 + hl
                        S.dma("sp", "qTh%d" % hs, qTh[hs][:], qT_s[hg], writes=["qTh%d" % hs])
                        S.dma("sp", "kTh%d" % hs, kz[hs][0][0:64, :], kT_s[hg, 0:64, :], writes=["kTh%d" % hs])
                        S.dma("sp", "kTh%d" % hs, kz[hs][1][64:128, :], kT_s[hg, 64:128, :], writes=["kTh%d" % hs])
                        S.dma("sp", "vh%d" % hs, vh[hs][:], v_s[hg].rearrange("(t p) e -> p t e", p=128), writes=["vh%d" % hs])

                    def qk(i):
                        hl, qb, c, kt = items[i]
                        hs = hl % 2
                        bsx = i % 4
                        mm(PB[bsx][:], kz[hs][c][:, kt * 128:(kt + 1) * 128], qTh[hs][:, qb * 512:(qb + 1) * 512], True, True,
                           ["qTh%d" % hs, "kTh%d" % hs], [pbk[bsx]])

                    def evac_acc(hl, qb, c):
                        for qt in range(4):
                            a = ACC[qt]
                            S.op("dve", lambda e: e.reciprocal(rs[:, c * 4 + qt:c * 4 + qt + 1], PB[a][:, 128:129]),
                                 [pbk[a]], ["rs%d_%d" % (c, qt)])
                            if c == 0:
                                S.op("dve", lambda e: e.tensor_scalar(o0[:, qt, :], PB[a][:, 0:128], rs[:, qt:qt + 1], None, op0=ALU.mult),
                                     [pbk[a], "rs0_%d" % qt], ["o0_%d" % qt])
                            else:
                                S.op("dve", lambda e: e.tensor_tensor(out=rs[:, 4 + qt:5 + qt], in0=rs[:, 4 + qt:5 + qt], in1=neg_lam[:], op=ALU.mult),
                                     ["rs1_%d" % qt, "neg_lam"], ["rs1_%d" % qt])
                                S.op("dve", lambda e: e.scalar_tensor_tensor(
                                    out=o1[:, qt, :], in0=PB[a][:, 0:128], scalar=rs[:, 4 + qt:5 + qt], in1=o0[:, qt, :],
                                    op0=ALU.mult, op1=ALU.add), [pbk[a], "rs1_%d" % qt, "o0_%d" % qt], ["o1_%d" % qt])

                    def post(hl, qb):
                        ds_ = qb % 2
                        hg = hh * 4 + hl
                        for qt in range(4):
                            S.op("act", lambda e: e.activation(jk[:], o1[:, qt, :], AF.Square, accum_out=ssq[:, qt:qt + 1]),
                                 ["o1_%d" % qt], ["jk", "ssq%d" % qt])
                            S.op("act", lambda e: e.activation(ssq[:, qt:qt + 1], ssq[:, qt:qt + 1], AF.Ln, scale=1.0 / 128.0, bias=EPS),
                                 ["ssq%d" % qt], ["ssq%d" % qt])
                            S.op("act", lambda e: e.activation(ssq[:, qt:qt + 1], ssq[:, qt:qt + 1], AF.Exp, scale=-0.5),
                                 ["ssq%d" % qt], ["ssq%d" % qt])
                            S.op("dve", lambda e: e.scalar_tensor_tensor(
                                out=o1[:, qt, :], in0=o1[:, qt, :], scalar=ssq[:, qt:qt + 1], in1=dnw[:],
                                op0=ALU.mult, op1=ALU.mult), ["o1_%d" % qt, "ssq%d" % qt, "dnw"], ["o1_%d" % qt])
                            S.op("dve", lambda e: e.tensor_tensor(out=yd[ds_][:, qt, :], in0=o1[:, qt, :], in1=sdg[ds_][:, qt, :], op=ALU.mult),
                                 ["o1_%d" % qt, "sdg%d" % ds_], ["yd%d" % ds_])
                        S.dma("pool", "yd%d" % ds_, ydiff_s[qb * 512:(qb + 1) * 512, hg * 128:(hg + 1) * 128].rearrange("(t p) c -> p t c", p=128),
                              yd[ds_][:], reads=["yd%d" % ds_])

                    for i_ in range(2):
                        S.op("pool", lambda e: e.memset(kz[i_][0][64:128, :], 0.0), [], ["kTh%d" % i_])
                        S.op("pool", lambda e: e.memset(kz[i_][1][0:64, :], 0.0), [], ["kTh%d" % i_])
                    load_head(0)
                    LOOK = 3
                    for i_ in range(LOOK):
                        qk(i_)
                    deferred = []
                    for i in range(nit):
                        hl, qb, c, kt = items[i]
                        hs = hl % 2
                        hg = hh * 4 + hl
                        if kt == 0 and c == 1:
                            ds_ = qb % 2
                            S.dma("sp", "sdg%d" % ds_, sdg[ds_][:],
                                  sdg_s[qb * 512:(qb + 1) * 512, hg * 128:(hg + 1) * 128].rearrange("(t p) c -> p t c", p=128),
                                  writes=["sdg%d" % ds_])
                        if kt == 0 and c == 0 and qb == 0 and hl + 1 < 4:
                            load_head(hl + 1)
                        bsx = i % 4
                        ps_ = i % 3
                        S.op("act", lambda e: e.activation(pT[ps_][:], PB[bsx][:], AF.Exp, scale=0.125),
                             [pbk[bsx]], ["pT%d" % ps_])
                        for qt in range(4):
                            mm(PB[ACC[qt]][:, 0:129], pT[ps_][:, qt * 128:(qt + 1) * 128], vh[hs][:, kt, 0:129],
                               kt == 0, kt == NT - 1, ["pT%d" % ps_, "vh%d" % hs], [pbk[ACC[qt]]])
                        if i + LOOK < nit:
                            qk(i + LOOK)
                        while deferred and deferred[0][0] <= i:
                            deferred.pop(0)[1]()
                        if kt == NT - 1:
                            evac_acc(hl, qb, c)
                            if c == 1:
                                deferred.append((i + 3, lambda hl=hl, qb=qb: post(hl, qb)))
                    while deferred:
                        deferred.pop(0)[1]()
                    S.flush()

            with ExitStack() as ls:
                wbr = sb("wbr", [128, 32, D], BF16, ls)
                wgl = sb("wgl", [128, 8, 3072], BF16, ls)
                gbb = sb("gbb", [1, 3072], BF16, ls)
                ssa = sb("ssa", [128, 2, NT], F32, ls)
                xTb = [sb("xTb%d" % i, [128, 8, 128], BF16, ls) for i in range(2)]
                ycat = [sb("ycat%d" % i, [128, 4096], BF16, ls) for i in range(2)]
                ycT = sb("ycT", [128, 32, 128], BF16, ls)
                gsb = [sb("gsb%d" % i, [128, 512], F32, ls) for i in range(2)]
                U = sb("U", [128, D], F32, ls)
                V = sb("V", [128, D], F32, ls)
                tmpv = sb("tmpv", [128, 512], F32, ls)
                rstd = sb("rstd", [128, 1], F32, ls)
                mrg = sb("mrg", [128, D], BF16, ls)
                mTs = [sb("mTs%d" % i, [128, 8, 128], BF16, ls) for i in range(2)]
                wv = w_in[L].rearrange("(k p) c -> p k c", p=128)
                for j in range(3):
                    S.dma("pool", "w0", wgl[:, :, j * 1024:(j + 1) * 1024], wv[:, :, O_GL + j * 1024:O_GL + (j + 1) * 1024], writes=["wgl"])
                wbv = w_br[L].rearrange("(k p) c -> p k c", p=128)
                for j in range(4):
                    S.dma("pool", "w1", wbr[:, j * 8:(j + 1) * 8, :], wbv[:, j * 8:(j + 1) * 8, :], writes=["wbr"])
                S.dma("pool", "a0", gbb[:], rowp[L, 0:1, R_GB:R_GB + 3072], writes=["gbb"])
                S.dma("sp", "a1", ssa[:], ss_s.rearrange("h p t -> p h t"), writes=["ssa"])
                pctr[0] = 0
                for b in range(NB):
                    for tt in range(4):
                        t = b * 4 + tt
                        s = t % 2
                        xk = "xTb%d" % s
                        ys = t % 2
                        yk = "ycat%d" % ys
                        rows = slice(t * 128, (t + 1) * 128)
                        S.dma("sp", xk, xTb[s][:], xT_s[:, :, rows], writes=[xk])
                        S.dma("sp", yk, ycat[ys][:, 0:2048], yssd_s[rows, :], writes=[yk])
                        S.dma("sp", yk, ycat[ys][:, 2048:3072], ydiff_s[rows, :], writes=[yk])
                        S.dma("sp", yk, ycat[ys][:, 3072:4096], ycross_s[rows, :], writes=[yk])
                        S.op("dve", lambda e, t=t: e.tensor_tensor(out=rstd[:], in0=ssa[:, 0, t:t + 1], in1=ssa[:, 1, t:t + 1], op=ALU.add), ["ssa"], ["rstd"])
                        S.op("act", lambda e: e.activation(rstd[:], rstd[:], AF.Ln, scale=1.0 / 2048.0, bias=EPS), ["rstd"], ["rstd"])
                        S.op("act", lambda e: e.activation(rstd[:], rstd[:], AF.Exp, scale=-0.5), ["rstd"], ["rstd"])
                        for q4 in range(4):
                            bi = nextbank()
                            pv = pbf(bi).rearrange("p (a b) -> p a b", a=8)
                            for k in range(8):
                                kk = q4 * 8 + k
                                tr(pv[:, k, :], ycat[ys][:, kk * 128:(kk + 1) * 128], ident_b[:], [yk], [pbk[bi]])
                            evac(ycT[:, q4 * 8:(q4 + 1) * 8, :], pv, [pbk[bi]], ["ycT%d" % q4])
                        kranges = [(0, 16), (16, 24), (24, 32)]
                        for j in range(3):
                            for cb in range(2):
                                gs_ = (j * 2 + cb) % 2
                                bgt = nextbank()
                                gc0 = j * 1024 + cb * 512
                                for k in range(8):
                                    mm(PB[bgt][:], xTb[s][:, k, :], wgl[:, k, gc0:gc0 + 512], k == 0, False,
                                       [xk, "wgl"], [pbk[bgt]])
                                mm(PB[bgt][:], ones_b[0:1, :], gbb[0:1, gc0:gc0 + 512], False, True, ["gbb"], [pbk[bgt]])
                                S.op("act", lambda e, bgt=bgt, gs_=gs_: e.activation(gsb[gs_][:], PB[bgt][:], AF.Sigmoid), [pbk[bgt]], ["gsb%d" % gs_])
                                bp = nextbank()
                                k0, k1 = kranges[j]
                                for kk in range(k0, k1):
                                    mm(PB[bp][:], ycT[:, kk, :], wbr[:, kk, cb * 512:(cb + 1) * 512], kk == k0, kk == k1 - 1,
                                       ["ycT%d" % (kk // 8), "wbr"], [pbk[bp]])
                                cs = slice(cb * 512, (cb + 1) * 512)
                                if j == 0:
                                    S.op("dve", lambda e, bp=bp, gs_=gs_, cs=cs: e.tensor_tensor(out=U[:, cs], in0=PB[bp][:], in1=gsb[gs_][:], op=ALU.mult),
                                         [pbk[bp], "gsb%d" % gs_], ["U%d" % cb])
                                elif j == 1:
                                    S.op("dve", lambda e, bp=bp, gs_=gs_, cs=cs: e.tensor_tensor(out=V[:, cs], in0=PB[bp][:], in1=gsb[gs_][:], op=ALU.mult),
                                         [pbk[bp], "gsb%d" % gs_], ["V%d" % cb])
                                else:
                                    S.op("dve", lambda e, bp=bp, gs_=gs_: e.tensor_tensor(out=tmpv[:], in0=PB[bp][:], in1=gsb[gs_][:], op=ALU.mult),
                                         [pbk[bp], "gsb%d" % gs_], ["tmpv"])
                                    S.op("pool", lambda e, cs=cs: e.tensor_tensor(out=V[:, cs], in0=V[:, cs], in1=tmpv[:], op=ALU.add),
                                         ["V%d" % cb, "tmpv"], ["V%d" % cb])
                        S.op("dve", lambda e: e.scalar_tensor_tensor(out=mrg[:], in0=U[:], scalar=rstd[:], in1=V[:], op0=ALU.mult, op1=ALU.add),
                             ["U0", "U1", "V0", "V1", "rstd"], ["mrg"])
                        bi = nextbank()
                        pv = pbf(bi).rearrange("p (a b) -> p a b", a=8)
                        for k in range(8):
                            tr(pv[:, k, :], mrg[:, k * 128:(k + 1) * 128], ident_b[:], ["mrg"], [pbk[bi]])
                        evac(mTs[s][:], pv, [pbk[bi]], ["mTs%d" % s])
                        S.dma("sp", "mTs%d" % s, mT_s[:, :, rows], mTs[s][:], reads=["mTs%d" % s])
                S.flush()

            with ExitStack() as ls:
                wo = sb("wo", [128, 8, D], BF16, ls)
                lng = sb("lng", [128, D], F32, ls)
                lnb = sb("lnb", [128, D], F32, ls)
                mTb = [sb("mTb%d" % i, [128, 8, 512], BF16, ls) for i in range(2)]
                xr = [sb("xr%d" % i, [128, D], F32, ls) for i in range(2)]
                r = sb("r", [128, D], F32, ls)
                st6 = sb("st6", [128, 2, 6], F32, ls)
                mvv = sb("mvv", [128, 2], F32, ls)
                xn = [sb("xn%d" % i, [128, D], F32, ls) for i in range(2)]
                S.dma("pool", "w0", wo[:], w_out[L].rearrange("(k p) c -> p k c", p=128), writes=["wo"])
                S.dma("sp", "a0", lng[:], rowp[L, :, R_LNG:R_LNG + D], writes=["lng"])
                S.dma("sp", "a1", lnb[:], rowp[L, :, R_LNB:R_LNB + D], writes=["lnb"])
                for b in range(NB):
                    s = b % 2
                    S.dma("sp", "mTb%d" % s, mTb[s][:], mT_s[:, :, b * 512:(b + 1) * 512], writes=["mTb%d" % s])
                    for tt in range(4):
                        t = b * 4 + tt
                        xs_ = t % 2
                        rows = slice(t * 128, (t + 1) * 128)
                        S.dma("sp", "xr%d" % xs_, xr[xs_][:], xsrc[rows, :], writes=["xr%d" % xs_])
                        for cb in range(2):
                            bi = nextbank()
                            for k in range(8):
                                mm(PB[bi][:], mTb[s][:, k, tt * 128:(tt + 1) * 128], wo[:, k, cb * 512:(cb + 1) * 512], k == 0, k == 7,
                                   ["mTb%d" % s, "wo"], [pbk[bi]])
                            S.op("dve", lambda e, bi=bi, cb=cb, xs_=xs_: e.scalar_tensor_tensor(
                                out=r[:, cb * 512:(cb + 1) * 512], in0=xr[xs_][:, cb * 512:(cb + 1) * 512], scalar=ALPHA, in1=PB[bi][:],
                                op0=ALU.mult, op1=ALU.add), ["xr%d" % xs_, pbk[bi]], ["r%d" % cb])
                            S.op("dve", lambda e, cb=cb: e.bn_stats(st6[:, cb, :], r[:, cb * 512:(cb + 1) * 512]), ["r%d" % cb], ["st6_%d" % cb])
                        S.op("dve", lambda e: e.bn_aggr(mvv[:], st6[:]), ["st6_0", "st6_1"], ["mvv"])
                        S.op("act", lambda e: e.activation(mvv[:, 1:2], mvv[:, 1:2], AF.Ln, bias=EPS), ["mvv"], ["mvv"])
                        S.op("act", lambda e: e.activation(mvv[:, 1:2], mvv[:, 1:2], AF.Exp, scale=-0.5), ["mvv"], ["mvv"])
                        S.op("dve", lambda e, xs_=xs_: e.tensor_scalar(xn[xs_][:], r[:], mvv[:, 0:1], mvv[:, 1:2], op0=ALU.subtract, op1=ALU.mult),
                             ["r0", "r1", "mvv"], ["xn%d" % xs_])
                        S.op("pool", lambda e, xs_=xs_: e.tensor_tensor(out=xn[xs_][:], in0=xn[xs_][:], in1=lng[:], op=ALU.mult),
                             ["xn%d" % xs_, "lng"], ["xn%d" % xs_])
                        S.op("dve", lambda e, xs_=xs_: e.tensor_tensor(out=xn[xs_][:], in0=xn[xs_][:], in1=lnb[:], op=ALU.add),
                             ["xn%d" % xs_, "lnb"], ["xn%d" % xs_])
                        S.dma("sp", "xn%d" % xs_, xdst[rows, :], xn[xs_][:], reads=["xn%d" % xs_])
                S.flush()
        print("program built: instr =", S.ninstr, flush=True)
    return nc


def make_consts():
    c = np.zeros((128, C_END), np.float32)
    s = np.arange(128)[:, None]
    t = np.arange(128)[None, :]
    c[:, C_ID:C_ID + 128] = (s == t)
    c[:, C_TL:C_TL + 128] = (s <= t)
    c[:, C_TU:C_TU + 128] = (s >= t)
    c[:, C_MF:C_MF + 128] = np.where(s <= t, 0.0, NEG)
    c[:, C_MB:C_MB + 128] = np.where(s >= t, 0.0, NEG)
    inv = (1.0 / (np.float32(10000.0) ** (np.arange(0, 64, 2, dtype=np.float32) / np.float32(64)))).astype(np.float32)
    c[:, C_INV:C_INV + 32] = inv[None, :]
    c[:, C_ONE:C_ONE + 128] = 1.0
    return c


def prep_inputs(inp, SEQ):
    f = lambda a: np.ascontiguousarray(np.asarray(a))
    B = inp["x"].shape[0]
    NT = SEQ // 128
    rowp = np.concatenate([
        f(inp["dt_bias"]).reshape(DEPTH, 64), f(inp["a_log"]).reshape(DEPTH, 64), f(inp["d_skip"]).reshape(DEPTH, 32),
        f(inp["ssd_norm_w"]), f(inp["diff_norm_w"]), f(inp["ln_g"]), f(inp["ln_b"]),
        f(inp["diff_lam"]).reshape(DEPTH, 256), f(inp["gate_b"]).reshape(DEPTH, 3072)], axis=1).astype(np.float32)
    rowp = np.ascontiguousarray(np.broadcast_to(rowp[:, None, :], (DEPTH, 128, R_END)))
    cw = f(inp["conv_w"])
    cb = f(inp["conv_b"])
    colp = np.concatenate([cw, cb[:, None, :]], axis=1)
    colp = np.ascontiguousarray(colp.reshape(DEPTH, 6, 24, 128).transpose(0, 3, 2, 1)).astype(np.float32)
    w_br = np.ascontiguousarray(np.concatenate([f(inp["w_br_ssd"]), f(inp["w_br_diff"]), f(inp["w_br_cross"])], axis=1))
    consts = make_consts()
    maps = []
    for b in range(B):
        pos = f(inp["positions"])[b].astype(np.int32).reshape(NT, 128).T
        maps.append({
            "x": f(inp["x"][b]), "mem": f(inp["mem"][b]), "pos": np.ascontiguousarray(pos),
            "w_in": f(inp["w_in"]), "w_kv": f(inp["w_mem_kv"]), "w_br": w_br, "w_out": f(inp["w_out"]),
            "rowp": rowp, "colp": colp, "consts": consts})
    return maps


_CACHE = {}


def kernel(**inputs):
    x = np.asarray(inputs["x"])
    B, SEQ, _ = x.shape
    if SEQ not in _CACHE:
        _CACHE[SEQ] = build_program(SEQ)
    nc = _CACHE[SEQ]
    maps = prep_inputs(inputs, SEQ)
    res = run_bass_kernel_spmd(nc, maps, core_ids=list(range(B)))
    out = np.stack([np.asarray(res.results[b]["out"]) for b in range(B)], axis=0)
    return out.astype(np.float32)
```

```python
import math
from contextlib import ExitStack
import numpy as np
import concourse.bass as bass
import concourse.mybir as mybir
from concourse.bass_utils import run_bass_kernel_spmd

F32 = mybir.dt.float32
BF16 = mybir.dt.bfloat16
I32 = mybir.dt.int32
ALU = mybir.AluOpType
AF = mybir.ActivationFunctionType

ENGS = ["pe", "act", "dve", "pool", "sp"]

D = 1024
DEPTH = 2
IN_COLS = 14400
EPS = 1e-5
ALPHA = (2 * DEPTH) ** 0.25
NEG = -30000.0

O_Z, O_XBC, O_DT, O_DQ, O_DK, O_DV, O_DG, O_CQ, O_CG, O_GL = (
    0, 2048, 5120, 5184, 6208, 7232, 8256, 9280, 10304, 11328)
R_DTB, R_ALOG, R_DSK, R_SNW, R_DNW, R_LNG, R_LNB, R_LAM, R_GB, R_END = (
    0, 64, 128, 160, 2208, 2336, 3360, 4384, 4640, 7712)
C_ID, C_TL, C_TU, C_MF, C_MB, C_INV, C_ONE, C_END = 0, 128, 256, 384, 512, 640, 672, 800


class _Rec:
    def __init__(self):
        self.call = None

    def __getattr__(self, name):
        def f(*args, **kw):
            self.call = (name, args, kw)
            return self
        return f


class Sched:
    def __init__(self, nc, stack):
        self.nc = nc
        self.stack = stack
        self.esem = {e: stack.enter_context(nc.semaphore("s_" + e)) for e in ENGS}
        self.count = {e: 0 for e in ENGS}
        self.dsem = {}
        self.waited = {e: {} for e in ENGS}
        self.last_w = {}
        self.readers = {}
        self.ops = {e: [] for e in ENGS}
        self.ninstr = 0

    def _sem(self, key):
        if key in self.esem:
            return self.esem[key]
        return self.dsem[key][0]

    def _deps(self, reads, writes):
        deps = {}

        def add(k, v):
            if deps.get(k, 0) < v:
                deps[k] = v
        for k in reads:
            if k in self.last_w:
                add(*self.last_w[k])
        for k in writes:
            if k in self.last_w:
                add(*self.last_w[k])
            for kk, vv in self.readers.get(k, {}).items():
                add(kk, vv)
        return deps

    def _emit_waits(self, e, deps, skip_self=False):
        for k, v in deps.items():
            if skip_self and k == e:
                continue
            if self.waited[e].get(k, 0) >= v:
                continue
            self.waited[e][k] = v
            sem = self._sem(k)
            self.ops[e].append(lambda eng, sem=sem, v=v: eng.wait_ge(sem, v))

    def _record(self, key, val, reads, writes):
        for k in writes:
            self.last_w[k] = (key, val)
            self.readers[k] = {}
        for k in reads:
            r = self.readers.setdefault(k, {})
            if r.get(key, 0) < val:
                r[key] = val

    def op(self, e, fn, reads=(), writes=()):
        rec = _Rec()
        fn(rec)
        name, args, kw = rec.call
        deps = self._deps(reads, writes)
        self._emit_waits(e, deps, skip_self=(e == "pe"))
        self.count[e] += 1
        val = self.count[e]
        sem = self.esem[e]
        self.ops[e].append(lambda eng, name=name, args=args, kw=kw, sem=sem:
                           getattr(eng, name)(*args, **kw).then_inc(sem, 1))
        self._record(e, val, reads, writes)
        self.ninstr += 1

    def dma(self, q, slot, out, in_, reads=(), writes=(), **kw):
        slot = q + "_" + slot
        if slot not in self.dsem:
            h = self.stack.enter_context(self.nc.semaphore("d_" + slot))
            self.dsem[slot] = [h, 0]
        deps = self._deps(reads, writes)
        self._emit_waits(q, deps)
        self.dsem[slot][1] += 16
        val = self.dsem[slot][1]
        sem = self.dsem[slot][0]
        self.ops[q].append(
            lambda eng, out=out, in_=in_, sem=sem, kw=kw:
            eng.dma_start(out=out, in_=in_, **kw).then_inc(sem, 16))
        self._record(slot, val, reads, writes)
        self.ninstr += 1

    def flush(self):
        alld = {e: self.count[e] for e in ENGS if self.count[e] > 0 and e != "sp"}
        for k, (h, c) in self.dsem.items():
            if c > 0:
                alld[k] = c
        self._emit_waits("sp", alld)
        ops = self.ops
        self.ops = {e: [] for e in ENGS}
        with self.nc.Block() as block:
            @block.tensor
            def _(eng):
                for f in ops["pe"]:
                    f(eng)

            @block.scalar
            def _(eng):
                for f in ops["act"]:
                    f(eng)

            @block.vector
            def _(eng):
                for f in ops["dve"]:
                    f(eng)

            @block.gpsimd
            def _(eng):
                for f in ops["pool"]:
                    f(eng)

            @block.sync
            def _(eng):
                for f in ops["sp"]:
                    f(eng)
        for e in ENGS:
            for k, v in alld.items():
                if self.waited[e].get(k, 0) < v:
                    self.waited[e][k] = v
        self.last_w = {}
        self.readers = {}


def bc(ap, shape):
    return ap.broadcast_to(shape)


def build_program(SEQ, dbg=()):
    NT = SEQ // 128
    NB = SEQ // 512
    nc = bass.Bass("TRN2", target_bir_lowering=False)

    def din(name, shape, dt=F32):
        return nc.dram_tensor(name, shape, dt, kind="ExternalInput").ap()

    x_in = din("x", [SEQ, D])
    mem_in = din("mem", [256, D])
    pos_in = din("pos", [128, NT], I32)
    w_in = din("w_in", [DEPTH, D, IN_COLS])
    w_kv = din("w_kv", [DEPTH, D, 2048])
    w_br = din("w_br", [DEPTH, 4096, D])
    w_out = din("w_out", [DEPTH, D, D])
    rowp = din("rowp", [DEPTH, 128, R_END])
    colp = din("colp", [DEPTH, 128, 24, 6])
    consts = din("consts", [128, C_END])
    out_ap = nc.dram_tensor("out", [SEQ, D], F32, kind="ExternalOutput").ap()

    def scr(name, shape, dt):
        kind = "ExternalOutput" if name in dbg else "Internal"
        return nc.dram_tensor(name, shape, dt, kind=kind).ap()

    xT_s = scr("xT_s", [128, 8, SEQ], BF16)
    raw_s = scr("raw_s", [2, 12, 128, SEQ + 4], F32)
    siluz_s = scr("siluz_s", [SEQ, 2048], BF16)
    dt_s = scr("dt_s", [SEQ, 2, 32], F32)
    qT_s = scr("qT_s", [8, 128, SEQ], BF16)
    kT_s = scr("kT_s", [8, 128, SEQ], BF16)
    v_s = scr("v_s", [8, SEQ, 130], BF16)
    sdg_s = scr("sdg_s", [SEQ, 1024], BF16)
    ycross_s = scr("ycross_s", [SEQ, 1024], BF16)
    xs_s = scr("xs_s", [2, SEQ, 1024], BF16)
    btm_s = scr("btm_s", [2, SEQ, 256], BF16)
    bt_s = scr("bt_s", [2, 2, 128, SEQ], BF16)
    ct_s = scr("ct_s", [2, 2, 128, SEQ], BF16)
    ypart_s = scr("ypart_s", [2, SEQ, 1024], F32)
    yssd_s = scr("yssd_s", [SEQ, 2048], BF16)
    ss_s = scr("ss_s", [2, 128, NT], F32)
    ydiff_s = scr("ydiff_s", [SEQ, 1024], BF16)
    mT_s = scr("mT_s", [128, 8, SEQ], BF16)
    x1_s = scr("x1_s", [SEQ, D], F32)

    with ExitStack() as st:
        S = Sched(nc, st)

        uniq = [0]

        def sb(name, shape, dt, stack=st):
            uniq[0] += 1
            return stack.enter_context(nc.sbuf_tensor("%s_%d" % (name, uniq[0]), shape, dt))

        PB = [st.enter_context(nc.psum_tensor("pb%d" % i, [128, 512], F32)) for i in range(8)]
        pbk = ["pb%d" % i for i in range(8)]
        pctr = [0]

        nbmod = [8]

        def nextbank():
            i = pctr[0] % nbmod[0]
            pctr[0] += 1
            return i

        def pbf(i):
            return PB[i][:].bitcast(BF16)

        cpy_ctr = [0]

        def evac(out, in_, reads, writes, eng=None):
            if eng is None:
                eng = "act" if cpy_ctr[0] % 2 == 0 else "dve"
                cpy_ctr[0] += 1
            if eng == "act":
                S.op("act", lambda e: e.copy(out, in_), reads, writes)
            else:
                S.op(eng, lambda e: e.tensor_copy(out, in_), reads, writes)

        def mm(out, lhsT, rhs, start, stop, reads, writes):
            S.op("pe", lambda e: e.matmul(out, lhsT, rhs, start=start, stop=stop), reads, writes)

        def tr(out, in_, ident, reads, writes):
            S.op("pe", lambda e: e.transpose(out, in_, ident), reads, writes)

        ident_b = sb("ident_b", [128, 128], BF16)
        ones_b = sb("ones_b", [128, 128], BF16)
        ones_f = sb("ones_f", [128, 128], F32)
        tri_f = sb("tri_f", [128, 2, 128], F32)
        tri_b = sb("tri_b", [128, 2, 128], BF16)
        mb_b = sb("mb_b", [128, 2, 4, 128], BF16)
        cos_t = sb("cos_t", [128, NT, 32], F32)
        sin_t = sb("sin_t", [128, NT, 32], F32)
        mkT = sb("mkT", [128, 8, 256], BF16)
        mv_aug = sb("mv_aug", [128, 2, 4, 258], BF16)
        neg_lam = sb("neg_lam", [128, 1], F32)
        A_all = sb("A_all", [128, 64], F32)

        with ExitStack() as ls:
            cst = sb("cst", [128, C_END], F32, ls)
            posi = sb("posi", [128, NT], I32, ls)
            posf = sb("posf", [128, NT], F32, ls)
            ang = sb("ang", [128, NT, 32], F32, ls)
            kf = sb("kf", [128, NT, 32], F32, ls)
            ki = sb("ki", [128, NT, 32], I32, ls)
            a2 = sb("a2", [128, NT, 32], F32, ls)
            S.dma("sp", "pl0", cst[:], consts, writes=["cst"])
            S.dma("sp", "pl1", posi[:], pos_in, writes=["posi"])
            S.op("dve", lambda e: e.tensor_copy(ident_b[:], cst[:, C_ID:C_ID + 128]), ["cst"], ["ident_b"])
            S.op("dve", lambda e: e.tensor_copy(ones_b[:], cst[:, C_ONE:C_ONE + 128]), ["cst"], ["ones_b"])
            S.op("dve", lambda e: e.tensor_copy(ones_f[:], cst[:, C_ONE:C_ONE + 128]), ["cst"], ["ones_f"])
            S.op("dve", lambda e: e.tensor_copy(tri_f[:, 0, :], cst[:, C_TL:C_TL + 128]), ["cst"], ["tri0"])
            S.op("dve", lambda e: e.tensor_copy(tri_f[:, 1, :], cst[:, C_TU:C_TU + 128]), ["cst"], ["tri1"])
            S.op("dve", lambda e: e.tensor_copy(tri_b[:, 0, :], cst[:, C_TL:C_TL + 128]), ["cst"], ["trib0"])
            S.op("dve", lambda e: e.tensor_copy(tri_b[:, 1, :], cst[:, C_TU:C_TU + 128]), ["cst"], ["trib1"])
            for d_ in range(2):
                off = C_MF if d_ == 0 else C_MB
                S.op("dve", lambda e, d_=d_, off=off: e.tensor_copy(
                    mb_b[:, d_, :, :], bc(cst[:, off:off + 128].unsqueeze(1), [128, 4, 128])),
                    ["cst"], ["mb%d" % d_])
            S.op("dve", lambda e: e.tensor_copy(posf[:], posi[:]), ["posi"], ["posf"])
            S.op("dve", lambda e: e.tensor_tensor(
                out=ang[:], in0=bc(posf[:].unsqueeze(2), [128, NT, 32]),
                in1=bc(cst[:, C_INV:C_INV + 32].unsqueeze(1), [128, NT, 32]), op=ALU.mult),
                ["posf", "cst"], ["ang"])
            C1 = float(np.float32(2 * math.pi))
            C2 = float(2 * math.pi - C1)
            S.op("dve", lambda e: e.tensor_scalar(kf[:], ang[:], 1.0 / (2 * math.pi), None, op0=ALU.mult), ["ang"], ["kf"])
            S.op("dve", lambda e: e.tensor_copy(ki[:], kf[:]), ["kf"], ["ki"])
            S.op("dve", lambda e: e.tensor_copy(kf[:], ki[:]), ["ki"], ["kf"])
            S.op("dve", lambda e: e.scalar_tensor_tensor(out=ang[:], in0=kf[:], scalar=-C1, in1=ang[:], op0=ALU.mult, op1=ALU.add), ["kf", "ang"], ["ang"])
            S.op("dve", lambda e: e.scalar_tensor_tensor(out=ang[:], in0=kf[:], scalar=-C2, in1=ang[:], op0=ALU.mult, op1=ALU.add), ["kf", "ang"], ["ang"])

            def wrap_sin(dst, shift, key):
                S.op("dve", lambda e: e.tensor_scalar(a2[:], ang[:], shift, None, op0=ALU.add), ["ang", "a2"], ["a2"])
                S.op("dve", lambda e: e.tensor_scalar(kf[:], a2[:], math.pi, None, op0=ALU.is_gt), ["a2", "kf"], ["kf"])
                S.op("dve", lambda e: e.scalar_tensor_tensor(out=a2[:], in0=kf[:], scalar=-2 * math.pi, in1=a2[:], op0=ALU.mult, op1=ALU.add), ["kf", "a2"], ["a2"])
                S.op("dve", lambda e: e.tensor_scalar(kf[:], a2[:], -math.pi, None, op0=ALU.is_lt), ["a2", "kf"], ["kf"])
                S.op("dve", lambda e: e.scalar_tensor_tensor(out=a2[:], in0=kf[:], scalar=2 * math.pi, in1=a2[:], op0=ALU.mult, op1=ALU.add), ["kf", "a2"], ["a2"])
                S.op("act", lambda e: e.activation(dst[:], a2[:], AF.Sin), ["a2"], [key])
            wrap_sin(sin_t, 0.0, "sin_t")
            wrap_sin(cos_t, math.pi / 2, "cos_t")
            S.flush()

        for L in range(DEPTH):
            xsrc = x_in if L == 0 else x1_s
            xdst = x1_s if L == 0 else out_ap
            lam_init = 0.8 - 0.6 * math.exp(-0.3 * L)

            with ExitStack() as ls:
                lamt = sb("lamt", [128, 256], F32, ls)
                lp = sb("lp", [128, 2, 64], F32, ls)
                ls2 = sb("ls2", [128, 2], F32, ls)
                alog = sb("alog", [128, 64], F32, ls)
                memb = sb("memb", [128, 2, D], BF16, ls)
                memT = sb("memT", [128, 8, 256], BF16, ls)
                wkv = sb("wkv", [128, 8, 2048], BF16, ls)
                S.dma("sp", "a0", lamt[:], rowp[L, :, R_LAM:R_LAM + 256], writes=["lamt"])
                S.dma("sp", "a1", alog[:], rowp[L, :, R_ALOG:R_ALOG + 64], writes=["alog"])
                S.dma("pool", "a2", memb[:], mem_in.rearrange("(c p) d -> p c d", p=128), writes=["memb"])
                S.dma("pool", "a3", wkv[:], w_kv[L].rearrange("(k p) c -> p k c", p=128), writes=["wkv"])
                lv = lamt[:].rearrange("p (a b) -> p a b", a=4)
                S.op("dve", lambda e: e.tensor_tensor(out=lp[:, 0, :], in0=lv[:, 0, :], in1=lv[:, 1, :], op=ALU.mult), ["lamt"], ["lp0"])
                S.op("dve", lambda e: e.tensor_tensor(out=lp[:, 1, :], in0=lv[:, 2, :], in1=lv[:, 3, :], op=ALU.mult), ["lamt"], ["lp1"])
                S.op("dve", lambda e: e.reduce_sum(ls2[:], lp[:], axis=mybir.AxisListType.X), ["lp0", "lp1"], ["ls2"])
                S.op("act", lambda e: e.activation(ls2[:], ls2[:], AF.Exp), ["ls2"], ["ls2"])
                S.op("dve", lambda e: e.tensor_tensor(out=neg_lam[:], in0=ls2[:, 1:2], in1=ls2[:, 0:1], op=ALU.subtract), ["ls2"], ["neg_lam"])
                S.op("dve", lambda e: e.tensor_scalar(neg_lam[:], neg_lam[:], -lam_init, None, op0=ALU.add), ["neg_lam"], ["neg_lam"])
                S.op("act", lambda e: e.activation(A_all[:], alog[:], AF.Exp), ["alog"], ["A_all"])
                S.op("dve", lambda e: e.tensor_scalar(A_all[:], A_all[:], -1.0, None, op0=ALU.mult), ["A_all"], ["A_all"])
                for mc in range(2):
                    bi = nextbank()
                    pv = pbf(bi).rearrange("p (a b) -> p a b", a=8)
                    for k in range(8):
                        tr(pv[:, k, :], memb[:, mc, k * 128:(k + 1) * 128], ident_b[:], ["memb", "ident_b"], [pbk[bi]])
                    evac(memT[:, :, mc * 128:(mc + 1) * 128], pv, [pbk[bi]], ["memT"])
                for j in range(8):
                    bi = nextbank()
                    for k in range(8):
                        mm(PB[bi][:, 0:256], wkv[:, k, j * 128:(j + 1) * 128], memT[:, k, :], k == 0, k == 7,
                           ["wkv", "memT"], [pbk[bi]])
                    evac(mkT[:, j, :], PB[bi][:, 0:256], [pbk[bi]], ["mkT"])
                S.op("pool", lambda e: e.memset(mv_aug[:, :, :, 256:257], 1.0), [], ["mv_aug"])
                S.op("pool", lambda e: e.memset(mv_aug[:, :, :, 257:258], 0.0), [], ["mv_aug"])
                for mc in range(2):
                    for cb in range(2):
                        bi = nextbank()
                        for k in range(8):
                            mm(PB[bi][:], memT[:, k, mc * 128:(mc + 1) * 128], wkv[:, k, 1024 + cb * 512:1024 + (cb + 1) * 512],
                               k == 0, k == 7, ["wkv", "memT"], [pbk[bi]])
                        evac(mv_aug[:, mc, cb * 2:(cb + 1) * 2, 0:256], PB[bi][:].rearrange("p (a b) -> p a b", a=2),
                             [pbk[bi]], ["mv_aug"])
                S.flush()

            with ExitStack() as ls:
                xb = [sb("xb%d" % i, [128, 4, D], BF16, ls) for i in range(2)]
                xTb = [sb("xTb%d" % i, [128, 8, 512], BF16, ls) for i in range(2)]
                for b in range(NB):
                    s = b % 2
                    S.dma("pool", "xb%d" % s, xb[s][:], xsrc[b * 512:(b + 1) * 512, :].rearrange("(t p) d -> p t d", p=128),
                          writes=["xb%d" % s])
                    for tt in range(4):
                        bi = nextbank()
                        pv = pbf(bi).rearrange("p (a b) -> p a b", a=8)
                        for k in range(8):
                            tr(pv[:, k, :], xb[s][:, tt, k * 128:(k + 1) * 128], ident_b[:], ["xb%d" % s], [pbk[bi]])
                        evac(xTb[s][:, :, tt * 128:(tt + 1) * 128], pv, [pbk[bi]], ["xTb%d" % s])
                    S.dma("sp", "xTb%d" % s, xT_s[:, :, b * 512:(b + 1) * 512], xTb[s][:], reads=["xTb%d" % s])
                S.flush()

            for hh in range(2):
                with ExitStack() as ls:
                    wfm = sb("wfm", [128, 8, 1536], BF16, ls)
                    xTb = [sb("xTb%d" % i, [128, 8, 512], BF16, ls) for i in range(2)]
                    stg = [sb("stg%d" % i, [128, 12, 512], F32, ls) for i in range(2)]
                    zt = sb("zt", [128, 12, 2], F32, ls)
                    wv = w_in[L].rearrange("(k p) c -> p k c", p=128)
                    S.dma("pool", "w0", wfm[:, :, 0:1024], wv[:, :, O_XBC + hh * 1024:O_XBC + (hh + 1) * 1024], writes=["wfm"])
                    S.dma("pool", "w0", wfm[:, :, 1024:1280], wv[:, :, O_XBC + 2048 + hh * 256:O_XBC + 2048 + (hh + 1) * 256], writes=["wfm"])
                    S.dma("pool", "w0", wfm[:, :, 1280:1536], wv[:, :, O_XBC + 2560 + hh * 256:O_XBC + 2560 + (hh + 1) * 256], writes=["wfm"])
                    S.op("pool", lambda e: e.memset(zt[:], 0.0), [], ["zt"])
                    rv = raw_s[hh].rearrange("c p s -> p c s")
                    S.dma("sp", "zt", rv[:, :, 0:2], zt[:], reads=["zt"])
                    S.dma("sp", "zt", rv[:, :, SEQ + 2:SEQ + 4], zt[:], reads=["zt"])
                    S.dma("sp", "xTb0", xTb[0][:], xT_s[:, :, 0:512], writes=["xTb0"])
                    for b in range(NB):
                        s = b % 2
                        if b + 1 < NB:
                            S.dma("sp", "xTb%d" % (1 - s), xTb[1 - s][:], xT_s[:, :, (b + 1) * 512:(b + 2) * 512], writes=["xTb%d" % (1 - s)])
                        for c in range(12):
                            bi = nextbank()
                            for k in range(8):
                                mm(PB[bi][:], wfm[:, k, c * 128:(c + 1) * 128], xTb[s][:, k, :], k == 0, k == 7,
                                   ["wfm", "xTb%d" % s], [pbk[bi]])
                            evac(stg[s][:, c, :], PB[bi][:], [pbk[bi]], ["stg%d" % s])
                        S.dma("sp", "stg%d" % s, rv[:, :, 2 + b * 512:2 + (b + 1) * 512], stg[s][:], reads=["stg%d" % s])
                    S.flush()

                with ExitStack() as ls:
                    wtm = sb("wtm", [128, 8, 4128], BF16, ls)
                    dtb = sb("dtb", [128, 32], F32, ls)
                    xTb = [sb("xTb%d" % i, [128, 8, 512], BF16, ls) for i in range(2)]
                    zs = [sb("zs%d" % i, [128, 4, 1024], BF16, ls) for i in range(2)]
                    dts = [sb("dts%d" % i, [128, 4, 32], F32, ls) for i in range(2)]
                    qTs = [sb("qTs%d" % i, [128, 4, 512], BF16, ls) for i in range(2)]
                    kTs = [sb("kTs%d" % i, [128, 4, 512], BF16, ls) for i in range(2)]
                    vs = [sb("vs%d" % i, [128, 4, 4, 130], BF16, ls) for i in range(2)]
                    dgs = [sb("dgs%d" % i, [128, 4, 512], BF16, ls) for i in range(2)]
                    ycs = [sb("ycs%d" % i, [128, 4, 512], BF16, ls) for i in range(2)]
                    dtx = sb("dtx", [128, 32], F32, ls)
                    ra = sb("ra", [128, 8, 32], F32, ls)
                    rb = sb("rb", [128, 8, 32], F32, ls)
                    qr = sb("qr", [128, 512], BF16, ls)
                    cqb = sb("cqb", [128, 512], BF16, ls)
                    cqT = sb("cqT", [128, 4, 128], BF16, ls)
                    pTc = sb("pTc", [128, 4, 128], BF16, ls)
                    cgs = sb("cgs", [128, 512], F32, ls)
                    rc = sb("rc", [128, 2], F32, ls)
                    wv = w_in[L].rearrange("(k p) c -> p k c", p=128)
                    segs = [(0, O_Z + hh * 1024, 1024), (1024, O_DT + hh * 16, 16), (1040, O_DT + 32 + hh * 16, 16),
                            (1056, O_DQ + hh * 512, 512), (1568, O_DK + hh * 512, 512), (2080, O_DV + hh * 512, 512),
                            (2592, O_DG + hh * 512, 512), (3104, O_CQ + hh * 512, 512), (3616, O_CG + hh * 512, 512)]
                    for (o, so, n) in segs:
                        S.dma("pool", "w0", wtm[:, :, o:o + n], wv[:, :, so:so + n], writes=["wtm"])
                    S.dma("sp", "a0", dtb[:, 0:16], rowp[L, :, R_DTB + hh * 16:R_DTB + hh * 16 + 16], writes=["dtb"])
                    S.dma("sp", "a0", dtb[:, 16:32], rowp[L, :, R_DTB + 32 + hh * 16:R_DTB + 32 + hh * 16 + 16], writes=["dtb"])
                    for i in range(2):
                        S.op("pool", lambda e, i=i: e.memset(vs[i][:, :, :, 128:129], 1.0), [], ["vs%d" % i])
                        S.op("pool", lambda e, i=i: e.memset(vs[i][:, :, :, 129:130], 0.0), [], ["vs%d" % i])

                    def grp(bi, width, t_lhs, col0, xk):
                        for k in range(8):
                            mm(PB[bi][:, 0:width], t_lhs(k), wtm[:, k, col0:col0 + width], k == 0, k == 7, ["wtm", xk], [pbk[bi]])

                    def rope(bi, dst, tkey, t):
                        qv = PB[bi][:].rearrange("p (a h j) -> p a h j", a=8, h=2)
                        t1, t2 = qv[:, :, 0, :], qv[:, :, 1, :]
                        cb_ = bc(cos_t[:, t, :].unsqueeze(1), [128, 8, 32])
                        sb_ = bc(sin_t[:, t, :].unsqueeze(1), [128, 8, 32])
                        dv = dst[:].rearrange("p (a h j) -> p a h j", a=8, h=2)
                        S.op("dve", lambda e: e.tensor_tensor(out=ra[:], in0=t1, in1=cb_, op=ALU.mult), [pbk[bi]], ["ra"])
                        S.op("dve", lambda e: e.tensor_tensor(out=rb[:], in0=t2, in1=sb_, op=ALU.mult), [pbk[bi]], ["rb"])
                        S.op("dve", lambda e: e.tensor_tensor(out=dv[:, :, 0, :], in0=ra[:], in1=rb[:], op=ALU.subtract), ["ra", "rb"], [tkey + "a"])
                        S.op("dve", lambda e: e.tensor_tensor(out=ra[:], in0=t1, in1=sb_, op=ALU.mult), [pbk[bi]], ["ra"])
                        S.op("dve", lambda e: e.tensor_tensor(out=rb[:], in0=t2, in1=cb_, op=ALU.mult), [pbk[bi]], ["rb"])
                        S.op("dve", lambda e: e.tensor_tensor(out=dv[:, :, 1, :], in0=ra[:], in1=rb[:], op=ALU.add), ["ra", "rb"], [tkey + "b"])

                    for b in range(NB):
                        s = b % 2
                        xk = "xTb%d" % s
                        S.dma("sp", xk, xTb[s][:], xT_s[:, :, b * 512:(b + 1) * 512], writes=[xk])
                        for tt in range(4):
                            t = b * 4 + tt
                            lh = lambda k, s=s, tt=tt: xTb[s][:, k, tt * 128:(tt + 1) * 128]
                            for cb in range(2):
                                bi = nextbank()
                                grp(bi, 512, lh, cb * 512, xk)
                                S.op("act", lambda e, bi=bi, cb=cb: e.activation(zs[s][:, tt, cb * 512:(cb + 1) * 512], PB[bi][:], AF.Silu),
                                     [pbk[bi]], ["zs%d" % s])
                            bi = nextbank()
                            grp(bi, 32, lh, 1024, xk)
                            S.op("dve", lambda e, bi=bi: e.tensor_tensor(out=dtx[:], in0=PB[bi][:, 0:32], in1=dtb[:], op=ALU.add),
                                 [pbk[bi], "dtb"], ["dtx"])
                            S.op("act", lambda e: e.activation(dtx[:], dtx[:], AF.Exp), ["dtx"], ["dtx"])
                            S.op("act", lambda e: e.activation(dts[s][:, tt, :], dtx[:], AF.Ln, bias=1.0), ["dtx"], ["dts%d" % s])
                            for (col0, dstT, nm) in ((1056, qTs, "qTs"), (1568, kTs, "kTs")):
                                bi = nextbank()
                                grp(bi, 512, lh, col0, xk)
                                rope(bi, qr, "qr", t)
                                b2 = nextbank()
                                pv = pbf(b2).rearrange("p (a b) -> p a b", a=8)
                                for h in range(4):
                                    tr(pv[:, h, :], qr[:, h * 128:(h + 1) * 128], ident_b[:], ["qra", "qrb"], [pbk[b2]])
                                evac(dstT[s][:, :, tt * 128:(tt + 1) * 128], pv[:, 0:4, :], [pbk[b2]], ["%s%d" % (nm, s)])
                            bi = nextbank()
                            grp(bi, 512, lh, 2080, xk)
                            evac(vs[s][:, tt, :, 0:128], PB[bi][:].rearrange("p (a b) -> p a b", a=4), [pbk[bi]], ["vs%d" % s])
                            bi = nextbank()
                            grp(bi, 512, lh, 2592, xk)
                            S.op("act", lambda e, bi=bi: e.activation(dgs[s][:, tt, :], PB[bi][:], AF.Silu), [pbk[bi]], ["dgs%d" % s])
                            bi = nextbank()
                            grp(bi, 512, lh, 3104, xk)
                            evac(cqb[:], PB[bi][:], [pbk[bi]], ["cqb"])
                            b2 = nextbank()
                            pv = pbf(b2).rearrange("p (a b) -> p a b", a=8)
                            for j in range(4):
                                tr(pv[:, j, :], cqb[:, j * 128:(j + 1) * 128], ident_b[:], ["cqb"], [pbk[b2]])
                            evac(cqT[:], pv[:, 0:4, :], [pbk[b2]], ["cqT"])
                            b3 = nextbank()
                            sv = PB[b3][:].rearrange("p (a b) -> p a b", a=4)
                            for hc in range(2):
                                for mc in range(2):
                                    for dc in range(2):
                                        mm(sv[:, hc * 2 + mc, :], mkT[:, (hh * 2 + hc) * 2 + dc, mc * 128:(mc + 1) * 128],
                                           cqT[:, hc * 2 + dc, :], dc == 0, dc == 1, ["mkT", "cqT"], [pbk[b3]])
                            S.op("act", lambda e, b3=b3: e.activation(pTc[:], PB[b3][:].rearrange("p (a b) -> p a b", a=4), AF.Exp, scale=1.0 / 16.0),
                                 [pbk[b3]], ["pTc"])
                            bi = nextbank()
                            grp(bi, 512, lh, 3616, xk)
                            S.op("act", lambda e, bi=bi: e.activation(cgs[:], PB[bi][:], AF.Silu), [pbk[bi]], ["cgs"])
                            for hc in range(2):
                                b4 = nextbank()
                                for mc in range(2):
                                    mm(PB[b4][:, 0:257], pTc[:, hc * 2 + mc, :], mv_aug[:, mc, hh * 2 + hc, 0:257], mc == 0, mc == 1,
                                       ["pTc", "mv_aug"], [pbk[b4]])
                                S.op("dve", lambda e, b4=b4, hc=hc: e.reciprocal(rc[:, hc:hc + 1], PB[b4][:, 256:257]), [pbk[b4]], ["rc%d" % hc])
                                S.op("dve", lambda e, b4=b4, hc=hc: e.scalar_tensor_tensor(
                                    out=ycs[s][:, tt, hc * 256:(hc + 1) * 256], in0=PB[b4][:, 0:256], scalar=rc[:, hc:hc + 1],
                                    in1=cgs[:, hc * 256:(hc + 1) * 256], op0=ALU.mult, op1=ALU.mult),
                                    [pbk[b4], "rc%d" % hc, "cgs"], ["ycs%d" % s])
                        rows = slice(b * 512, (b + 1) * 512)
                        S.dma("sp", "zs%d" % s, siluz_s[rows, hh * 1024:(hh + 1) * 1024].rearrange("(t p) c -> p t c", p=128), zs[s][:], reads=["zs%d" % s])
                        S.dma("sp", "dts%d" % s, dt_s[rows, hh, :].rearrange("(t p) c -> p t c", p=128), dts[s][:], reads=["dts%d" % s])
                        S.dma("sp", "qTs%d" % s, qT_s[hh * 4:(hh + 1) * 4, :, rows].rearrange("h p s -> p h s"), qTs[s][:], reads=["qTs%d" % s])
                        S.dma("sp", "kTs%d" % s, kT_s[hh * 4:(hh + 1) * 4, :, rows].rearrange("h p s -> p h s"), kTs[s][:], reads=["kTs%d" % s])
                        for h in range(4):
                            S.dma("sp", "vs%d" % s, v_s[hh * 4 + h, rows, :].rearrange("(t p) e -> p t e", p=128), vs[s][:, :, h, :], reads=["vs%d" % s])
                        S.dma("sp", "dgs%d" % s, sdg_s[rows, hh * 512:(hh + 1) * 512].rearrange("(t p) c -> p t c", p=128), dgs[s][:], reads=["dgs%d" % s])
                        S.dma("sp", "ycs%d" % s, ycross_s[rows, hh * 512:(hh + 1) * 512].rearrange("(t p) c -> p t c", p=128), ycs[s][:], reads=["ycs%d" % s])
                    S.flush()

                with ExitStack() as ls:
                    cp = sb("cp", [128, 12, 6], F32, ls)
                    raw = [sb("raw%d" % i, [128, 12, 516], F32, ls) for i in range(2)]
                    acc = [sb("acc%d" % i, [128, 512], F32, ls) for i in range(2)]
                    xc = [sb("xc%d" % i, [128, 12, 512], BF16, ls) for i in range(2)]
                    xst = [sb("xst%d" % i, [128, 4, 1024], BF16, ls) for i in range(2)]
                    bst = [sb("bst%d" % i, [128, 4, 256], BF16, ls) for i in range(2)]
                    S.dma("sp", "a0", cp[:, 0:8, :], colp[L, :, hh * 8:(hh + 1) * 8, :], writes=["cp"])
                    S.dma("sp", "a0", cp[:, 8:10, :], colp[L, :, 16 + hh * 2:16 + (hh + 1) * 2, :], writes=["cp"])
                    S.dma("sp", "a0", cp[:, 10:12, :], colp[L, :, 20 + hh * 2:20 + (hh + 1) * 2, :], writes=["cp"])
                    rv = raw_s[hh].rearrange("c p s -> p c s")
                    S.dma("sp", "raw0", raw[0][:], rv[:, :, 0:516], writes=["raw0"])
                    for b in range(NB):
                        s = b % 2
                        if b + 1 < NB:
                            S.dma("sp", "raw%d" % (1 - s), raw[1 - s][:], rv[:, :, (b + 1) * 512:(b + 1) * 512 + 516], writes=["raw%d" % (1 - s)])
                        for c in range(12):
                            a_ = c % 2
                            ak = "acc%d" % a_
                            S.op("dve", lambda e, c=c, a_=a_: e.tensor_scalar(acc[a_][:], raw[s][:, c, 0:512], cp[:, c, 0:1], None, op0=ALU.mult),
                                 ["raw%d" % s, "cp"], [ak])
                            for j in range(1, 5):
                                S.op("dve", lambda e, c=c, a_=a_, j=j: e.scalar_tensor_tensor(
                                    out=acc[a_][:], in0=raw[s][:, c, j:j + 512], scalar=cp[:, c, j:j + 1], in1=acc[a_][:],
                                    op0=ALU.mult, op1=ALU.add), ["raw%d" % s, "cp", ak], [ak])
                            S.op("act", lambda e, c=c, a_=a_: e.activation(xc[s][:, c, :], acc[a_][:], AF.Silu, bias=cp[:, c, 5:6]),
                                 [ak, "cp"], ["xc%d_%d" % (s, c)])
                        for tt in range(4):
                            bi = nextbank()
                            pv = pbf(bi).rearrange("p (a b) -> p a b", a=8)
                            for c in range(8):
                                tr(pv[:, c, :], xc[s][:, c, tt * 128:(tt + 1) * 128], ident_b[:], ["xc%d_%d" % (s, c)], [pbk[bi]])
                            evac(xst[s][:, tt, :].rearrange("p (a b) -> p a b", a=8), pv, [pbk[bi]], ["xst%d" % s])
                            bi = nextbank()
                            pv = pbf(bi).rearrange("p (a b) -> p a b", a=8)
                            for c in range(2):
                                tr(pv[:, c, :], xc[s][:, 8 + c, tt * 128:(tt + 1) * 128], ident_b[:], ["xc%d_%d" % (s, 8 + c)], [pbk[bi]])
                            evac(bst[s][:, tt, :].rearrange("p (a b) -> p a b", a=2), pv[:, 0:2, :], [pbk[bi]], ["bst%d" % s])
                        rows = slice(b * 512, (b + 1) * 512)
                        S.dma("sp", "xst%d" % s, xs_s[hh, rows, :].rearrange("(t p) c -> p t c", p=128), xst[s][:], reads=["xst%d" % s])
                        S.dma("sp", "bst%d" % s, btm_s[hh, rows, :].rearrange("(t p) c -> p t c", p=128), bst[s][:], reads=["bst%d" % s])
                        S.dma("sp", "xcB%d" % s, bt_s[hh, :, :, rows].rearrange("g p s -> p g s"), xc[s][:, 8:10, :],
                              reads=["xc%d_8" % s, "xc%d_9" % s])
                        S.dma("sp", "xcC%d" % s, ct_s[hh, :, :, rows].rearrange("g p s -> p g s"), xc[s][:, 10:12, :],
                              reads=["xc%d_10" % s, "xc%d_11" % s])
                    S.flush()

                for dr in range(2):
                    with ExitStack() as ls:
                        state = sb("state", [128, 2, 512], F32, ls)
                        prevb = sb("prevb", [128, 2, 512], BF16, ls)
                        dsk = sb("dsk", [128, 16], F32, ls)
                        nw = sb("nw", [128, 1024], F32, ls)
                        sscol = sb("sscol", [128, NT], F32, ls)
                        xs = [sb("xs%d" % i, [128, 1024], BF16, ls) for i in range(3)]
                        btm = [sb("btm%d" % i, [128, 256], BF16, ls) for i in range(3)]
                        bt = [sb("bt%d" % i, [128, 2, 128], BF16, ls) for i in range(3)]
                        ct = [sb("ct%d" % i, [128, 2, 128], BF16, ls) for i in range(3)]
                        dtt = [sb("dtt%d" % i, [128, 16], F32, ls) for i in range(3)]
                        ypl = [sb("ypl%d" % i, [128, 1024], F32, ls) for i in range(3)]
                        zsl = [sb("zsl%d" % i, [128, 1024], BF16, ls) for i in range(3)]
                        yo = [sb("yo%d" % i, [128, 1024], F32 if dr == 0 else BF16, ls) for i in range(2)]
                        a_t = [sb("a_t%d" % i, [128, 16], F32, ls) for i in range(2)]
                        a_hi = [sb("a_hi%d" % i, [128, 16], BF16, ls) for i in range(2)]
                        a_lo = [sb("a_lo%d" % i, [128, 16], BF16, ls) for i in range(2)]
                        ut = [sb("ut%d" % i, [128, 32], F32, ls) for i in range(2)]
                        lndt = [sb("lndt%d" % i, [128, 16], F32, ls) for i in range(2)]
                        biasM = [sb("biasM%d" % i, [128, 16], F32, ls) for i in range(2)]
                        wst = [sb("wst%d" % i, [128, 16], F32, ls) for i in range(2)]
                        eu = [sb("eu%d" % i, [128, 16], F32, ls) for i in range(2)]
                        cd = [sb("cd%d" % i, [128, 16], F32, ls) for i in range(2)]
                        rUh = sb("rUh", [128, 16, 128], BF16, ls)
                        rUl = sb("rUl", [128, 16, 128], BF16, ls)
                        dec = [sb("dec%d" % i, [128, 8, 128], BF16, ls) for i in range(2)]
                        MT = [sb("MT%d" % i, [128, 8, 128], BF16, ls) for i in range(2)]
                        xsw = [sb("xsw%d" % i, [128, 512], BF16, ls) for i in range(2)]
                        dsb = [sb("dsb%d" % i, [128, 1024], F32, ls) for i in range(2)]
                        ssb = [sb("ssb%d" % i, [128, 1024], F32, ls) for i in range(2)]
                        t1 = sb("t1", [128, 512], F32, ls)
                        t2 = sb("t2", [128, 512], F32, ls)
                        t3 = [sb("t3_%d" % i, [128, 1024], F32, ls) for i in range(2)]
                        t4 = sb("t4", [128, 1024], F32, ls)
                        yacc = sb("yacc", [128, 1024], F32, ls)
                        yg = sb("yg", [128, 1024], F32, ls)
                        junk = sb("junk", [128, 1024], BF16, ls)
                        S.op("pool", lambda e: e.memset(state[:], 0.0), [], ["state0", "state1"])
                        S.op("pool", lambda e: e.memset(prevb[:], 0.0), [], ["prevb0", "prevb1"])
                        S.dma("sp", "a0", dsk[:], rowp[L, :, R_DSK + hh * 16:R_DSK + (hh + 1) * 16], writes=["dsk"])
                        S.dma("sp", "a1", nw[:], rowp[L, :, R_SNW + hh * 1024:R_SNW + (hh + 1) * 1024], writes=["nw"])
                        Acol = A_all[:, dr * 32 + hh * 16:dr * 32 + hh * 16 + 16]
                        order = list(range(NT)) if dr == 0 else list(range(NT - 1, -1, -1))

                        def chunk_loads(it):
                            c = order[it]
                            s = it % 3
                            rows = slice(c * 128, (c + 1) * 128)
                            S.dma("sp", "dtt%d" % s, dtt[s][:], dt_s[rows, hh, dr * 16:(dr + 1) * 16], writes=["dtt%d" % s])
                            S.dma("sp", "bt%d" % s, bt[s][:], bt_s[hh, :, :, rows].rearrange("g p s -> p g s"), writes=["bt%d" % s])
                            S.dma("sp", "ct%d" % s, ct[s][:], ct_s[hh, :, :, rows].rearrange("g p s -> p g s"), writes=["ct%d" % s])
                            S.dma("sp", "xs%d" % s, xs[s][:], xs_s[hh, rows, :], writes=["xs%d" % s])
                            S.dma("sp", "btm%d" % s, btm[s][:], btm_s[hh, rows, :], writes=["btm%d" % s])
                            if dr == 1:
                                S.dma("sp", "ypl%d" % s, ypl[s][:], ypart_s[hh, rows, :], writes=["ypl%d" % s])
                                S.dma("sp", "zsl%d" % s, zsl[s][:], siluz_s[rows, hh * 1024:(hh + 1) * 1024], writes=["zsl%d" % s])

                        bg_of = {}
                        nbmod[0] = 6

                        def early(it):
                            s = it % 2
                            ks = str(s)
                            l = it % 3
                            kl = str(l)
                            if dr == 0:
                                S.op("pool", lambda e: e.tensor_tensor(
                                    out=t3[s][:].rearrange("p (a b) -> p a b", a=16), in0=xs[l][:].rearrange("p (a b) -> p a b", a=16),
                                    in1=bc(dsk[:].unsqueeze(2), [128, 16, 64]), op=ALU.mult), ["xs" + kl, "dsk"], ["t3_" + ks])
                            S.op("dve", lambda e: e.tensor_tensor(out=a_t[s][:], in0=dtt[l][:], in1=Acol, op=ALU.mult), ["dtt" + kl, "A_all"], ["a_t" + ks])
                            S.op("dve", lambda e: e.tensor_copy(a_hi[s][:], a_t[s][:]), ["a_t" + ks], ["a_hi" + ks])
                            S.op("dve", lambda e: e.tensor_tensor(out=a_lo[s][:], in0=a_t[s][:], in1=a_hi[s][:], op=ALU.subtract),
                                 ["a_t" + ks, "a_hi" + ks], ["a_lo" + ks])
                            bu = nextbank()
                            mm(PB[bu][:, 0:16], tri_f[:, dr, :], a_t[s][:], True, True, ["a_t" + ks], [pbk[bu]])
                            mm(PB[bu][:, 16:32], ones_f[:], a_t[s][:], True, True, ["a_t" + ks], [pbk[bu]])
                            S.op("act", lambda e: e.copy(ut[s][:], PB[bu][:, 0:32]), [pbk[bu]], ["ut" + ks])
                            S.op("act", lambda e: e.activation(lndt[s][:], dtt[l][:], AF.Ln), ["dtt" + kl], ["lndt" + ks])
                            S.op("dve", lambda e: e.tensor_tensor(out=biasM[s][:], in0=lndt[s][:], in1=ut[s][:, 0:16], op=ALU.subtract),
                                 ["lndt" + ks, "ut" + ks], ["biasM" + ks])
                            S.op("dve", lambda e: e.tensor_tensor(out=wst[s][:], in0=ut[s][:, 16:32], in1=ut[s][:, 0:16], op=ALU.subtract),
                                 ["ut" + ks], ["wst" + ks])
                            S.op("act", lambda e: e.activation(wst[s][:], wst[s][:], AF.Exp), ["wst" + ks], ["wst" + ks])
                            S.op("dve", lambda e: e.tensor_tensor(out=wst[s][:], in0=wst[s][:], in1=dtt[l][:], op=ALU.mult),
                                 ["wst" + ks, "dtt" + kl], ["wst" + ks])
                            S.op("act", lambda e: e.activation(eu[s][:], ut[s][:, 0:16], AF.Exp), ["ut" + ks], ["eu" + ks])
                            S.op("act", lambda e: e.activation(cd[s][:], ut[s][:, 16:32], AF.Exp), ["ut" + ks], ["cd" + ks])
                            S.op("dve", lambda e: e.tensor_tensor(
                                out=rUh[:], in0=bc(tri_b[:, dr, :].unsqueeze(1), [128, 16, 128]),
                                in1=bc(a_hi[s][:].unsqueeze(2), [128, 16, 128]), op=ALU.mult), ["a_hi" + ks], ["rUh"])
                            S.op("dve", lambda e: e.tensor_tensor(
                                out=rUl[:], in0=bc(tri_b[:, dr, :].unsqueeze(1), [128, 16, 128]),
                                in1=bc(a_lo[s][:].unsqueeze(2), [128, 16, 128]), op=ALU.mult), ["a_lo" + ks], ["rUl"])
                            bg = 6 + (it % 2)
                            for g in range(2):
                                mm(PB[bg][:, g * 128:(g + 1) * 128], bt[l][:, g, :], ct[l][:, g, :], True, True,
                                   ["bt" + kl, "ct" + kl], [pbk[bg]])
                            for g in range(2):
                                dk = "dec%d" % g
                                for hq in range(2):
                                    bq = nextbank()
                                    h0 = g * 8 + hq * 4
                                    bqv = PB[bq][:].rearrange("p (a b) -> p a b", a=4)
                                    mm(bqv, ones_b[:], rUh[:, h0:h0 + 4, :], True, False, ["rUh"], [pbk[bq]])
                                    mm(bqv, ones_b[:], rUl[:, h0:h0 + 4, :], False, False, ["rUl"], [pbk[bq]])
                                    mm(bqv, ident_b[:], mb_b[:, dr, :, :], False, True, [], [pbk[bq]])
                                    for j in range(4):
                                        S.op("act", lambda e: e.activation(
                                            dec[g][:, hq * 4 + j, :], PB[bq][:, j * 128:(j + 1) * 128], AF.Exp,
                                            bias=biasM[s][:, h0 + j:h0 + j + 1]), [pbk[bq], "biasM" + ks], [dk])
                            bg_of[it] = bg

                        def early_b(it):
                            s = it % 2
                            ks = str(s)
                            l = it % 3
                            kl = str(l)
                            bg = bg_of[it]
                            for g in range(2):
                                dk = "dec%d" % g
                                S.op("dve", lambda e: e.tensor_tensor(
                                    out=MT[g][:], in0=dec[g][:], in1=bc(PB[bg][:, g * 128:(g + 1) * 128].unsqueeze(1), [128, 8, 128]),
                                    op=ALU.mult), [dk, pbk[bg]], ["MT%d" % g])
                                bd = nextbank()
                                for h in range(8):
                                    mm(PB[bd][:, h * 64:(h + 1) * 64], MT[g][:, h, :], xs[l][:, (g * 8 + h) * 64:(g * 8 + h + 1) * 64],
                                       True, True, ["MT%d" % g, "xs" + kl], [pbk[bd]])
                                S.op("act", lambda e: e.copy(dsb[s][:, g * 512:(g + 1) * 512], PB[bd][:]), [pbk[bd]], ["dsb%d_%d" % (s, g)])
                                S.op("dve", lambda e: e.tensor_tensor(
                                    out=xsw[g][:].rearrange("p (a b) -> p a b", a=8),
                                    in0=xs[l][:, g * 512:(g + 1) * 512].rearrange("p (a b) -> p a b", a=8),
                                    in1=bc(wst[s][:, g * 8:(g + 1) * 8].unsqueeze(2), [128, 8, 64]), op=ALU.mult),
                                    ["xs" + kl, "wst" + ks], ["xsw%d" % g])
                                bs_ = nextbank()
                                mm(PB[bs_][:], btm[l][:, g * 128:(g + 1) * 128], xsw[g][:], True, True, ["btm" + kl, "xsw%d" % g], [pbk[bs_]])
                                S.op("act", lambda e: e.copy(ssb[s][:, g * 512:(g + 1) * 512], PB[bs_][:]), [pbk[bs_]], ["ssb%d_%d" % (s, g)])

                        def late(it):
                            c = order[it]
                            s = it % 2
                            ks = str(s)
                            l = it % 3
                            kl = str(l)
                            rows = slice(c * 128, (c + 1) * 128)
                            for g in range(2):
                                bo = nextbank()
                                mm(PB[bo][:], ct[l][:, g, :], prevb[:, g, :], True, True, ["ct" + kl, "prevb%d" % g], [pbk[bo]])
                                S.op("dve", lambda e: e.tensor_tensor(
                                    out=t1[:].rearrange("p (a b) -> p a b", a=8), in0=PB[bo][:].rearrange("p (a b) -> p a b", a=8),
                                    in1=bc(eu[s][:, g * 8:(g + 1) * 8].unsqueeze(2), [128, 8, 64]), op=ALU.mult), [pbk[bo], "eu" + ks], ["t1"])
                                S.op("dve", lambda e: e.tensor_tensor(out=yacc[:, g * 512:(g + 1) * 512], in0=t1[:],
                                                                      in1=dsb[s][:, g * 512:(g + 1) * 512], op=ALU.add),
                                     ["t1", "dsb%d_%d" % (s, g)], ["yacc%d" % g])
                                S.op("pool", lambda e: e.tensor_tensor(
                                    out=t2[:].rearrange("p (a b) -> p a b", a=8), in0=state[:, g, :].rearrange("p (a b) -> p a b", a=8),
                                    in1=bc(cd[s][:, g * 8:(g + 1) * 8].unsqueeze(2), [128, 8, 64]), op=ALU.mult), ["state%d" % g, "cd" + ks], ["t2"])
                                S.op("dve", lambda e: e.tensor_tensor(out=state[:, g, :], in0=t2[:], in1=ssb[s][:, g * 512:(g + 1) * 512], op=ALU.add),
                                     ["t2", "ssb%d_%d" % (s, g)], ["state%d" % g])
                                S.op("act", lambda e: e.copy(prevb[:, g, :], state[:, g, :]), ["state%d" % g], ["prevb%d" % g])
                            if dr == 0:
                                S.op("dve", lambda e: e.tensor_tensor(out=yo[s][:], in0=yacc[:], in1=t3[s][:], op=ALU.add),
                                     ["yacc0", "yacc1", "t3_" + ks], ["yo" + ks])
                                S.dma("sp", "yo%d" % s, ypart_s[hh, rows, :], yo[s][:], reads=["yo" + ks])
                            else:
                                S.op("pool", lambda e: e.tensor_tensor(out=t4[:], in0=yacc[:], in1=ypl[l][:], op=ALU.add),
                                     ["yacc0", "yacc1", "ypl" + kl], ["t4"])
                                S.op("dve", lambda e: e.tensor_tensor(out=yg[:], in0=t4[:], in1=zsl[l][:], op=ALU.mult), ["t4", "zsl" + kl], ["yg"])
                                S.op("act", lambda e: e.activation(junk[:], yg[:], AF.Square, accum_out=sscol[:, c:c + 1]), ["yg"], ["junk", "sscol"])
                                S.op("dve", lambda e: e.tensor_tensor(out=yo[s][:], in0=yg[:], in1=nw[:], op=ALU.mult), ["yg", "nw"], ["yo" + ks])
                                S.dma("sp", "yo%d" % s, yssd_s[rows, hh * 1024:(hh + 1) * 1024], yo[s][:], reads=["yo" + ks])

                        for it in range(min(3, NT)):
                            chunk_loads(it)
                        early(0)
                        early_b(0)
                        for it in range(NT):
                            if it + 1 < NT:
                                early(it + 1)
                            late(it)
                            if it + 1 < NT:
                                early_b(it + 1)
                            if it + 3 < NT:
                                chunk_loads(it + 3)
                        if dr == 1:
                            S.dma("sp", "a2", ss_s[hh], sscol[:], reads=["sscol"])
                        nbmod[0] = 8
                        S.flush()

                with ExitStack() as ls:
                    qTh = [sb("qTh%d" % i, [128, SEQ], BF16, ls) for i in range(2)]
                    kz = [[sb("kz%d_%d" % (i, c), [128, SEQ], BF16, ls) for c in range(2)] for i in range(2)]
                    vh = [sb("vh%d" % i, [128, NT, 130], BF16, ls) for i in range(2)]
                    pT = [sb("pT%d" % i, [128, 512], BF16, ls) for i in range(3)]
                    dnw = sb("dnw", [128, 128], F32, ls)
                    sdg = [sb("sdg%d" % i, [128, 4, 128], BF16, ls) for i in range(2)]
                    o0 = sb("o0", [128, 4, 128], F32, ls)
                    o1 = sb("o1", [128, 4, 128], F32, ls)
                    rs = sb("rs", [128, 8], F32, ls)
                    ssq = sb("ssq", [128, 4], F32, ls)
                    jk = sb("jk", [128, 128], BF16, ls)
                    yd = [sb("yd%d" % i, [128, 4, 128], BF16, ls) for i in range(2)]
                    S.dma("sp", "a0", dnw[:], rowp[L, :, R_DNW:R_DNW + 128], writes=["dnw"])
                    S.op("dve", lambda e: e.tensor_scalar(dnw[:], dnw[:], 1.0 - lam_init, None, op0=ALU.mult), ["dnw"], ["dnw"])
                    ACC = [4, 5, 6, 7]
                    items = [(hl, qb, c, kt) for hl in range(4) for qb in range(NB) for c in range(2) for kt in range(NT)]
                    nit = len(items)

                    def load_head(hl):
                        hs = hl % 2
                        hg = hh * 4 + hl
                        S.dma("sp", "qTh%d" % hs, qTh[hs][:], qT_s[hg], writes=["qTh%d" % hs])
                        S.dma("sp", "kTh%d" % hs, kz[hs][0][0:64, :], kT_s[hg, 0:64, :], writes=["kTh%d" % hs])
                        S.dma("sp", "kTh%d" % hs, kz[hs][1][64:128, :], kT_s[hg, 64:128, :], writes=["kTh%d" % hs])
                        S.dma("sp", "vh%d" % hs, vh[hs][:], v_s[hg].rearrange("(t p) e -> p t e", p=128), writes=["vh%d" % hs])

                    def qk(i):
                        hl, qb, c, kt = items[i]
                        hs = hl % 2
                        bsx = i % 4
                        mm(PB[bsx][:], kz[hs][c][:, kt * 128:(kt + 1) * 128], qTh[hs][:, qb * 512:(qb + 1) * 512], True, True,
                           ["qTh%d" % hs, "kTh%d" % hs], [pbk[bsx]])

                    def evac_acc(hl, qb, c):
                        for qt in range(4):
                            a = ACC[qt]
                            S.op("dve", lambda e: e.reciprocal(rs[:, c * 4 + qt:c * 4 + qt + 1], PB[a][:, 128:129]),
                                 [pbk[a]], ["rs%d_%d" % (c, qt)])
                            if c == 0:
                                S.op("dve", lambda e: e.tensor_scalar(o0[:, qt, :], PB[a][:, 0:128], rs[:, qt:qt + 1], None, op0=ALU.mult),
                                     [pbk[a], "rs0_%d" % qt], ["o0_%d" % qt])
                            else:
                                S.op("dve", lambda e: e.tensor_tensor(out=rs[:, 4 + qt:5 + qt], in0=rs[:, 4 + qt:5 + qt], in1=neg_lam[:], op=ALU.mult),
                                     ["rs1_%d" % qt, "neg_lam"], ["rs1_%d" % qt])
                                S.op("dve", lambda e: e.scalar_tensor_tensor(
                                    out=o1[:, qt, :], in0=PB[a][:, 0:128], scalar=rs[:, 4 + qt:5 + qt], in1=o0[:, qt, :],
                                    op0=ALU.mult, op1=ALU.add), [pbk[a], "rs1_%d" % qt, "o0_%d" % qt], ["o1_%d" % qt])

                    def post(hl, qb):
                        ds_ = qb % 2
                        hg = hh * 4 + hl
                        for qt in range(4):
                            S.op("act", lambda e: e.activation(jk[:], o1[:, qt, :], AF.Square, accum_out=ssq[:, qt:qt + 1]),
                                 ["o1_%d" % qt], ["jk", "ssq%d" % qt])
                            S.op("act", lambda e: e.activation(ssq[:, qt:qt + 1], ssq[:, qt:qt + 1], AF.Ln, scale=1.0 / 128.0, bias=EPS),
                                 ["ssq%d" % qt], ["ssq%d" % qt])
                            S.op("act", lambda e: e.activation(ssq[:, qt:qt + 1], ssq[:, qt:qt + 1], AF.Exp, scale=-0.5),
                                 ["ssq%d" % qt], ["ssq%d" % qt])
                            S.op("dve", lambda e: e.scalar_tensor_tensor(
                                out=o1[:, qt, :], in0=o1[:, qt, :], scalar=ssq[:, qt:qt + 1], in1=dnw[:],
                                op0=ALU.mult, op1=ALU.mult), ["o1_%d" % qt, "ssq%d" % qt, "dnw"], ["o1_%d" % qt])
                            S.op("dve", lambda e: e.tensor_tensor(out=yd[ds_][:, qt, :], in0=o1[:, qt, :], in1=sdg[ds_][:, qt, :], op=ALU.mult),
                                 ["o1_%d" % qt, "sdg%d" % ds_], ["yd%d" % ds_])
                        S.dma("pool", "yd%d" % ds_, ydiff_s[qb * 512:(qb + 1) * 512, hg * 128:(hg + 1) * 128].rearrange("(t p) c -> p t c", p=128),
                              yd[ds_][:], reads=["yd%d" % ds_])

                    for i_ in range(2):
                        S.op("pool", lambda e: e.memset(kz[i_][0][64:128, :], 0.0), [], ["kTh%d" % i_])
                        S.op("pool", lambda e: e.memset(kz[i_][1][0:64, :], 0.0), [], ["kTh%d" % i_])
                    load_head(0)
                    LOOK = 3
                    for i_ in range(LOOK):
                        qk(i_)
                    deferred = []
                    for i in range(nit):
                        hl, qb, c, kt = items[i]
                        hs = hl % 2
                        hg = hh * 4 + hl
                        if kt == 0 and c == 1:
                            ds_ = qb % 2
                            S.dma("sp", "sdg%d" % ds_, sdg[ds_][:],
                                  sdg_s[qb * 512:(qb + 1) * 512, hg * 128:(hg + 1) * 128].rearrange("(t p) c -> p t c", p=128),
                                  writes=["sdg%d" % ds_])
                        if kt == 0 and c == 0 and qb == 0 and hl + 1 < 4:
                            load_head(hl + 1)
                        bsx = i % 4
                        ps_ = i % 3
                        S.op("act", lambda e: e.activation(pT[ps_][:], PB[bsx][:], AF.Exp, scale=0.125),
                             [pbk[bsx]], ["pT%d" % ps_])
                        for qt in range(4):
                            mm(PB[ACC[qt]][:, 0:129], pT[ps_][:, qt * 128:(qt + 1) * 128], vh[hs][:, kt, 0:129],
                               kt == 0, kt == NT - 1, ["pT%d" % ps_, "vh%d" % hs], [pbk[ACC[qt]]])
                        if i + LOOK < nit:
                            qk(i + LOOK)
                        while deferred and deferred[0][0] <= i:
                            deferred.pop(0)[1]()
                        if kt == NT - 1:
                            evac_acc(hl, qb, c)
                            if c == 1:
                                deferred.append((i + 3, lambda hl=hl, qb=qb: post(hl, qb)))
                    while deferred:
                        deferred.pop(0)[1]()
                    S.flush()

            with ExitStack() as ls:
                wbr = sb("wbr", [128, 32, D], BF16, ls)
                wgl = sb("wgl", [128, 8, 3072], BF16, ls)
                gbb = sb("gbb", [1, 3072], BF16, ls)
                ssa = sb("ssa", [128, 2, NT], F32, ls)
                xTb = [sb("xTb%d" % i, [128, 8, 128], BF16, ls) for i in range(2)]
                ycat = [sb("ycat%d" % i, [128, 4096], BF16, ls) for i in range(2)]
                ycT = sb("ycT", [128, 32, 128], BF16, ls)
                gsb = [sb("gsb%d" % i, [128, 512], F32, ls) for i in range(2)]
                U = sb("U", [128, D], F32, ls)
                V = sb("V", [128, D], F32, ls)
                tmpv = sb("tmpv", [128, 512], F32, ls)
                rstd = sb("rstd", [128, 1], F32, ls)
                mrg = sb("mrg", [128, D], BF16, ls)
                mTs = [sb("mTs%d" % i, [128, 8, 128], BF16, ls) for i in range(2)]
                wv = w_in[L].rearrange("(k p) c -> p k c", p=128)
                for j in range(3):
                    S.dma("pool", "w0", wgl[:, :, j * 1024:(j + 1) * 1024], wv[:, :, O_GL + j * 1024:O_GL + (j + 1) * 1024], writes=["wgl"])
                wbv = w_br[L].rearrange("(k p) c -> p k c", p=128)
                for j in range(4):
                    S.dma("pool", "w1", wbr[:, j * 8:(j + 1) * 8, :], wbv[:, j * 8:(j + 1) * 8, :], writes=["wbr"])
                S.dma("pool", "a0", gbb[:], rowp[L, 0:1, R_GB:R_GB + 3072], writes=["gbb"])
                S.dma("sp", "a1", ssa[:], ss_s.rearrange("h p t -> p h t"), writes=["ssa"])
                pctr[0] = 0
                for b in range(NB):
                    for tt in range(4):
                        t = b * 4 + tt
                        s = t % 2
                        xk = "xTb%d" % s
                        ys = t % 2
                        yk = "ycat%d" % ys
                        rows = slice(t * 128, (t + 1) * 128)
                        S.dma("sp", xk, xTb[s][:], xT_s[:, :, rows], writes=[xk])
                        S.dma("sp", yk, ycat[ys][:, 0:2048], yssd_s[rows, :], writes=[yk])
                        S.dma("sp", yk, ycat[ys][:, 2048:3072], ydiff_s[rows, :], writes=[yk])
                        S.dma("sp", yk, ycat[ys][:, 3072:4096], ycross_s[rows, :], writes=[yk])
                        S.op("dve", lambda e, t=t: e.tensor_tensor(out=rstd[:], in0=ssa[:, 0, t:t + 1], in1=ssa[:, 1, t:t + 1], op=ALU.add), ["ssa"], ["rstd"])
                        S.op("act", lambda e: e.activation(rstd[:], rstd[:], AF.Ln, scale=1.0 / 2048.0, bias=EPS), ["rstd"], ["rstd"])
                        S.op("act", lambda e: e.activation(rstd[:], rstd[:], AF.Exp, scale=-0.5), ["rstd"], ["rstd"])
                        for q4 in range(4):
                            bi = nextbank()
                            pv = pbf(bi).rearrange("p (a b) -> p a b", a=8)
                            for k in range(8):
                                kk = q4 * 8 + k
                                tr(pv[:, k, :], ycat[ys][:, kk * 128:(kk + 1) * 128], ident_b[:], [yk], [pbk[bi]])
                            evac(ycT[:, q4 * 8:(q4 + 1) * 8, :], pv, [pbk[bi]], ["ycT%d" % q4])
                        kranges = [(0, 16), (16, 24), (24, 32)]
                        for j in range(3):
                            for cb in range(2):
                                gs_ = (j * 2 + cb) % 2
                                bgt = nextbank()
                                gc0 = j * 1024 + cb * 512
                                for k in range(8):
                                    mm(PB[bgt][:], xTb[s][:, k, :], wgl[:, k, gc0:gc0 + 512], k == 0, False,
                                       [xk, "wgl"], [pbk[bgt]])
                                mm(PB[bgt][:], ones_b[0:1, :], gbb[0:1, gc0:gc0 + 512], False, True, ["gbb"], [pbk[bgt]])
                                S.op("act", lambda e, bgt=bgt, gs_=gs_: e.activation(gsb[gs_][:], PB[bgt][:], AF.Sigmoid), [pbk[bgt]], ["gsb%d" % gs_])
                                bp = nextbank()
                                k0, k1 = kranges[j]
                                for kk in range(k0, k1):
                                    mm(PB[bp][:], ycT[:, kk, :], wbr[:, kk, cb * 512:(cb + 1) * 512], kk == k0, kk == k1 - 1,
                                       ["ycT%d" % (kk // 8), "wbr"], [pbk[bp]])
                                cs = slice(cb * 512, (cb + 1) * 512)
                                if j == 0:
                                    S.op("dve", lambda e, bp=bp, gs_=gs_, cs=cs: e.tensor_tensor(out=U[:, cs], in0=PB[bp][:], in1=gsb[gs_][:], op=ALU.mult),
                                         [pbk[bp], "gsb%d" % gs_], ["U%d" % cb])
                                elif j == 1:
                                    S.op("dve", lambda e, bp=bp, gs_=gs_, cs=cs: e.tensor_tensor(out=V[:, cs], in0=PB[bp][:], in1=gsb[gs_][:], op=ALU.mult),
                                         [pbk[bp], "gsb%d" % gs_], ["V%d" % cb])
                                else:
                                    S.op("dve", lambda e, bp=bp, gs_=gs_: e.tensor_tensor(out=tmpv[:], in0=PB[bp][:], in1=gsb[gs_][:], op=ALU.mult),
                                         [pbk[bp], "gsb%d" % gs_], ["tmpv"])
                                    S.op("pool", lambda e, cs=cs: e.tensor_tensor(out=V[:, cs], in0=V[:, cs], in1=tmpv[:], op=ALU.add),
                                         ["V%d" % cb, "tmpv"], ["V%d" % cb])
                        S.op("dve", lambda e: e.scalar_tensor_tensor(out=mrg[:], in0=U[:], scalar=rstd[:], in1=V[:], op0=ALU.mult, op1=ALU.add),
                             ["U0", "U1", "V0", "V1", "rstd"], ["mrg"])
                        bi = nextbank()
                        pv = pbf(bi).rearrange("p (a b) -> p a b", a=8)
                        for k in range(8):
                            tr(pv[:, k, :], mrg[:, k * 128:(k + 1) * 128], ident_b[:], ["mrg"], [pbk[bi]])
                        evac(mTs[s][:], pv, [pbk[bi]], ["mTs%d" % s])
                        S.dma("sp", "mTs%d" % s, mT_s[:, :, rows], mTs[s][:], reads=["mTs%d" % s])
                S.flush()

            with ExitStack() as ls:
                wo = sb("wo", [128, 8, D], BF16, ls)
                lng = sb("lng", [128, D], F32, ls)
                lnb = sb("lnb", [128, D], F32, ls)
                mTb = [sb("mTb%d" % i, [128, 8, 512], BF16, ls) for i in range(2)]
                xr = [sb("xr%d" % i, [128, D], F32, ls) for i in range(2)]
                r = sb("r", [128, D], F32, ls)
                st6 = sb("st6", [128, 2, 6], F32, ls)
                mvv = sb("mvv", [128, 2], F32, ls)
                xn = [sb("xn%d" % i, [128, D], F32, ls) for i in range(2)]
                S.dma("pool", "w0", wo[:], w_out[L].rearrange("(k p) c -> p k c", p=128), writes=["wo"])
                S.dma("sp", "a0", lng[:], rowp[L, :, R_LNG:R_LNG + D], writes=["lng"])
                S.dma("sp", "a1", lnb[:], rowp[L, :, R_LNB:R_LNB + D], writes=["lnb"])
                for b in range(NB):
                    s = b % 2
                    S.dma("sp", "mTb%d" % s, mTb[s][:], mT_s[:, :, b * 512:(b + 1) * 512], writes=["mTb%d" % s])
                    for tt in range(4):
                        t = b * 4 + tt
                        xs_ = t % 2
                        rows = slice(t * 128, (t + 1) * 128)
                        S.dma("sp", "xr%d" % xs_, xr[xs_][:], xsrc[rows, :], writes=["xr%d" % xs_])
                        for cb in range(2):
                            bi = nextbank()
                            for k in range(8):
                                mm(PB[bi][:], mTb[s][:, k, tt * 128:(tt + 1) * 128], wo[:, k, cb * 512:(cb + 1) * 512], k == 0, k == 7,
                                   ["mTb%d" % s, "wo"], [pbk[bi]])
                            S.op("dve", lambda e, bi=bi, cb=cb, xs_=xs_: e.scalar_tensor_tensor(
                                out=r[:, cb * 512:(cb + 1) * 512], in0=xr[xs_][:, cb * 512:(cb + 1) * 512], scalar=ALPHA, in1=PB[bi][:],
                                op0=ALU.mult, op1=ALU.add), ["xr%d" % xs_, pbk[bi]], ["r%d" % cb])
                            S.op("dve", lambda e, cb=cb: e.bn_stats(st6[:, cb, :], r[:, cb * 512:(cb + 1) * 512]), ["r%d" % cb], ["st6_%d" % cb])
                        S.op("dve", lambda e: e.bn_aggr(mvv[:], st6[:]), ["st6_0", "st6_1"], ["mvv"])
                        S.op("act", lambda e: e.activation(mvv[:, 1:2], mvv[:, 1:2], AF.Ln, bias=EPS), ["mvv"], ["mvv"])
                        S.op("act", lambda e: e.activation(mvv[:, 1:2], mvv[:, 1:2], AF.Exp, scale=-0.5), ["mvv"], ["mvv"])
                        S.op("dve", lambda e, xs_=xs_: e.tensor_scalar(xn[xs_][:], r[:], mvv[:, 0:1], mvv[:, 1:2], op0=ALU.subtract, op1=ALU.mult),
                             ["r0", "r1", "mvv"], ["xn%d" % xs_])
                        S.op("pool", lambda e, xs_=xs_: e.tensor_tensor(out=xn[xs_][:], in0=xn[xs_][:], in1=lng[:], op=ALU.mult),
                             ["xn%d" % xs_, "lng"], ["xn%d" % xs_])
                        S.op("dve", lambda e, xs_=xs_: e.tensor_tensor(out=xn[xs_][:], in0=xn[xs_][:], in1=lnb[:], op=ALU.add),
                             ["xn%d" % xs_, "lnb"], ["xn%d" % xs_])
                        S.dma("sp", "xn%d" % xs_, xdst[rows, :], xn[xs_][:], reads=["xn%d" % xs_])
                S.flush()
        print("program built: instr =", S.ninstr, flush=True)
    return nc


def make_consts():
    c = np.zeros((128, C_END), np.float32)
    s = np.arange(128)[:, None]
    t = np.arange(128)[None, :]
    c[:, C_ID:C_ID + 128] = (s == t)
    c[:, C_TL:C_TL + 128] = (s <= t)
    c[:, C_TU:C_TU + 128] = (s >= t)
    c[:, C_MF:C_MF + 128] = np.where(s <= t, 0.0, NEG)
    c[:, C_MB:C_MB + 128] = np.where(s >= t, 0.0, NEG)
    inv = (1.0 / (np.float32(10000.0) ** (np.arange(0, 64, 2, dtype=np.float32) / np.float32(64)))).astype(np.float32)
    c[:, C_INV:C_INV + 32] = inv[None, :]
    c[:, C_ONE:C_ONE + 128] = 1.0
    return c


def prep_inputs(inp, SEQ):
    f = lambda a: np.ascontiguousarray(np.asarray(a))
    B = inp["x"].shape[0]
    NT = SEQ // 128
    rowp = np.concatenate([
        f(inp["dt_bias"]).reshape(DEPTH, 64), f(inp["a_log"]).reshape(DEPTH, 64), f(inp["d_skip"]).reshape(DEPTH, 32),
        f(inp["ssd_norm_w"]), f(inp["diff_norm_w"]), f(inp["ln_g"]), f(inp["ln_b"]),
        f(inp["diff_lam"]).reshape(DEPTH, 256), f(inp["gate_b"]).reshape(DEPTH, 3072)], axis=1).astype(np.float32)
    rowp = np.ascontiguousarray(np.broadcast_to(rowp[:, None, :], (DEPTH, 128, R_END)))
    cw = f(inp["conv_w"])
    cb = f(inp["conv_b"])
    colp = np.concatenate([cw, cb[:, None, :]], axis=1)
    colp = np.ascontiguousarray(colp.reshape(DEPTH, 6, 24, 128).transpose(0, 3, 2, 1)).astype(np.float32)
    w_br = np.ascontiguousarray(np.concatenate([f(inp["w_br_ssd"]), f(inp["w_br_diff"]), f(inp["w_br_cross"])], axis=1))
    consts = make_consts()
    maps = []
    for b in range(B):
        pos = f(inp["positions"])[b].astype(np.int32).reshape(NT, 128).T
        maps.append({
            "x": f(inp["x"][b]), "mem": f(inp["mem"][b]), "pos": np.ascontiguousarray(pos),
            "w_in": f(inp["w_in"]), "w_kv": f(inp["w_mem_kv"]), "w_br": w_br, "w_out": f(inp["w_out"]),
            "rowp": rowp, "colp": colp, "consts": consts})
    return maps


_CACHE = {}


def kernel(**inputs):
    x = np.asarray(inputs["x"])
    B, SEQ, _ = x.shape
    if SEQ not in _CACHE:
        _CACHE[SEQ] = build_program(SEQ)
    nc = _CACHE[SEQ]
    maps = prep_inputs(inputs, SEQ)
    res = run_bass_kernel_spmd(nc, maps, core_ids=list(range(B)))
    out = np.stack([np.asarray(res.results[b]["out"]) for b in range(B)], axis=0)
    return out.astype(np.float32)
```

```python
import math
from contextlib import ExitStack
import numpy as np
import concourse.bass as bass
import concourse.mybir as mybir
from concourse.bass_utils import run_bass_kernel_spmd

F32 = mybir.dt.float32
BF16 = mybir.dt.bfloat16
I32 = mybir.dt.int32
ALU = mybir.AluOpType
AF = mybir.ActivationFunctionType

ENGS = ["pe", "act", "dve", "pool", "sp"]

D = 1024
DEPTH = 2
IN_COLS = 14400
EPS = 1e-5
ALPHA = (2 * DEPTH) ** 0.25
NEG = -30000.0

O_Z, O_XBC, O_DT, O_DQ, O_DK, O_DV, O_DG, O_CQ, O_CG, O_GL = (
    0, 2048, 5120, 5184, 6208, 7232, 8256, 9280, 10304, 11328)
R_DTB, R_ALOG, R_DSK, R_SNW, R_DNW, R_LNG, R_LNB, R_LAM, R_GB, R_END = (
    0, 64, 128, 160, 2208, 2336, 3360, 4384, 4640, 7712)
C_ID, C_TL, C_TU, C_MF, C_MB, C_INV, C_ONE, C_END = 0, 128, 256, 384, 512, 640, 672, 800


class _Rec:
    def __init__(self):
        self.call = None

    def __getattr__(self, name):
        def f(*args, **kw):
            self.call = (name, args, kw)
            return self
        return f


class Sched:
    def __init__(self, nc, stack):
        self.nc = nc
        self.stack = stack
        self.esem = {e: stack.enter_context(nc.semaphore("s_" + e)) for e in ENGS}
        self.count = {e: 0 for e in ENGS}
        self.dsem = {}
        self.waited = {e: {} for e in ENGS}
        self.last_w = {}
        self.readers = {}
        self.ops = {e: [] for e in ENGS}
        self.ninstr = 0

    def _sem(self, key):
        if key in self.esem:
            return self.esem[key]
        return self.dsem[key][0]

    def _deps(self, reads, writes):
        deps = {}

        def add(k, v):
            if deps.get(k, 0) < v:
                deps[k] = v
        for k in reads:
            if k in self.last_w:
                add(*self.last_w[k])
        for k in writes:
            if k in self.last_w:
                add(*self.last_w[k])
            for kk, vv in self.readers.get(k, {}).items():
                add(kk, vv)
        return deps

    def _emit_waits(self, e, deps, skip_self=False):
        for k, v in deps.items():
            if skip_self and k == e:
                continue
            if self.waited[e].get(k, 0) >= v:
                continue
            self.waited[e][k] = v
            sem = self._sem(k)
            self.ops[e].append(lambda eng, sem=sem, v=v: eng.wait_ge(sem, v))

    def _record(self, key, val, reads, writes):
        for k in writes:
            self.last_w[k] = (key, val)
            self.readers[k] = {}
        for k in reads:
            r = self.readers.setdefault(k, {})
            if r.get(key, 0) < val:
                r[key] = val

    def op(self, e, fn, reads=(), writes=()):
        rec = _Rec()
        fn(rec)
        name, args, kw = rec.call
        deps = self._deps(reads, writes)
        self._emit_waits(e, deps, skip_self=(e == "pe"))
        self.count[e] += 1
        val = self.count[e]
        sem = self.esem[e]
        self.ops[e].append(lambda eng, name=name, args=args, kw=kw, sem=sem:
                           getattr(eng, name)(*args, **kw).then_inc(sem, 1))
        self._record(e, val, reads, writes)
        self.ninstr += 1

    def dma(self, q, slot, out, in_, reads=(), writes=(), **kw):
        slot = q + "_" + slot
        if slot not in self.dsem:
            h = self.stack.enter_context(self.nc.semaphore("d_" + slot))
            self.dsem[slot] = [h, 0]
        deps = self._deps(reads, writes)
        self._emit_waits(q, deps)
        self.dsem[slot][1] += 16
        val = self.dsem[slot][1]
        sem = self.dsem[slot][0]
        self.ops[q].append(
            lambda eng, out=out, in_=in_, sem=sem, kw=kw:
            eng.dma_start(out=out, in_=in_, **kw).then_inc(sem, 16))
        self._record(slot, val, reads, writes)
        self.ninstr += 1

    def flush(self):
        alld = {e: self.count[e] for e in ENGS if self.count[e] > 0 and e != "sp"}
        for k, (h, c) in self.dsem.items():
            if c > 0:
                alld[k] = c
        self._emit_waits("sp", alld)
        ops = self.ops
        self.ops = {e: [] for e in ENGS}
        with self.nc.Block() as block:
            @block.tensor
            def _(eng):
                for f in ops["pe"]:
                    f(eng)

            @block.scalar
            def _(eng):
                for f in ops["act"]:
                    f(eng)

            @block.vector
            def _(eng):
                for f in ops["dve"]:
                    f(eng)

            @block.gpsimd
            def _(eng):
                for f in ops["pool"]:
                    f(eng)

            @block.sync
            def _(eng):
                for f in ops["sp"]:
                    f(eng)
        for e in ENGS:
            for k, v in alld.items():
                if self.waited[e].get(k, 0) < v:
                    self.waited[e][k] = v
        self.last_w = {}
        self.readers = {}


def bc(ap, shape):
    return ap.broadcast_to(shape)


def build_program(SEQ, dbg=()):
    NT = SEQ // 128
    NB = SEQ // 512
    nc = bass.Bass("TRN2", target_bir_lowering=False)

    def din(name, shape, dt=F32):
        return nc.dram_tensor(name, shape, dt, kind="ExternalInput").ap()

    x_in = din("x", [SEQ, D])
    mem_in = din("mem", [256, D])
    pos_in = din("pos", [128, NT], I32)
    w_in = din("w_in", [DEPTH, D, IN_COLS])
    w_kv = din("w_kv", [DEPTH, D, 2048])
    w_br = din("w_br", [DEPTH, 4096, D])
    w_out = din("w_out", [DEPTH, D, D])
    rowp = din("rowp", [DEPTH, 128, R_END])
    colp = din("colp", [DEPTH, 128, 24, 6])
    consts = din("consts", [128, C_END])
    out_ap = nc.dram_tensor("out", [SEQ, D], F32, kind="ExternalOutput").ap()

    def scr(name, shape, dt):
        kind = "ExternalOutput" if name in dbg else "Internal"
        return nc.dram_tensor(name, shape, dt, kind=kind).ap()

    xT_s = scr("xT_s", [128, 8, SEQ], BF16)
    raw_s = scr("raw_s", [2, 12, 128, SEQ + 4], F32)
    siluz_s = scr("siluz_s", [SEQ, 2048], BF16)
    dt_s = scr("dt_s", [SEQ, 2, 32], F32)
    qT_s = scr("qT_s", [8, 128, SEQ], BF16)
    kT_s = scr("kT_s", [8, 128, SEQ], BF16)
    v_s = scr("v_s", [8, SEQ, 130], BF16)
    sdg_s = scr("sdg_s", [SEQ, 1024], BF16)
    ycross_s = scr("ycross_s", [SEQ, 1024], BF16)
    xs_s = scr("xs_s", [2, SEQ, 1024], BF16)
    btm_s = scr("btm_s", [2, SEQ, 256], BF16)
    bt_s = scr("bt_s", [2, 2, 128, SEQ], BF16)
    ct_s = scr("ct_s", [2, 2, 128, SEQ], BF16)
    ypart_s = scr("ypart_s", [2, SEQ, 1024], F32)
    yssd_s = scr("yssd_s", [SEQ, 2048], BF16)
    ss_s = scr("ss_s", [2, 128, NT], F32)
    ydiff_s = scr("ydiff_s", [SEQ, 1024], BF16)
    mT_s = scr("mT_s", [128, 8, SEQ], BF16)
    x1_s = scr("x1_s", [SEQ, D], F32)

    with ExitStack() as st:
        S = Sched(nc, st)

        uniq = [0]

        def sb(name, shape, dt, stack=st):
            uniq[0] += 1
            return stack.enter_context(nc.sbuf_tensor("%s_%d" % (name, uniq[0]), shape, dt))

        PB = [st.enter_context(nc.psum_tensor("pb%d" % i, [128, 512], F32)) for i in range(8)]
        pbk = ["pb%d" % i for i in range(8)]
        pctr = [0]

        nbmod = [8]

        def nextbank():
            i = pctr[0] % nbmod[0]
            pctr[0] += 1
            return i

        def pbf(i):
            return PB[i][:].bitcast(BF16)

        cpy_ctr = [0]

        def evac(out, in_, reads, writes, eng=None):
            if eng is None:
                eng = "act" if cpy_ctr[0] % 2 == 0 else "dve"
                cpy_ctr[0] += 1
            if eng == "act":
                S.op("act", lambda e: e.copy(out, in_), reads, writes)
            else:
                S.op(eng, lambda e: e.tensor_copy(out, in_), reads, writes)

        def mm(out, lhsT, rhs, start, stop, reads, writes):
            S.op("pe", lambda e: e.matmul(out, lhsT, rhs, start=start, stop=stop), reads, writes)

        def tr(out, in_, ident, reads, writes):
            S.op("pe", lambda e: e.transpose(out, in_, ident), reads, writes)

        ident_b = sb("ident_b", [128, 128], BF16)
        ones_b = sb("ones_b", [128, 128], BF16)
        ones_f = sb("ones_f", [128, 128], F32)
        tri_f = sb("tri_f", [128, 2, 128], F32)
        tri_b = sb("tri_b", [128, 2, 128], BF16)
        mb_b = sb("mb_b", [128, 2, 4, 128], BF16)
        cos_t = sb("cos_t", [128, NT, 32], F32)
        sin_t = sb("sin_t", [128, NT, 32], F32)
        mkT = sb("mkT", [128, 8, 256], BF16)
        mv_aug = sb("mv_aug", [128, 2, 4, 258], BF16)
        neg_lam = sb("neg_lam", [128, 1], F32)
        A_all = sb("A_all", [128, 64], F32)

        with ExitStack() as ls:
            cst = sb("cst", [128, C_END], F32, ls)
            posi = sb("posi", [128, NT], I32, ls)
            posf = sb("posf", [128, NT], F32, ls)
            ang = sb("ang", [128, NT, 32], F32, ls)
            kf = sb("kf", [128, NT, 32], F32, ls)
            ki = sb("ki", [128, NT, 32], I32, ls)
            a2 = sb("a2", [128, NT, 32], F32, ls)
            S.dma("sp", "pl0", cst[:], consts, writes=["cst"])
            S.dma("sp", "pl1", posi[:], pos_in, writes=["posi"])
            S.op("dve", lambda e: e.tensor_copy(ident_b[:], cst[:, C_ID:C_ID + 128]), ["cst"], ["ident_b"])
            S.op("dve", lambda e: e.tensor_copy(ones_b[:], cst[:, C_ONE:C_ONE + 128]), ["cst"], ["ones_b"])
            S.op("dve", lambda e: e.tensor_copy(ones_f[:], cst[:, C_ONE:C_ONE + 128]), ["cst"], ["ones_f"])
            S.op("dve", lambda e: e.tensor_copy(tri_f[:, 0, :], cst[:, C_TL:C_TL + 128]), ["cst"], ["tri0"])
            S.op("dve", lambda e: e.tensor_copy(tri_f[:, 1, :], cst[:, C_TU:C_TU + 128]), ["cst"], ["tri1"])
            S.op("dve", lambda e: e.tensor_copy(tri_b[:, 0, :], cst[:, C_TL:C_TL + 128]), ["cst"], ["trib0"])
            S.op("dve", lambda e: e.tensor_copy(tri_b[:, 1, :], cst[:, C_TU:C_TU + 128]), ["cst"], ["trib1"])
            for d_ in range(2):
                off = C_MF if d_ == 0 else C_MB
                S.op("dve", lambda e, d_=d_, off=off: e.tensor_copy(
                    mb_b[:, d_, :, :], bc(cst[:, off:off + 128].unsqueeze(1), [128, 4, 128])),
                    ["cst"], ["mb%d" % d_])
            S.op("dve", lambda e: e.tensor_copy(posf[:], posi[:]), ["posi"], ["posf"])
            S.op("dve", lambda e: e.tensor_tensor(
                out=ang[:], in0=bc(posf[:].unsqueeze(2), [128, NT, 32]),
                in1=bc(cst[:, C_INV:C_INV + 32].unsqueeze(1), [128, NT, 32]), op=ALU.mult),
                ["posf", "cst"], ["ang"])
            C1 = float(np.float32(2 * math.pi))
            C2 = float(2 * math.pi - C1)
            S.op("dve", lambda e: e.tensor_scalar(kf[:], ang[:], 1.0 / (2 * math.pi), None, op0=ALU.mult), ["ang"], ["kf"])
            S.op("dve", lambda e: e.tensor_copy(ki[:], kf[:]), ["kf"], ["ki"])
            S.op("dve", lambda e: e.tensor_copy(kf[:], ki[:]), ["ki"], ["kf"])
            S.op("dve", lambda e: e.scalar_tensor_tensor(out=ang[:], in0=kf[:], scalar=-C1, in1=ang[:], op0=ALU.mult, op1=ALU.add), ["kf", "ang"], ["ang"])
            S.op("dve", lambda e: e.scalar_tensor_tensor(out=ang[:], in0=kf[:], scalar=-C2, in1=ang[:], op0=ALU.mult, op1=ALU.add), ["kf", "ang"], ["ang"])

            def wrap_sin(dst, shift, key):
                S.op("dve", lambda e: e.tensor_scalar(a2[:], ang[:], shift, None, op0=ALU.add), ["ang", "a2"], ["a2"])
                S.op("dve", lambda e: e.tensor_scalar(kf[:], a2[:], math.pi, None, op0=ALU.is_gt), ["a2", "kf"], ["kf"])
                S.op("dve", lambda e: e.scalar_tensor_tensor(out=a2[:], in0=kf[:], scalar=-2 * math.pi, in1=a2[:], op0=ALU.mult, op1=ALU.add), ["kf", "a2"], ["a2"])
                S.op("dve", lambda e: e.tensor_scalar(kf[:], a2[:], -math.pi, None, op0=ALU.is_lt), ["a2", "kf"], ["kf"])
                S.op("dve", lambda e: e.scalar_tensor_tensor(out=a2[:], in0=kf[:], scalar=2 * math.pi, in1=a2[:], op0=ALU.mult, op1=ALU.add), ["kf", "a2"], ["a2"])
                S.op("act", lambda e: e.activation(dst[:], a2[:], AF.Sin), ["a2"], [key])
            wrap_sin(sin_t, 0.0, "sin_t")
            wrap_sin(cos_t, math.pi / 2, "cos_t")
            S.flush()

        for L in range(DEPTH):
            xsrc = x_in if L == 0 else x1_s
            xdst = x1_s if L == 0 else out_ap
            lam_init = 0.8 - 0.6 * math.exp(-0.3 * L)

            with ExitStack() as ls:
                lamt = sb("lamt", [128, 256], F32, ls)
                lp = sb("lp", [128, 2, 64], F32, ls)
                ls2 = sb("ls2", [128, 2], F32, ls)
                alog = sb("alog", [128, 64], F32, ls)
                memb = sb("memb", [128, 2, D], BF16, ls)
                memT = sb("memT", [128, 8, 256], BF16, ls)
                wkv = sb("wkv", [128, 8, 2048], BF16, ls)
                S.dma("sp", "a0", lamt[:], rowp[L, :, R_LAM:R_LAM + 256], writes=["lamt"])
                S.dma("sp", "a1", alog[:], rowp[L, :, R_ALOG:R_ALOG + 64], writes=["alog"])
                S.dma("pool", "a2", memb[:], mem_in.rearrange("(c p) d -> p c d", p=128), writes=["memb"])
                S.dma("pool", "a3", wkv[:], w_kv[L].rearrange("(k p) c -> p k c", p=128), writes=["wkv"])
                lv = lamt[:].rearrange("p (a b) -> p a b", a=4)
                S.op("dve", lambda e: e.tensor_tensor(out=lp[:, 0, :], in0=lv[:, 0, :], in1=lv[:, 1, :], op=ALU.mult), ["lamt"], ["lp0"])
                S.op("dve", lambda e: e.tensor_tensor(out=lp[:, 1, :], in0=lv[:, 2, :], in1=lv[:, 3, :], op=ALU.mult), ["lamt"], ["lp1"])
                S.op("dve", lambda e: e.reduce_sum(ls2[:], lp[:], axis=mybir.AxisListType.X), ["lp0", "lp1"], ["ls2"])
                S.op("act", lambda e: e.activation(ls2[:], ls2[:], AF.Exp), ["ls2"], ["ls2"])
                S.op("dve", lambda e: e.tensor_tensor(out=neg_lam[:], in0=ls2[:, 1:2], in1=ls2[:, 0:1], op=ALU.subtract), ["ls2"], ["neg_lam"])
                S.op("dve", lambda e: e.tensor_scalar(neg_lam[:], neg_lam[:], -lam_init, None, op0=ALU.add), ["neg_lam"], ["neg_lam"])
                S.op("act", lambda e: e.activation(A_all[:], alog[:], AF.Exp), ["alog"], ["A_all"])
                S.op("dve", lambda e: e.tensor_scalar(A_all[:], A_all[:], -1.0, None, op0=ALU.mult), ["A_all"], ["A_all"])
                for mc in range(2):
                    bi = nextbank()
                    pv = pbf(bi).rearrange("p (a b) -> p a b", a=8)
                    for k in range(8):
                        tr(pv[:, k, :], memb[:, mc, k * 128:(k + 1) * 128], ident_b[:], ["memb", "ident_b"], [pbk[bi]])
                    evac(memT[:, :, mc * 128:(mc + 1) * 128], pv, [pbk[bi]], ["memT"])
                for j in range(8):
                    bi = nextbank()
                    for k in range(8):
                        mm(PB[bi][:, 0:256], wkv[:, k, j * 128:(j + 1) * 128], memT[:, k, :], k == 0, k == 7,
                           ["wkv", "memT"], [pbk[bi]])
                    evac(mkT[:, j, :], PB[bi][:, 0:256], [pbk[bi]], ["mkT"])
                S.op("pool", lambda e: e.memset(mv_aug[:, :, :, 256:257], 1.0), [], ["mv_aug"])
                S.op("pool", lambda e: e.memset(mv_aug[:, :, :, 257:258], 0.0), [], ["mv_aug"])
                for mc in range(2):
                    for cb in range(2):
                        bi = nextbank()
                        for k in range(8):
                            mm(PB[bi][:], memT[:, k, mc * 128:(mc + 1) * 128], wkv[:, k, 1024 + cb * 512:1024 + (cb + 1) * 512],
                               k == 0, k == 7, ["wkv", "memT"], [pbk[bi]])
                        evac(mv_aug[:, mc, cb * 2:(cb + 1) * 2, 0:256], PB[bi][:].rearrange("p (a b) -> p a b", a=2),
                             [pbk[bi]], ["mv_aug"])
                S.flush()

            with ExitStack() as ls:
                xb = [sb("xb%d" % i, [128, 4, D], BF16, ls) for i in range(2)]
                xTb = [sb("xTb%d" % i, [128, 8, 512], BF16, ls) for i in range(2)]
                for b in range(NB):
                    s = b % 2
                    S.dma("pool", "xb%d" % s, xb[s][:], xsrc[b * 512:(b + 1) * 512, :].rearrange("(t p) d -> p t d", p=128),
                          writes=["xb%d" % s])
                    for tt in range(4):
                        bi = nextbank()
                        pv = pbf(bi).rearrange("p (a b) -> p a b", a=8)
                        for k in range(8):
                            tr(pv[:, k, :], xb[s][:, tt, k * 128:(k + 1) * 128], ident_b[:], ["xb%d" % s], [pbk[bi]])
                        evac(xTb[s][:, :, tt * 128:(tt + 1) * 128], pv, [pbk[bi]], ["xTb%d" % s])
                    S.dma("sp", "xTb%d" % s, xT_s[:, :, b * 512:(b + 1) * 512], xTb[s][:], reads=["xTb%d" % s])
                S.flush()

            for hh in range(2):
                with ExitStack() as ls:
                    wfm = sb("wfm", [128, 8, 1536], BF16, ls)
                    xTb = [sb("xTb%d" % i, [128, 8, 512], BF16, ls) for i in range(2)]
                    stg = [sb("stg%d" % i, [128, 12, 512], F32, ls) for i in range(2)]
                    zt = sb("zt", [128, 12, 2], F32, ls)
                    wv = w_in[L].rearrange("(k p) c -> p k c", p=128)
                    S.dma("pool", "w0", wfm[:, :, 0:1024], wv[:, :, O_XBC + hh * 1024:O_XBC + (hh + 1) * 1024], writes=["wfm"])
                    S.dma("pool", "w0", wfm[:, :, 1024:1280], wv[:, :, O_XBC + 2048 + hh * 256:O_XBC + 2048 + (hh + 1) * 256], writes=["wfm"])
                    S.dma("pool", "w0", wfm[:, :, 1280:1536], wv[:, :, O_XBC + 2560 + hh * 256:O_XBC + 2560 + (hh + 1) * 256], writes=["wfm"])
                    S.op("pool", lambda e: e.memset(zt[:], 0.0), [], ["zt"])
                    rv = raw_s[hh].rearrange("c p s -> p c s")
                    S.dma("sp", "zt", rv[:, :, 0:2], zt[:], reads=["zt"])
                    S.dma("sp", "zt", rv[:, :, SEQ + 2:SEQ + 4], zt[:], reads=["zt"])
                    S.dma("sp", "xTb0", xTb[0][:], xT_s[:, :, 0:512], writes=["xTb0"])
                    for b in range(NB):
                        s = b % 2
                        if b + 1 < NB:
                            S.dma("sp", "xTb%d" % (1 - s), xTb[1 - s][:], xT_s[:, :, (b + 1) * 512:(b + 2) * 512], writes=["xTb%d" % (1 - s)])
                        for c in range(12):
                            bi = nextbank()
                            for k in range(8):
                                mm(PB[bi][:], wfm[:, k, c * 128:(c + 1) * 128], xTb[s][:, k, :], k == 0, k == 7,
                                   ["wfm", "xTb%d" % s], [pbk[bi]])
                            evac(stg[s][:, c, :], PB[bi][:], [pbk[bi]], ["stg%d" % s])
                        S.dma("sp", "stg%d" % s, rv[:, :, 2 + b * 512:2 + (b + 1) * 512], stg[s][:], reads=["stg%d" % s])
                    S.flush()

                with ExitStack() as ls:
                    wtm = sb("wtm", [128, 8, 4128], BF16, ls)
                    dtb = sb("dtb", [128, 32], F32, ls)
                    xTb = [sb("xTb%d" % i, [128, 8, 512], BF16, ls) for i in range(2)]
                    zs = [sb("zs%d" % i, [128, 4, 1024], BF16, ls) for i in range(2)]
                    dts = [sb("dts%d" % i, [128, 4, 32], F32, ls) for i in range(2)]
                    qTs = [sb("qTs%d" % i, [128, 4, 512], BF16, ls) for i in range(2)]
                    kTs = [sb("kTs%d" % i, [128, 4, 512], BF16, ls) for i in range(2)]
                    vs = [sb("vs%d" % i, [128, 4, 4, 130], BF16, ls) for i in range(2)]
                    dgs = [sb("dgs%d" % i, [128, 4, 512], BF16, ls) for i in range(2)]
                    ycs = [sb("ycs%d" % i, [128, 4, 512], BF16, ls) for i in range(2)]
                    dtx = sb("dtx", [128, 32], F32, ls)
                    ra = sb("ra", [128, 8, 32], F32, ls)
                    rb = sb("rb", [128, 8, 32], F32, ls)
                    qr = sb("qr", [128, 512], BF16, ls)
                    cqb = sb("cqb", [128, 512], BF16, ls)
                    cqT = sb("cqT", [128, 4, 128], BF16, ls)
                    pTc = sb("pTc", [128, 4, 128], BF16, ls)
                    cgs = sb("cgs", [128, 512], F32, ls)
                    rc = sb("rc", [128, 2], F32, ls)
                    wv = w_in[L].rearrange("(k p) c -> p k c", p=128)
                    segs = [(0, O_Z + hh * 1024, 1024), (1024, O_DT + hh * 16, 16), (1040, O_DT + 32 + hh * 16, 16),
                            (1056, O_DQ + hh * 512, 512), (1568, O_DK + hh * 512, 512), (2080, O_DV + hh * 512, 512),
                            (2592, O_DG + hh * 512, 512), (3104, O_CQ + hh * 512, 512), (3616, O_CG + hh * 512, 512)]
                    for (o, so, n) in segs:
                        S.dma("pool", "w0", wtm[:, :, o:o + n], wv[:, :, so:so + n], writes=["wtm"])
                    S.dma("sp", "a0", dtb[:, 0:16], rowp[L, :, R_DTB + hh * 16:R_DTB + hh * 16 + 16], writes=["dtb"])
                    S.dma("sp", "a0", dtb[:, 16:32], rowp[L, :, R_DTB + 32 + hh * 16:R_DTB + 32 + hh * 16 + 16], writes=["dtb"])
                    for i in range(2):
                        S.op("pool", lambda e, i=i: e.memset(vs[i][:, :, :, 128:129], 1.0), [], ["vs%d" % i])
                        S.op("pool", lambda e, i=i: e.memset(vs[i][:, :, :, 129:130], 0.0), [], ["vs%d" % i])

                    def grp(bi, width, t_lhs, col0, xk):
                        for k in range(8):
                            mm(PB[bi][:, 0:width], t_lhs(k), wtm[:, k, col0:col0 + width], k == 0, k == 7, ["wtm", xk], [pbk[bi]])

                    def rope(bi, dst, tkey, t):
                        qv = PB[bi][:].rearrange("p (a h j) -> p a h j", a=8, h=2)
                        t1, t2 = qv[:, :, 0, :], qv[:, :, 1, :]
                        cb_ = bc(cos_t[:, t, :].unsqueeze(1), [128, 8, 32])
                        sb_ = bc(sin_t[:, t, :].unsqueeze(1), [128, 8, 32])
                        dv = dst[:].rearrange("p (a h j) -> p a h j", a=8, h=2)
                        S.op("dve", lambda e: e.tensor_tensor(out=ra[:], in0=t1, in1=cb_, op=ALU.mult), [pbk[bi]], ["ra"])
                        S.op("dve", lambda e: e.tensor_tensor(out=rb[:], in0=t2, in1=sb_, op=ALU.mult), [pbk[bi]], ["rb"])
                        S.op("dve", lambda e: e.tensor_tensor(out=dv[:, :, 0, :], in0=ra[:], in1=rb[:], op=ALU.subtract), ["ra", "rb"], [tkey + "a"])
                        S.op("dve", lambda e: e.tensor_tensor(out=ra[:], in0=t1, in1=sb_, op=ALU.mult), [pbk[bi]], ["ra"])
                        S.op("dve", lambda e: e.tensor_tensor(out=rb[:], in0=t2, in1=cb_, op=ALU.mult), [pbk[bi]], ["rb"])
                        S.op("dve", lambda e: e.tensor_tensor(out=dv[:, :, 1, :], in0=ra[:], in1=rb[:], op=ALU.add), ["ra", "rb"], [tkey + "b"])

                    for b in range(NB):
                        s = b % 2
                        xk = "xTb%d" % s
                        S.dma("sp", xk, xTb[s][:], xT_s[:, :, b * 512:(b + 1) * 512], writes=[xk])
                        for tt in range(4):
                            t = b * 4 + tt
                            lh = lambda k, s=s, tt=tt: xTb[s][:, k, tt * 128:(tt + 1) * 128]
                            for cb in range(2):
                                bi = nextbank()
                                grp(bi, 512, lh, cb * 512, xk)
                                S.op("act", lambda e, bi=bi, cb=cb: e.activation(zs[s][:, tt, cb * 512:(cb + 1) * 512], PB[bi][:], AF.Silu),
                                     [pbk[bi]], ["zs%d" % s])
                            bi = nextbank()
                            grp(bi, 32, lh, 1024, xk)
                            S.op("dve", lambda e, bi=bi: e.tensor_tensor(out=dtx[:], in0=PB[bi][:, 0:32], in1=dtb[:], op=ALU.add),
                                 [pbk[bi], "dtb"], ["dtx"])
                            S.op("act", lambda e: e.activation(dtx[:], dtx[:], AF.Exp), ["dtx"], ["dtx"])
                            S.op("act", lambda e: e.activation(dts[s][:, tt, :], dtx[:], AF.Ln, bias=1.0), ["dtx"], ["dts%d" % s])
                            for (col0, dstT, nm) in ((1056, qTs, "qTs"), (1568, kTs, "kTs")):
                                bi = nextbank()
                                grp(bi, 512, lh, col0, xk)
                                rope(bi, qr, "qr", t)
                                b2 = nextbank()
                                pv = pbf(b2).rearrange("p (a b) -> p a b", a=8)
                                for h in range(4):
                                    tr(pv[:, h, :], qr[:, h * 128:(h + 1) * 128], ident_b[:], ["qra", "qrb"], [pbk[b2]])
                                evac(dstT[s][:, :, tt * 128:(tt + 1) * 128], pv[:, 0:4, :], [pbk[b2]], ["%s%d" % (nm, s)])
                            bi = nextbank()
                            grp(bi, 512, lh, 2080, xk)
                            evac(vs[s][:, tt, :, 0:128], PB[bi][:].rearrange("p (a b) -> p a b", a=4), [pbk[bi]], ["vs%d" % s])
                            bi = nextbank()
                            grp(bi, 512, lh, 2592, xk)
                            S.op("act", lambda e, bi=bi: e.activation(dgs[s][:, tt, :], PB[bi][:], AF.Silu), [pbk[bi]], ["dgs%d" % s])
                            bi = nextbank()
                            grp(bi, 512, lh, 3104, xk)
                            evac(cqb[:], PB[bi][:], [pbk[bi]], ["cqb"])
                            b2 = nextbank()
                            pv = pbf(b2).rearrange("p (a b) -> p a b", a=8)
                            for j in range(4):
                                tr(pv[:, j, :], cqb[:, j * 128:(j + 1) * 128], ident_b[:], ["cqb"], [pbk[b2]])
                            evac(cqT[:], pv[:, 0:4, :], [pbk[b2]], ["cqT"])
                            b3 = nextbank()
                            sv = PB[b3][:].rearrange("p (a b) -> p a b", a=4)
                            for hc in range(2):
                                for mc in range(2):
                                    for dc in range(2):
                                        mm(sv[:, hc * 2 + mc, :], mkT[:, (hh * 2 + hc) * 2 + dc, mc * 128:(mc + 1) * 128],
                                           cqT[:, hc * 2 + dc, :], dc == 0, dc == 1, ["mkT", "cqT"], [pbk[b3]])
                            S.op("act", lambda e, b3=b3: e.activation(pTc[:], PB[b3][:].rearrange("p (a b) -> p a b", a=4), AF.Exp, scale=1.0 / 16.0),
                                 [pbk[b3]], ["pTc"])
                            bi = nextbank()
                            grp(bi, 512, lh, 3616, xk)
                            S.op("act", lambda e, bi=bi: e.activation(cgs[:], PB[bi][:], AF.Silu), [pbk[bi]], ["cgs"])
                            for hc in range(2):
                                b4 = nextbank()
                                for mc in range(2):
                                    mm(PB[b4][:, 0:257], pTc[:, hc * 2 + mc, :], mv_aug[:, mc, hh * 2 + hc, 0:257], mc == 0, mc == 1,
                                       ["pTc", "mv_aug"], [pbk[b4]])
                                S.op("dve", lambda e, b4=b4, hc=hc: e.reciprocal(rc[:, hc:hc + 1], PB[b4][:, 256:257]), [pbk[b4]], ["rc%d" % hc])
                                S.op("dve", lambda e, b4=b4, hc=hc: e.scalar_tensor_tensor(
                                    out=ycs[s][:, tt, hc * 256:(hc + 1) * 256], in0=PB[b4][:, 0:256], scalar=rc[:, hc:hc + 1],
                                    in1=cgs[:, hc * 256:(hc + 1) * 256], op0=ALU.mult, op1=ALU.mult),
                                    [pbk[b4], "rc%d" % hc, "cgs"], ["ycs%d" % s])
                        rows = slice(b * 512, (b + 1) * 512)
                        S.dma("sp", "zs%d" % s, siluz_s[rows, hh * 1024:(hh + 1) * 1024].rearrange("(t p) c -> p t c", p=128), zs[s][:], reads=["zs%d" % s])
                        S.dma("sp", "dts%d" % s, dt_s[rows, hh, :].rearrange("(t p) c -> p t c", p=128), dts[s][:], reads=["dts%d" % s])
                        S.dma("sp", "qTs%d" % s, qT_s[hh * 4:(hh + 1) * 4, :, rows].rearrange("h p s -> p h s"), qTs[s][:], reads=["qTs%d" % s])
                        S.dma("sp", "kTs%d" % s, kT_s[hh * 4:(hh + 1) * 4, :, rows].rearrange("h p s -> p h s"), kTs[s][:], reads=["kTs%d" % s])
                        for h in range(4):
                            S.dma("sp", "vs%d" % s, v_s[hh * 4 + h, rows, :].rearrange("(t p) e -> p t e", p=128), vs[s][:, :, h, :], reads=["vs%d" % s])
                        S.dma("sp", "dgs%d" % s, sdg_s[rows, hh * 512:(hh + 1) * 512].rearrange("(t p) c -> p t c", p=128), dgs[s][:], reads=["dgs%d" % s])
                        S.dma("sp", "ycs%d" % s, ycross_s[rows, hh * 512:(hh + 1) * 512].rearrange("(t p) c -> p t c", p=128), ycs[s][:], reads=["ycs%d" % s])
                    S.flush()

                with ExitStack() as ls:
                    cp = sb("cp", [128, 12, 6], F32, ls)
                    raw = [sb("raw%d" % i, [128, 12, 516], F32, ls) for i in range(2)]
                    acc = [sb("acc%d" % i, [128, 512], F32, ls) for i in range(2)]
                    xc = [sb("xc%d" % i, [128, 12, 512], BF16, ls) for i in range(2)]
                    xst = [sb("xst%d" % i, [128, 4, 1024], BF16, ls) for i in range(2)]
                    bst = [sb("bst%d" % i, [128, 4, 256], BF16, ls) for i in range(2)]
                    S.dma("sp", "a0", cp[:, 0:8, :], colp[L, :, hh * 8:(hh + 1) * 8, :], writes=["cp"])
                    S.dma("sp", "a0", cp[:, 8:10, :], colp[L, :, 16 + hh * 2:16 + (hh + 1) * 2, :], writes=["cp"])
                    S.dma("sp", "a0", cp[:, 10:12, :], colp[L, :, 20 + hh * 2:20 + (hh + 1) * 2, :], writes=["cp"])
                    rv = raw_s[hh].rearrange("c p s -> p c s")
                    S.dma("sp", "raw0", raw[0][:], rv[:, :, 0:516], writes=["raw0"])
                    for b in range(NB):
                        s = b % 2
                        if b + 1 < NB:
                            S.dma("sp", "raw%d" % (1 - s), raw[1 - s][:], rv[:, :, (b + 1) * 512:(b + 1) * 512 + 516], writes=["raw%d" % (1 - s)])
                        for c in range(12):
                            a_ = c % 2
                            ak = "acc%d" % a_
                            S.op("dve", lambda e, c=c, a_=a_: e.tensor_scalar(acc[a_][:], raw[s][:, c, 0:512], cp[:, c, 0:1], None, op0=ALU.mult),
                                 ["raw%d" % s, "cp"], [ak])
                            for j in range(1, 5):
                                S.op("dve", lambda e, c=c, a_=a_, j=j: e.scalar_tensor_tensor(
                                    out=acc[a_][:], in0=raw[s][:, c, j:j + 512], scalar=cp[:, c, j:j + 1], in1=acc[a_][:],
                                    op0=ALU.mult, op1=ALU.add), ["raw%d" % s, "cp", ak], [ak])
                            S.op("act", lambda e, c=c, a_=a_: e.activation(xc[s][:, c, :], acc[a_][:], AF.Silu, bias=cp[:, c, 5:6]),
                                 [ak, "cp"], ["xc%d_%d" % (s, c)])
                        for tt in range(4):
                            bi = nextbank()
                            pv = pbf(bi).rearrange("p (a b) -> p a b", a=8)
                            for c in range(8):
                                tr(pv[:, c, :], xc[s][:, c, tt * 128:(tt + 1) * 128], ident_b[:], ["xc%d_%d" % (s, c)], [pbk[bi]])
                            evac(xst[s][:, tt, :].rearrange("p (a b) -> p a b", a=8), pv, [pbk[bi]], ["xst%d" % s])
                            bi = nextbank()
                            pv = pbf(bi).rearrange("p (a b) -> p a b", a=8)
                            for c in range(2):
                                tr(pv[:, c, :], xc[s][:, 8 + c, tt * 128:(tt + 1) * 128], ident_b[:], ["xc%d_%d" % (s, 8 + c)], [pbk[bi]])
                            evac(bst[s][:, tt, :].rearrange("p (a b) -> p a b", a=2), pv[:, 0:2, :], [pbk[bi]], ["bst%d" % s])
                        rows = slice(b * 512, (b + 1) * 512)
                        S.dma("sp", "xst%d" % s, xs_s[hh, rows, :].rearrange("(t p) c -> p t c", p=128), xst[s][:], reads=["xst%d" % s])
                        S.dma("sp", "bst%d" % s, btm_s[hh, rows, :].rearrange("(t p) c -> p t c", p=128), bst[s][:], reads=["bst%d" % s])
                        S.dma("sp", "xcB%d" % s, bt_s[hh, :, :, rows].rearrange("g p s -> p g s"), xc[s][:, 8:10, :],
                              reads=["xc%d_8" % s, "xc%d_9" % s])
                        S.dma("sp", "xcC%d" % s, ct_s[hh, :, :, rows].rearrange("g p s -> p g s"), xc[s][:, 10:12, :],
                              reads=["xc%d_10" % s, "xc%d_11" % s])
                    S.flush()

                for dr in range(2):
                    with ExitStack() as ls:
                        state = sb("state", [128, 2, 512], F32, ls)
                        prevb = sb("prevb", [128, 2, 512], BF16, ls)
                        dsk = sb("dsk", [128, 16], F32, ls)
                        nw = sb("nw", [128, 1024], F32, ls)
                        sscol = sb("sscol", [128, NT], F32, ls)
                        xs = [sb("xs%d" % i, [128, 1024], BF16, ls) for i in range(3)]
                        btm = [sb("btm%d" % i, [128, 256], BF16, ls) for i in range(3)]
                        bt = [sb("bt%d" % i, [128, 2, 128], BF16, ls) for i in range(3)]
                        ct = [sb("ct%d" % i, [128, 2, 128], BF16, ls) for i in range(3)]
                        dtt = [sb("dtt%d" % i, [128, 16], F32, ls) for i in range(3)]
                        ypl = [sb("ypl%d" % i, [128, 1024], F32, ls) for i in range(3)]
                        zsl = [sb("zsl%d" % i, [128, 1024], BF16, ls) for i in range(3)]
                        yo = [sb("yo%d" % i, [128, 1024], F32 if dr == 0 else BF16, ls) for i in range(2)]
                        a_t = [sb("a_t%d" % i, [128, 16], F32, ls) for i in range(2)]
                        a_hi = [sb("a_hi%d" % i, [128, 16], BF16, ls) for i in range(2)]
                        a_lo = [sb("a_lo%d" % i, [128, 16], BF16, ls) for i in range(2)]
                        ut = [sb("ut%d" % i, [128, 32], F32, ls) for i in range(2)]
                        lndt = [sb("lndt%d" % i, [128, 16], F32, ls) for i in range(2)]
                        biasM = [sb("biasM%d" % i, [128, 16], F32, ls) for i in range(2)]
                        wst = [sb("wst%d" % i, [128, 16], F32, ls) for i in range(2)]
                        eu = [sb("eu%d" % i, [128, 16], F32, ls) for i in range(2)]
                        cd = [sb("cd%d" % i, [128, 16], F32, ls) for i in range(2)]
                        rUh = sb("rUh", [128, 16, 128], BF16, ls)
                        rUl = sb("rUl", [128, 16, 128], BF16, ls)
                        dec = [sb("dec%d" % i, [128, 8, 128], BF16, ls) for i in range(2)]
                        MT = [sb("MT%d" % i, [128, 8, 128], BF16, ls) for i in range(2)]
                        xsw = [sb("xsw%d" % i, [128, 512], BF16, ls) for i in range(2)]
                        dsb = [sb("dsb%d" % i, [128, 1024], F32, ls) for i in range(2)]
                        ssb = [sb("ssb%d" % i, [128, 1024], F32, ls) for i in range(2)]
                        t1 = sb("t1", [128, 512], F32, ls)
                        t2 = sb("t2", [128, 512], F32, ls)
                        t3 = [sb("t3_%d" % i, [128, 1024], F32, ls) for i in range(2)]
                        t4 = sb("t4", [128, 1024], F32, ls)
                        yacc = sb("yacc", [128, 1024], F32, ls)
                        yg = sb("yg", [128, 1024], F32, ls)
                        junk = sb("junk", [128, 1024], BF16, ls)
                        S.op("pool", lambda e: e.memset(state[:], 0.0), [], ["state0", "state1"])
                        S.op("pool", lambda e: e.memset(prevb[:], 0.0), [], ["prevb0", "prevb1"])
                        S.dma("sp", "a0", dsk[:], rowp[L, :, R_DSK + hh * 16:R_DSK + (hh + 1) * 16], writes=["dsk"])
                        S.dma("sp", "a1", nw[:], rowp[L, :, R_SNW + hh * 1024:R_SNW + (hh + 1) * 1024], writes=["nw"])
                        Acol = A_all[:, dr * 32 + hh * 16:dr * 32 + hh * 16 + 16]
                        order = list(range(NT)) if dr == 0 else list(range(NT - 1, -1, -1))

                        def chunk_loads(it):
                            c = order[it]
                            s = it % 3
                            rows = slice(c * 128, (c + 1) * 128)
                            S.dma("sp", "dtt%d" % s, dtt[s][:], dt_s[rows, hh, dr * 16:(dr + 1) * 16], writes=["dtt%d" % s])
                            S.dma("sp", "bt%d" % s, bt[s][:], bt_s[hh, :, :, rows].rearrange("g p s -> p g s"), writes=["bt%d" % s])
                            S.dma("sp", "ct%d" % s, ct[s][:], ct_s[hh, :, :, rows].rearrange("g p s -> p g s"), writes=["ct%d" % s])
                            S.dma("sp", "xs%d" % s, xs[s][:], xs_s[hh, rows, :], writes=["xs%d" % s])
                            S.dma("sp", "btm%d" % s, btm[s][:], btm_s[hh, rows, :], writes=["btm%d" % s])
                            if dr == 1:
                                S.dma("sp", "ypl%d" % s, ypl[s][:], ypart_s[hh, rows, :], writes=["ypl%d" % s])
                                S.dma("sp", "zsl%d" % s, zsl[s][:], siluz_s[rows, hh * 1024:(hh + 1) * 1024], writes=["zsl%d" % s])

                        bg_of = {}
                        nbmod[0] = 6

                        def early(it):
                            s = it % 2
                            ks = str(s)
                            l = it % 3
                            kl = str(l)
                            if dr == 0:
                                S.op("pool", lambda e: e.tensor_tensor(
                                    out=t3[s][:].rearrange("p (a b) -> p a b", a=16), in0=xs[l][:].rearrange("p (a b) -> p a b", a=16),
                                    in1=bc(dsk[:].unsqueeze(2), [128, 16, 64]), op=ALU.mult), ["xs" + kl, "dsk"], ["t3_" + ks])
                            S.op("dve", lambda e: e.tensor_tensor(out=a_t[s][:], in0=dtt[l][:], in1=Acol, op=ALU.mult), ["dtt" + kl, "A_all"], ["a_t" + ks])
                            S.op("dve", lambda e: e.tensor_copy(a_hi[s][:], a_t[s][:]), ["a_t" + ks], ["a_hi" + ks])
                            S.op("dve", lambda e: e.tensor_tensor(out=a_lo[s][:], in0=a_t[s][:], in1=a_hi[s][:], op=ALU.subtract),
                                 ["a_t" + ks, "a_hi" + ks], ["a_lo" + ks])
                            bu = nextbank()
                            mm(PB[bu][:, 0:16], tri_f[:, dr, :], a_t[s][:], True, True, ["a_t" + ks], [pbk[bu]])
                            mm(PB[bu][:, 16:32], ones_f[:], a_t[s][:], True, True, ["a_t" + ks], [pbk[bu]])
                            S.op("act", lambda e: e.copy(ut[s][:], PB[bu][:, 0:32]), [pbk[bu]], ["ut" + ks])
                            S.op("act", lambda e: e.activation(lndt[s][:], dtt[l][:], AF.Ln), ["dtt" + kl], ["lndt" + ks])
                            S.op("dve", lambda e: e.tensor_tensor(out=biasM[s][:], in0=lndt[s][:], in1=ut[s][:, 0:16], op=ALU.subtract),
                                 ["lndt" + ks, "ut" + ks], ["biasM" + ks])
                            S.op("dve", lambda e: e.tensor_tensor(out=wst[s][:], in0=ut[s][:, 16:32], in1=ut[s][:, 0:16], op=ALU.subtract),
                                 ["ut" + ks], ["wst" + ks])
                            S.op("act", lambda e: e.activation(wst[s][:], wst[s][:], AF.Exp), ["wst" + ks], ["wst" + ks])
                            S.op("dve", lambda e: e.tensor_tensor(out=wst[s][:], in0=wst[s][:], in1=dtt[l][:], op=ALU.mult),
                                 ["wst" + ks, "dtt" + kl], ["wst" + ks])
                            S.op("act", lambda e: e.activation(eu[s][:], ut[s][:, 0:16], AF.Exp), ["ut" + ks], ["eu" + ks])
                            S.op("act", lambda e: e.activation(cd[s][:], ut[s][:, 16:32], AF.Exp), ["ut" + ks], ["cd" + ks])
                            S.op("dve", lambda e: e.tensor_tensor(
                                out=rUh[:], in0=bc(tri_b[:, dr, :].unsqueeze(1), [128, 16, 128]),
                                in1=bc(a_hi[s][:].unsqueeze(2), [128, 16, 128]), op=ALU.mult), ["a_hi" + ks], ["rUh"])
                            S.op("dve", lambda e: e.tensor_tensor(
                                out=rUl[:], in0=bc(tri_b[:, dr, :].unsqueeze(1), [128, 16, 128]),
                                in1=bc(a_lo[s][:].unsqueeze(2), [128, 16, 128]), op=ALU.mult), ["a_lo" + ks], ["rUl"])
                            bg = 6 + (it % 2)
                            for g in range(2):
                                mm(PB[bg][:, g * 128:(g + 1) * 128], bt[l][:, g, :], ct[l][:, g, :], True, True,
                                   ["bt" + kl, "ct" + kl], [pbk[bg]])
                            for g in range(2):
                                dk = "dec%d" % g
                                for hq in range(2):
                                    bq = nextbank()
                                    h0 = g * 8 + hq * 4
                                    bqv = PB[bq][:].rearrange("p (a b) -> p a b", a=4)
                                    mm(bqv, ones_b[:], rUh[:, h0:h0 + 4, :], True, False, ["rUh"], [pbk[bq]])
                                    mm(bqv, ones_b[:], rUl[:, h0:h0 + 4, :], False, False, ["rUl"], [pbk[bq]])
                                    mm(bqv, ident_b[:], mb_b[:, dr, :, :], False, True, [], [pbk[bq]])
                                    for j in range(4):
                                        S.op("act", lambda e: e.activation(
                                            dec[g][:, hq * 4 + j, :], PB[bq][:, j * 128:(j + 1) * 128], AF.Exp,
                                            bias=biasM[s][:, h0 + j:h0 + j + 1]), [pbk[bq], "biasM" + ks], [dk])
                            bg_of[it] = bg

                        def early_b(it):
                            s = it % 2
                            ks = str(s)
                            l = it % 3
                            kl = str(l)
                            bg = bg_of[it]
                            for g in range(2):
                                dk = "dec%d" % g
                                S.op("dve", lambda e: e.tensor_tensor(
                                    out=MT[g][:], in0=dec[g][:], in1=bc(PB[bg][:, g * 128:(g + 1) * 128].unsqueeze(1), [128, 8, 128]),
                                    op=ALU.mult), [dk, pbk[bg]], ["MT%d" % g])
                                bd = nextbank()
                                for h in range(8):
                                    mm(PB[bd][:, h * 64:(h + 1) * 64], MT[g][:, h, :], xs[l][:, (g * 8 + h) * 64:(g * 8 + h + 1) * 64],
                                       True, True, ["MT%d" % g, "xs" + kl], [pbk[bd]])
                                S.op("act", lambda e: e.copy(dsb[s][:, g * 512:(g + 1) * 512], PB[bd][:]), [pbk[bd]], ["dsb%d_%d" % (s, g)])
                                S.op("dve", lambda e: e.tensor_tensor(
                                    out=xsw[g][:].rearrange("p (a b) -> p a b", a=8),
                                    in0=xs[l][:, g * 512:(g + 1) * 512].rearrange("p (a b) -> p a b", a=8),
                                    in1=bc(wst[s][:, g * 8:(g + 1) * 8].unsqueeze(2), [128, 8, 64]), op=ALU.mult),
                                    ["xs" + kl, "wst" + ks], ["xsw%d" % g])
                                bs_ = nextbank()
                                mm(PB[bs_][:], btm[l][:, g * 128:(g + 1) * 128], xsw[g][:], True, True, ["btm" + kl, "xsw%d" % g], [pbk[bs_]])
                                S.op("act", lambda e: e.copy(ssb[s][:, g * 512:(g + 1) * 512], PB[bs_][:]), [pbk[bs_]], ["ssb%d_%d" % (s, g)])

                        def late(it):
                            c = order[it]
                            s = it % 2
                            ks = str(s)
                            l = it % 3
                            kl = str(l)
                            rows = slice(c * 128, (c + 1) * 128)
                            for g in range(2):
                                bo = nextbank()
                                mm(PB[bo][:], ct[l][:, g, :], prevb[:, g, :], True, True, ["ct" + kl, "prevb%d" % g], [pbk[bo]])
                                S.op("dve", lambda e: e.tensor_tensor(
                                    out=t1[:].rearrange("p (a b) -> p a b", a=8), in0=PB[bo][:].rearrange("p (a b) -> p a b", a=8),
                                    in1=bc(eu[s][:, g * 8:(g + 1) * 8].unsqueeze(2), [128, 8, 64]), op=ALU.mult), [pbk[bo], "eu" + ks], ["t1"])
                                S.op("dve", lambda e: e.tensor_tensor(out=yacc[:, g * 512:(g + 1) * 512], in0=t1[:],
                                                                      in1=dsb[s][:, g * 512:(g + 1) * 512], op=ALU.add),
                                     ["t1", "dsb%d_%d" % (s, g)], ["yacc%d" % g])
                                S.op("pool", lambda e: e.tensor_tensor(
                                    out=t2[:].rearrange("p (a b) -> p a b", a=8), in0=state[:, g, :].rearrange("p (a b) -> p a b", a=8),
                                    in1=bc(cd[s][:, g * 8:(g + 1) * 8].unsqueeze(2), [128, 8, 64]), op=ALU.mult), ["state%d" % g, "cd" + ks], ["t2"])
                                S.op("dve", lambda e: e.tensor_tensor(out=state[:, g, :], in0=t2[:], in1=ssb[s][:, g * 512:(g + 1) * 512], op=ALU.add),
                                     ["t2", "ssb%d_%d" % (s, g)], ["state%d" % g])
                                S.op("act", lambda e: e.copy(prevb[:, g, :], state[:, g, :]), ["state%d" % g], ["prevb%d" % g])
                            if dr == 0:
                                S.op("dve", lambda e: e.tensor_tensor(out=yo[s][:], in0=yacc[:], in1=t3[s][:], op=ALU.add),
                                     ["yacc0", "yacc1", "t3_" + ks], ["yo" + ks])
                                S.dma("sp", "yo%d" % s, ypart_s[hh, rows, :], yo[s][:], reads=["yo" + ks])
                            else:
                                S.op("pool", lambda e: e.tensor_tensor(out=t4[:], in0=yacc[:], in1=ypl[l][:], op=ALU.add),
                                     ["yacc0", "yacc1", "ypl" + kl], ["t4"])
                                S.op("dve", lambda e: e.tensor_tensor(out=yg[:], in0=t4[:], in1=zsl[l][:], op=ALU.mult), ["t4", "zsl" + kl], ["yg"])
                                S.op("act", lambda e: e.activation(junk[:], yg[:], AF.Square, accum_out=sscol[:, c:c + 1]), ["yg"], ["junk", "sscol"])
                                S.op("dve", lambda e: e.tensor_tensor(out=yo[s][:], in0=yg[:], in1=nw[:], op=ALU.mult), ["yg", "nw"], ["yo" + ks])
                                S.dma("sp", "yo%d" % s, yssd_s[rows, hh * 1024:(hh + 1) * 1024], yo[s][:], reads=["yo" + ks])

                        for it in range(min(3, NT)):
                            chunk_loads(it)
                        early(0)
                        early_b(0)
                        for it in range(NT):
                            if it + 1 < NT:
                                early(it + 1)
                            late(it)
                            if it + 1 < NT:
                                early_b(it + 1)
                            if it + 3 < NT:
                                chunk_loads(it + 3)
                        if dr == 1:
                            S.dma("sp", "a2", ss_s[hh], sscol[:], reads=["sscol"])
                        nbmod[0] = 8
                        S.flush()

                with ExitStack() as ls:
                    qTh = [sb("qTh%d" % i, [128, SEQ], BF16, ls) for i in range(2)]
                    kz = [[sb("kz%d_%d" % (i, c), [128, SEQ], BF16, ls) for c in range(2)] for i in range(2)]
                    vh = [sb("vh%d" % i, [128, NT, 130], BF16, ls) for i in range(2)]
                    pT = [sb("pT%d" % i, [128, 512], BF16, ls) for i in range(3)]
                    dnw = sb("dnw", [128, 128], F32, ls)
                    sdg = [sb("sdg%d" % i, [128, 4, 128], BF16, ls) for i in range(2)]
                    o0 = sb("o0", [128, 4, 128], F32, ls)
                    o1 = sb("o1", [128, 4, 128], F32, ls)
                    rs = sb("rs", [128, 8], F32, ls)
                    ssq = sb("ssq", [128, 4], F32, ls)
                    jk = sb("jk", [128, 128], BF16, ls)
                    yd = [sb("yd%d" % i, [128, 4, 128], BF16, ls) for i in range(2)]
                    S.dma("sp", "a0", dnw[:], rowp[L, :, R_DNW:R_DNW + 128], writes=["dnw"])
                    S.op("dve", lambda e: e.tensor_scalar(dnw[:], dnw[:], 1.0 - lam_init, None, op0=ALU.mult), ["dnw"], ["dnw"])
                    ACC = [4, 5, 6, 7]
                    items = [(hl, qb, c, kt) for hl in range(4) for qb in range(NB) for c in range(2) for kt in range(NT)]
                    nit = len(items)

                    def load_head(hl):
                        hs = hl % 2
                        hg = hh * 4 + hl
                        S.dma("sp", "qTh%d" % hs, qTh[hs][:], qT_s[hg], writes=["qTh%d" % hs])
                        S.dma("sp", "kTh%d" % hs, kz[hs][0][0:64, :], kT_s[hg, 0:64, :], writes=["kTh%d" % hs])
                        S.dma("sp", "kTh%d" % hs, kz[hs][1][64:128, :], kT_s[hg, 64:128, :], writes=["kTh%d" % hs])
                        S.dma("sp", "vh%d" % hs, vh[hs][:], v_s[hg].rearrange("(t p) e -> p t e", p=128), writes=["vh%d" % hs])

                    def qk(i):
                        hl, qb, c, kt = items[i]
                        hs = hl % 2
                        bsx = i % 4
                        mm(PB[bsx][:], kz[hs][c][:, kt * 128:(kt + 1) * 128], qTh[hs][:, qb * 512:(qb + 1) * 512], True, True,
                           ["qTh%d" % hs, "kTh%d" % hs], [pbk[bsx]])

                    def evac_acc(hl, qb, c):
                        for qt in range(4):
                            a = ACC[qt]
                            S.op("dve", lambda e: e.reciprocal(rs[:, c * 4 + qt:c * 4 + qt + 1], PB[a][:, 128:129]),
                                 [pbk[a]], ["rs%d_%d" % (c, qt)])
                            if c == 0:
                                S.op("dve", lambda e: e.tensor_scalar(o0[:, qt, :], PB[a][:, 0:128], rs[:, qt:qt + 1], None, op0=ALU.mult),
                                     [pbk[a], "rs0_%d" % qt], ["o0_%d" % qt])
                            else:
                                S.op("dve", lambda e: e.tensor_tensor(out=rs[:, 4 + qt:5 + qt], in0=rs[:, 4 + qt:5 + qt], in1=neg_lam[:], op=ALU.mult),
                                     ["rs1_%d" % qt, "neg_lam"], ["rs1_%d" % qt])
                                S.op("dve", lambda e: e.scalar_tensor_tensor(
                                    out=o1[:, qt, :], in0=PB[a][:, 0:128], scalar=rs[:, 4 + qt:5 + qt], in1=o0[:, qt, :],
                                    op0=ALU.mult, op1=ALU.add), [pbk[a], "rs1_%d" % qt, "o0_%d" % qt], ["o1_%d" % qt])

                    def post(hl, qb):
                        ds_ = qb % 2
                        hg = hh * 4 + hl
                        for qt in range(4):
                            S.op("act", lambda e: e.activation(jk[:], o1[:, qt, :], AF.Square, accum_out=ssq[:, qt:qt + 1]),
                                 ["o1_%d" % qt], ["jk", "ssq%d" % qt])
                            S.op("act", lambda e: e.activation(ssq[:, qt:qt + 1], ssq[:, qt:qt + 1], AF.Ln, scale=1.0 / 128.0, bias=EPS),
                                 ["ssq%d" % qt], ["ssq%d" % qt])
                            S.op("act", lambda e: e.activation(ssq[:, qt:qt + 1], ssq[:, qt:qt + 1], AF.Exp, scale=-0.5),
                                 ["ssq%d" % qt], ["ssq%d" % qt])
                            S.op("dve", lambda e: e.scalar_tensor_tensor(
                                out=o1[:, qt, :], in0=o1[:, qt, :], scalar=ssq[:, qt:qt + 1], in1=dnw[:],
                                op0=ALU.mult, op1=ALU.mult), ["o1_%d" % qt, "ssq%d" % qt, "dnw"], ["o1_%d" % qt])
                            S.op("dve", lambda e: e.tensor_tensor(out=yd[ds_][:, qt, :], in0=o1[:, qt, :], in1=sdg[ds_][:, qt, :], op=ALU.mult),
                                 ["o1_%d" % qt, "sdg%d" % ds_], ["yd%d" % ds_])
                        S.dma("pool", "yd%d" % ds_, ydiff_s[qb * 512:(qb + 1) * 512, hg * 128:(hg + 1) * 128].rearrange("(t p) c -> p t c", p=128),
                              yd[ds_][:], reads=["yd%d" % ds_])

                    for i_ in range(2):
                        S.op("pool", lambda e: e.memset(kz[i_][0][64:128, :], 0.0), [], ["kTh%d" % i_])
                        S.op("pool", lambda e: e.memset(kz[i_][1][0:64, :], 0.0), [], ["kTh%d" % i_])
                    load_head(0)
                    LOOK = 3
                    for i_ in range(LOOK):
                        qk(i_)
                    deferred = []
                    for i in range(nit):
                        hl, qb, c, kt = items[i]
                        hs = hl % 2
                        hg = hh * 4 + hl
                        if kt == 0 and c == 1:
                            ds_ = qb % 2
                            S.dma("sp", "sdg%d" % ds_, sdg[ds_][:],
                                  sdg_s[qb * 512:(qb + 1) * 512, hg * 128:(hg + 1) * 128].rearrange("(t p) c -> p t c", p=128),
                                  writes=["sdg%d" % ds_])
                        if kt == 0 and c == 0 and qb == 0 and hl + 1 < 4:
                            load_head(hl + 1)
                        bsx = i % 4
                        ps_ = i % 3
                        S.op("act", lambda e: e.activation(pT[ps_][:], PB[bsx][:], AF.Exp, scale=0.125),
                             [pbk[bsx]], ["pT%d" % ps_])
                        for qt in range(4):
                            mm(PB[ACC[qt]][:, 0:129], pT[ps_][:, qt * 128:(qt + 1) * 128], vh[hs][:, kt, 0:129],
                               kt == 0, kt == NT - 1, ["pT%d" % ps_, "vh%d" % hs], [pbk[ACC[qt]]])
                        if i + LOOK < nit:
                            qk(i + LOOK)
                        while deferred and deferred[0][0] <= i:
                            deferred.pop(0)[1]()
                        if kt == NT - 1:
                            evac_acc(hl, qb, c)
                            if c == 1:
                                deferred.append((i + 3, lambda hl=hl, qb=qb: post(hl, qb)))
                    while deferred:
                        deferred.pop(0)[1]()
                    S.flush()

            with ExitStack() as ls:
                wbr = sb("wbr", [128, 32, D], BF16, ls)
                wgl = sb("wgl", [128, 8, 3072], BF16, ls)
                gbb = sb("gbb", [1, 3072], BF16, ls)
                ssa = sb("ssa", [128, 2, NT], F32, ls)
                xTb = [sb("xTb%d" % i, [128, 8, 128], BF16, ls) for i in range(2)]
                ycat = [sb("ycat%d" % i, [128, 4096], BF16, ls) for i in range(2)]
                ycT = sb("ycT", [128, 32, 128], BF16, ls)
                gsb = [sb("gsb%d" % i, [128, 512], F32, ls) for i in range(2)]
                U = sb("U", [128, D], F32, ls)
                V = sb("V", [128, D], F32, ls)
                tmpv = sb("tmpv", [128, 512], F32, ls)
                rstd = sb("rstd", [128, 1], F32, ls)
                mrg = sb("mrg", [128, D], BF16, ls)
                mTs = [sb("mTs%d" % i, [128, 8, 128], BF16, ls) for i in range(2)]
                wv = w_in[L].rearrange("(k p) c -> p k c", p=128)
                for j in range(3):
                    S.dma("pool", "w0", wgl[:, :, j * 1024:(j + 1) * 1024], wv[:, :, O_GL + j * 1024:O_GL + (j + 1) * 1024], writes=["wgl"])
                wbv = w_br[L].rearrange("(k p) c -> p k c", p=128)
                for j in range(4):
                    S.dma("pool", "w1", wbr[:, j * 8:(j + 1) * 8, :], wbv[:, j * 8:(j + 1) * 8, :], writes=["wbr"])
                S.dma("pool", "a0", gbb[:], rowp[L, 0:1, R_GB:R_GB + 3072], writes=["gbb"])
                S.dma("sp", "a1", ssa[:], ss_s.rearrange("h p t -> p h t"), writes=["ssa"])
                pctr[0] = 0
                def loads7(t):
                    s_ = t % 2
                    rows_ = slice(t * 128, (t + 1) * 128)
                    S.dma("sp", "xTb%d" % s_, xTb[s_][:], xT_s[:, :, rows_], writes=["xTb%d" % s_])
                    S.dma("sp", "ycat%d" % s_, ycat[s_][:, 0:2048], yssd_s[rows_, :], writes=["ycat%d" % s_])
                    S.dma("sp", "ycat%d" % s_, ycat[s_][:, 2048:3072], ydiff_s[rows_, :], writes=["ycat%d" % s_])
                    S.dma("sp", "ycat%d" % s_, ycat[s_][:, 3072:4096], ycross_s[rows_, :], writes=["ycat%d" % s_])

                loads7(0)
                for b in range(NB):
                    for tt in range(4):
                        t = b * 4 + tt
                        s = t % 2
                        xk = "xTb%d" % s
                        ys = t % 2
                        yk = "ycat%d" % ys
                        rows = slice(t * 128, (t + 1) * 128)
                        if t + 1 < NT:
                            loads7(t + 1)
                        S.op("dve", lambda e, t=t: e.tensor_tensor(out=rstd[:], in0=ssa[:, 0, t:t + 1], in1=ssa[:, 1, t:t + 1], op=ALU.add), ["ssa"], ["rstd"])
                        S.op("act", lambda e: e.activation(rstd[:], rstd[:], AF.Ln, scale=1.0 / 2048.0, bias=EPS), ["rstd"], ["rstd"])
                        S.op("act", lambda e: e.activation(rstd[:], rstd[:], AF.Exp, scale=-0.5), ["rstd"], ["rstd"])
                        for q4 in range(4):
                            bi = nextbank()
                            pv = pbf(bi).rearrange("p (a b) -> p a b", a=8)
                            for k in range(8):
                                kk = q4 * 8 + k
                                tr(pv[:, k, :], ycat[ys][:, kk * 128:(kk + 1) * 128], ident_b[:], [yk], [pbk[bi]])
                            evac(ycT[:, q4 * 8:(q4 + 1) * 8, :], pv, [pbk[bi]], ["ycT%d" % q4])
                        kranges = [(0, 16), (16, 24), (24, 32)]
                        for j in range(3):
                            for cb in range(2):
                                gs_ = (j * 2 + cb) % 2
                                bgt = nextbank()
                                gc0 = j * 1024 + cb * 512
                                for k in range(8):
                                    mm(PB[bgt][:], xTb[s][:, k, :], wgl[:, k, gc0:gc0 + 512], k == 0, False,
                                       [xk, "wgl"], [pbk[bgt]])
                                mm(PB[bgt][:], ones_b[0:1, :], gbb[0:1, gc0:gc0 + 512], False, True, ["gbb"], [pbk[bgt]])
                                S.op("act", lambda e, bgt=bgt, gs_=gs_: e.activation(gsb[gs_][:], PB[bgt][:], AF.Sigmoid), [pbk[bgt]], ["gsb%d" % gs_])
                                bp = nextbank()
                                k0, k1 = kranges[j]
                                for kk in range(k0, k1):
                                    mm(PB[bp][:], ycT[:, kk, :], wbr[:, kk, cb * 512:(cb + 1) * 512], kk == k0, kk == k1 - 1,
                                       ["ycT%d" % (kk // 8), "wbr"], [pbk[bp]])
                                cs = slice(cb * 512, (cb + 1) * 512)
                                if j == 0:
                                    S.op("dve", lambda e, bp=bp, gs_=gs_, cs=cs: e.tensor_tensor(out=U[:, cs], in0=PB[bp][:], in1=gsb[gs_][:], op=ALU.mult),
                                         [pbk[bp], "gsb%d" % gs_], ["U%d" % cb])
                                elif j == 1:
                                    S.op("dve", lambda e, bp=bp, gs_=gs_, cs=cs: e.tensor_tensor(out=V[:, cs], in0=PB[bp][:], in1=gsb[gs_][:], op=ALU.mult),
                                         [pbk[bp], "gsb%d" % gs_], ["V%d" % cb])
                                else:
                                    S.op("dve", lambda e, bp=bp, gs_=gs_: e.tensor_tensor(out=tmpv[:], in0=PB[bp][:], in1=gsb[gs_][:], op=ALU.mult),
                                         [pbk[bp], "gsb%d" % gs_], ["tmpv"])
                                    S.op("pool", lambda e, cs=cs: e.tensor_tensor(out=V[:, cs], in0=V[:, cs], in1=tmpv[:], op=ALU.add),
                                         ["V%d" % cb, "tmpv"], ["V%d" % cb])
                        S.op("dve", lambda e: e.scalar_tensor_tensor(out=mrg[:], in0=U[:], scalar=rstd[:], in1=V[:], op0=ALU.mult, op1=ALU.add),
                             ["U0", "U1", "V0", "V1", "rstd"], ["mrg"])
                        bi = nextbank()
                        pv = pbf(bi).rearrange("p (a b) -> p a b", a=8)
                        for k in range(8):
                            tr(pv[:, k, :], mrg[:, k * 128:(k + 1) * 128], ident_b[:], ["mrg"], [pbk[bi]])
                        evac(mTs[s][:], pv, [pbk[bi]], ["mTs%d" % s])
                        S.dma("sp", "mTs%d" % s, mT_s[:, :, rows], mTs[s][:], reads=["mTs%d" % s])
                S.flush()

            with ExitStack() as ls:
                wo = sb("wo", [128, 8, D], BF16, ls)
                lng = sb("lng", [128, D], F32, ls)
                lnb = sb("lnb", [128, D], F32, ls)
                mTb = [sb("mTb%d" % i, [128, 8, 512], BF16, ls) for i in range(2)]
                xr = [sb("xr%d" % i, [128, D], F32, ls) for i in range(2)]
                r = sb("r", [128, D], F32, ls)
                st6 = sb("st6", [128, 2, 6], F32, ls)
                mvv = sb("mvv", [128, 2], F32, ls)
                xn = [sb("xn%d" % i, [128, D], F32, ls) for i in range(2)]
                S.dma("pool", "w0", wo[:], w_out[L].rearrange("(k p) c -> p k c", p=128), writes=["wo"])
                S.dma("sp", "a0", lng[:], rowp[L, :, R_LNG:R_LNG + D], writes=["lng"])
                S.dma("sp", "a1", lnb[:], rowp[L, :, R_LNB:R_LNB + D], writes=["lnb"])
                for b in range(NB):
                    s = b % 2
                    S.dma("sp", "mTb%d" % s, mTb[s][:], mT_s[:, :, b * 512:(b + 1) * 512], writes=["mTb%d" % s])
                    for tt in range(4):
                        t = b * 4 + tt
                        xs_ = t % 2
                        rows = slice(t * 128, (t + 1) * 128)
                        S.dma("sp", "xr%d" % xs_, xr[xs_][:], xsrc[rows, :], writes=["xr%d" % xs_])
                        for cb in range(2):
                            bi = nextbank()
                            for k in range(8):
                                mm(PB[bi][:], mTb[s][:, k, tt * 128:(tt + 1) * 128], wo[:, k, cb * 512:(cb + 1) * 512], k == 0, k == 7,
                                   ["mTb%d" % s, "wo"], [pbk[bi]])
                            S.op("dve", lambda e, bi=bi, cb=cb, xs_=xs_: e.scalar_tensor_tensor(
                                out=r[:, cb * 512:(cb + 1) * 512], in0=xr[xs_][:, cb * 512:(cb + 1) * 512], scalar=ALPHA, in1=PB[bi][:],
                                op0=ALU.mult, op1=ALU.add), ["xr%d" % xs_, pbk[bi]], ["r%d" % cb])
                            S.op("dve", lambda e, cb=cb: e.bn_stats(st6[:, cb, :], r[:, cb * 512:(cb + 1) * 512]), ["r%d" % cb], ["st6_%d" % cb])
                        S.op("dve", lambda e: e.bn_aggr(mvv[:], st6[:]), ["st6_0", "st6_1"], ["mvv"])
                        S.op("act", lambda e: e.activation(mvv[:, 1:2], mvv[:, 1:2], AF.Ln, bias=EPS), ["mvv"], ["mvv"])
                        S.op("act", lambda e: e.activation(mvv[:, 1:2], mvv[:, 1:2], AF.Exp, scale=-0.5), ["mvv"], ["mvv"])
                        S.op("dve", lambda e, xs_=xs_: e.tensor_scalar(xn[xs_][:], r[:], mvv[:, 0:1], mvv[:, 1:2], op0=ALU.subtract, op1=ALU.mult),
                             ["r0", "r1", "mvv"], ["xn%d" % xs_])
                        S.op("pool", lambda e, xs_=xs_: e.tensor_tensor(out=xn[xs_][:], in0=xn[xs_][:], in1=lng[:], op=ALU.mult),
                             ["xn%d" % xs_, "lng"], ["xn%d" % xs_])
                        S.op("dve", lambda e, xs_=xs_: e.tensor_tensor(out=xn[xs_][:], in0=xn[xs_][:], in1=lnb[:], op=ALU.add),
                             ["xn%d" % xs_, "lnb"], ["xn%d" % xs_])
                        S.dma("sp", "xn%d" % xs_, xdst[rows, :], xn[xs_][:], reads=["xn%d" % xs_])
                S.flush()
        print("program built: instr =", S.ninstr, flush=True)
    return nc


def make_consts():
    c = np.zeros((128, C_END), np.float32)
    s = np.arange(128)[:, None]
    t = np.arange(128)[None, :]
    c[:, C_ID:C_ID + 128] = (s == t)
    c[:, C_TL:C_TL + 128] = (s <= t)
    c[:, C_TU:C_TU + 128] = (s >= t)
    c[:, C_MF:C_MF + 128] = np.where(s <= t, 0.0, NEG)
    c[:, C_MB:C_MB + 128] = np.where(s >= t, 0.0, NEG)
    inv = (1.0 / (np.float32(10000.0) ** (np.arange(0, 64, 2, dtype=np.float32) / np.float32(64)))).astype(np.float32)
    c[:, C_INV:C_INV + 32] = inv[None, :]
    c[:, C_ONE:C_ONE + 128] = 1.0
    return c


def prep_inputs(inp, SEQ):
    f = lambda a: np.ascontiguousarray(np.asarray(a))
    B = inp["x"].shape[0]
    NT = SEQ // 128
    rowp = np.concatenate([
        f(inp["dt_bias"]).reshape(DEPTH, 64), f(inp["a_log"]).reshape(DEPTH, 64), f(inp["d_skip"]).reshape(DEPTH, 32),
        f(inp["ssd_norm_w"]), f(inp["diff_norm_w"]), f(inp["ln_g"]), f(inp["ln_b"]),
        f(inp["diff_lam"]).reshape(DEPTH, 256), f(inp["gate_b"]).reshape(DEPTH, 3072)], axis=1).astype(np.float32)
    rowp = np.ascontiguousarray(np.broadcast_to(rowp[:, None, :], (DEPTH, 128, R_END)))
    cw = f(inp["conv_w"])
    cb = f(inp["conv_b"])
    colp = np.concatenate([cw, cb[:, None, :]], axis=1)
    colp = np.ascontiguousarray(colp.reshape(DEPTH, 6, 24, 128).transpose(0, 3, 2, 1)).astype(np.float32)
    w_br = np.ascontiguousarray(np.concatenate([f(inp["w_br_ssd"]), f(inp["w_br_diff"]), f(inp["w_br_cross"])], axis=1))
    consts = make_consts()
    maps = []
    for b in range(B):
        pos = f(inp["positions"])[b].astype(np.int32).reshape(NT, 128).T
        maps.append({
            "x": f(inp["x"][b]), "mem": f(inp["mem"][b]), "pos": np.ascontiguousarray(pos),
            "w_in": f(inp["w_in"]), "w_kv": f(inp["w_mem_kv"]), "w_br": w_br, "w_out": f(inp["w_out"]),
            "rowp": rowp, "colp": colp, "consts": consts})
    return maps


_CACHE = {}


def kernel(**inputs):
    x = np.asarray(inputs["x"])
    B, SEQ, _ = x.shape
    if SEQ not in _CACHE:
        _CACHE[SEQ] = build_program(SEQ)
    nc = _CACHE[SEQ]
    maps = prep_inputs(inputs, SEQ)
    res = run_bass_kernel_spmd(nc, maps, core_ids=list(range(B)))
    out = np.stack([np.asarray(res.results[b]["out"]) for b in range(B)], axis=0)
    return out.astype(np.float32)
```
